# Optimizing a Trainium2 kernel written in Bass

```python
import jax, jax.numpy as jnp
from jax import lax
import numpy as np

D_MODEL = 1024
BATCH = 8
SEQ = 4096
DEPTH = 2
DEC_BATCH = 2
DEC_SEQ = 16384
PAST_LEN = 128

GRID_W = 64
HEAD_DIM = 64
A_HEADS = 8
NA_KH = 8
NA_KW = 16
B_HEADS = 8
B_KV_HEADS = 2
B_GROUP = B_HEADS // B_KV_HEADS
C_HEADS = 16
C_NOPE = 64
C_ROPE = 32
C_V = 64
C_Q_RANK = 384
C_KV_RANK = 256
ROPE_THETA = 10000.0
Q_BLOCK = 128
EPS = 1e-6
N_EVEN = (DEPTH + 1) // 2
N_ODD = DEPTH // 2

A_WIDTH = A_HEADS * HEAD_DIM
B_WIDTH = B_HEADS * HEAD_DIM
B_KV_WIDTH = B_KV_HEADS * HEAD_DIM
MIX0_WIDTH = A_WIDTH + B_WIDTH
IN0_WIDTH = 3 * A_WIDTH + B_WIDTH + 2 * B_KV_WIDTH + MIX0_WIDTH
C_QK_DIM = C_NOPE + C_ROPE
C_WIDTH = C_HEADS * C_V
IN1_WIDTH = C_Q_RANK + C_KV_RANK + C_ROPE + C_WIDTH

kernel_name = "hybrid_natten_gqa_mla_gated_encoder"


def rms_norm(x, g):
    xf = x.astype(jnp.float32)
    y = xf * lax.rsqrt(jnp.mean(xf * xf, axis=-1, keepdims=True) + EPS)
    return (y * g.astype(jnp.float32)).astype(x.dtype)


def axial_angles(n_tok, rot_dim):
    n_freq = rot_dim // 4
    inv = ROPE_THETA ** (-jnp.arange(n_freq, dtype=jnp.float32) / n_freq)
    t = jnp.arange(n_tok, dtype=jnp.int32)
    row = (t // GRID_W).astype(jnp.float32)
    col = (t % GRID_W).astype(jnp.float32)
    ang = jnp.concatenate([row[:, None] * inv[None], col[:, None] * inv[None]], axis=-1)
    return jnp.cos(ang), jnp.sin(ang)


def apply_rope(x, cos, sin):
    h = x.shape[-1] // 2
    x1 = x[..., :h].astype(jnp.float32)
    x2 = x[..., h:].astype(jnp.float32)
    return jnp.concatenate([x1 * cos - x2 * sin, x1 * sin + x2 * cos], axis=-1).astype(x.dtype)


def heads(t, n_heads, d):
    b, s, _ = t.shape
    return t.reshape(b, s, n_heads, d).transpose(0, 2, 1, 3)


def merge_heads(t):
    b, h, s, d = t.shape
    return t.transpose(0, 2, 1, 3).reshape(b, s, h * d)


def neighbourhood_attention(q, k, v, rpb):
    b, h, s, d = q.shape
    rows = s // GRID_W
    kh = min(NA_KH, rows)
    kw = NA_KW
    qg = q.reshape(b, h, rows, GRID_W, d)
    kg = k.reshape(b, h, rows, GRID_W, d)
    vg = v.reshape(b, h, rows, GRID_W, d)
    col = jnp.arange(GRID_W, dtype=jnp.int32)
    col_start = jnp.clip(col - kw // 2, 0, GRID_W - kw)
    col_idx = col_start[:, None] + jnp.arange(kw, dtype=jnp.int32)[None]
    col_off = col_idx - col[:, None]
    row_start = jnp.clip(jnp.arange(rows, dtype=jnp.int32) - kh // 2, 0, rows - kh)
    scale = d ** -0.5

    def one_row(args):
        r, rs = args
        qr = lax.dynamic_index_in_dim(qg, r, axis=2, keepdims=False)
        kb = lax.dynamic_slice_in_dim(kg, rs, kh, axis=2)
        vb = lax.dynamic_slice_in_dim(vg, rs, kh, axis=2)
        kwin = kb[:, :, :, col_idx]
        vwin = vb[:, :, :, col_idx]
        row_off = rs + jnp.arange(kh, dtype=jnp.int32) - r
        bias = rpb[:, row_off[:, None, None] + NA_KH - 1,
                   col_off[None] + NA_KW - 1]
        sc = jnp.einsum('bhwd,bhrwkd->bhwrk', qr, kwin).astype(jnp.float32) * scale
        sc = sc + bias.transpose(0, 2, 1, 3)[None].astype(jnp.float32)
        p = jax.nn.softmax(sc.reshape(b, h, GRID_W, kh * kw), axis=-1)
        p = p.reshape(b, h, GRID_W, kh, kw).astype(v.dtype)
        return jnp.einsum('bhwrk,bhrwkd->bhwd', p, vwin)

    out = lax.map(one_row, (jnp.arange(rows, dtype=jnp.int32), row_start))
    return out.transpose(1, 2, 0, 3, 4).reshape(b, h, s, d)


def block_attention(q, k, v, scale):
    b, hk, g, s, dk = q.shape
    dv = v.shape[-1]
    nb = s // Q_BLOCK
    qb = q.reshape(b, hk, g, nb, Q_BLOCK, dk).transpose(3, 0, 1, 2, 4, 5)

    def one_block(qi):
        sc = jnp.einsum('bkgqd,bksd->bkgqs', qi, k).astype(jnp.float32) * scale
        p = jax.nn.softmax(sc, axis=-1).astype(v.dtype)
        return jnp.einsum('bkgqs,bksd->bkgqd', p, v)

    out = lax.map(one_block, qb)
    return out.transpose(1, 2, 3, 0, 4, 5).reshape(b, hk, g, s, dv)


def layer_even(x, g_norm, w_in, rpb, qn_g, kn_g, w_out):
    b, s, _ = x.shape
    h = rms_norm(x, g_norm)
    proj = h @ w_in
    o1 = A_WIDTH
    o2 = 2 * A_WIDTH
    o3 = 3 * A_WIDTH
    o4 = o3 + B_WIDTH
    o5 = o4 + B_KV_WIDTH
    o6 = o5 + B_KV_WIDTH
    qa, ka, va, qb, kb, vb, gate = jnp.split(proj, [o1, o2, o3, o4, o5, o6], axis=-1)
    out_a = neighbourhood_attention(heads(qa, A_HEADS, HEAD_DIM), heads(ka, A_HEADS, HEAD_DIM),
                                    heads(va, A_HEADS, HEAD_DIM), rpb)
    cos, sin = axial_angles(s, HEAD_DIM)
    qh = apply_rope(rms_norm(heads(qb, B_HEADS, HEAD_DIM), qn_g), cos, sin)
    kh = apply_rope(rms_norm(heads(kb, B_KV_HEADS, HEAD_DIM), kn_g), cos, sin)
    vh = heads(vb, B_KV_HEADS, HEAD_DIM)
    qh = qh.reshape(b, B_KV_HEADS, B_GROUP, s, HEAD_DIM)
    out_b = block_attention(qh, kh, vh, HEAD_DIM ** -0.5).reshape(b, B_HEADS, s, HEAD_DIM)
    mixed = jnp.concatenate([merge_heads(out_a), merge_heads(out_b)], axis=-1)
    return x + (mixed * jax.nn.silu(gate)) @ w_out


def layer_odd(x, g_norm, w_in, qlat_g, kvlat_g, w_uq, w_ukv, w_out):
    b, s, _ = x.shape
    h = rms_norm(x, g_norm)
    proj = h @ w_in
    cq, ckv, k_rope, gate = jnp.split(
        proj, [C_Q_RANK, C_Q_RANK + C_KV_RANK, C_Q_RANK + C_KV_RANK + C_ROPE], axis=-1)
    q = heads(rms_norm(cq, qlat_g) @ w_uq, C_HEADS, C_QK_DIM)
    kv = heads(rms_norm(ckv, kvlat_g) @ w_ukv, C_HEADS, C_NOPE + C_V)
    q_nope, q_rope = q[..., :C_NOPE], q[..., C_NOPE:]
    k_nope, v = kv[..., :C_NOPE], kv[..., C_NOPE:]
    cos, sin = axial_angles(s, C_ROPE)
    q_rope = apply_rope(q_rope, cos, sin)
    k_r = apply_rope(k_rope[:, None], cos, sin)
    k = jnp.concatenate([k_nope, jnp.broadcast_to(k_r, (b, C_HEADS, s, C_ROPE))], axis=-1)
    qf = jnp.concatenate([q_nope, q_rope], axis=-1)[:, :, None]
    out = block_attention(qf, k, v, C_QK_DIM ** -0.5)[:, :, 0]
    mixed = merge_heads(out)
    return x + (mixed * jax.nn.silu(gate)) @ w_out


def trunk(x, norm_e, w_in_e, rpb_a, qnorm_b, knorm_b, w_out_e,
          norm_o, w_in_o, qlat_g, kvlat_g, w_uq, w_ukv, w_out_o, norm_f):
    for layer in range(DEPTH):
        i = layer // 2
        if layer % 2 == 0:
            x = layer_even(x, norm_e[i], w_in_e[i], rpb_a[i], qnorm_b[i], knorm_b[i], w_out_e[i])
        else:
            x = layer_odd(x, norm_o[i], w_in_o[i], qlat_g[i], kvlat_g[i], w_uq[i], w_ukv[i],
                          w_out_o[i])
    return rms_norm(x, norm_f)


def setup_inputs(seed: int = 0) -> dict:
    key = jax.random.key(seed)
    ks = jax.random.split(key, 18)

    def nrm(k, shape, fan_in):
        return jax.random.normal(k, shape, jnp.float32) * (fan_in ** -0.5)

    def gain(k, shape):
        return 1.0 + 0.01 * jax.random.normal(k, shape, jnp.float32)

    return {
        "x_prompt": jax.random.normal(ks[0], (BATCH, SEQ, D_MODEL), jnp.float32),
        "x_sample": jax.random.normal(ks[1], (DEC_BATCH, DEC_SEQ, D_MODEL), jnp.float32),
        "norm_e": gain(ks[2], (N_EVEN, D_MODEL)),
        "w_in_e": nrm(ks[3], (N_EVEN, D_MODEL, IN0_WIDTH), D_MODEL),
        "rpb_a": 0.1 * jax.random.normal(ks[4], (N_EVEN, A_HEADS, 2 * NA_KH - 1, 2 * NA_KW - 1),
                                         jnp.float32),
        "qnorm_b": gain(ks[5], (N_EVEN, HEAD_DIM)),
        "knorm_b": gain(ks[6], (N_EVEN, HEAD_DIM)),
        "w_out_e": nrm(ks[7], (N_EVEN, MIX0_WIDTH, D_MODEL), MIX0_WIDTH),
        "norm_o": gain(ks[8], (N_ODD, D_MODEL)),
        "w_in_o": nrm(ks[9], (N_ODD, D_MODEL, IN1_WIDTH), D_MODEL),
        "qlat_g": gain(ks[10], (N_ODD, C_Q_RANK)),
        "kvlat_g": gain(ks[11], (N_ODD, C_KV_RANK)),
        "w_uq": nrm(ks[12], (N_ODD, C_Q_RANK, C_HEADS * C_QK_DIM), C_Q_RANK),
        "w_ukv": nrm(ks[13], (N_ODD, C_KV_RANK, C_HEADS * (C_NOPE + C_V)), C_KV_RANK),
        "w_out_o": nrm(ks[14], (N_ODD, C_WIDTH, D_MODEL), C_WIDTH),
        "norm_f": gain(ks[15], (D_MODEL,)),
    }


def reference(x_prompt, x_sample, norm_e, w_in_e, rpb_a, qnorm_b, knorm_b, w_out_e,
              norm_o, w_in_o, qlat_g, kvlat_g, w_uq, w_ukv, w_out_o, norm_f):
    y_prompt = trunk(x_prompt, norm_e, w_in_e, rpb_a, qnorm_b, knorm_b, w_out_e,
                     norm_o, w_in_o, qlat_g, kvlat_g, w_uq, w_ukv, w_out_o, norm_f)
    y_sample = trunk(x_sample, norm_e, w_in_e, rpb_a, qnorm_b, knorm_b, w_out_e,
                     norm_o, w_in_o, qlat_g, kvlat_g, w_uq, w_ukv, w_out_o, norm_f)
    return (y_prompt, y_sample)
```

```python
import contextlib
import os
import numpy as np
import concourse.bass as bass
import concourse.mybir as mybir
from concourse.bass_utils import run_bass_kernel_spmd

F32 = mybir.dt.float32
BF16 = mybir.dt.bfloat16
AF = mybir.ActivationFunctionType
ALU = mybir.AluOpType
AX = mybir.AxisListType

ENGS = ("pe", "act", "dve", "pool", "sp")
MAXV = 8000
EPS = 1e-6
NEG = -30000.0
DEBUG = bool(int(os.environ.get("KDEBUG", "0")))
STOP = int(os.environ.get("KSTOP", "99"))
NOCC = os.environ.get("KNOCC", "")
KSUB = int(os.environ.get("KSUB", "99"))
KNG = int(os.environ.get("KNG", "99"))
KP1 = int(os.environ.get("KP1", "99"))
KQT = int(os.environ.get("KQT", "32"))


class Buf:
    __slots__ = ("name", "writers", "readers", "excl")

    def __init__(self, name="", excl=False):
        self.name = name
        self.writers = []
        self.readers = []
        self.excl = excl


class Op:
    __slots__ = ("eng", "fn", "deps", "dma", "stream", "idx", "signal", "sem", "val", "inc")


class Sched:
    def __init__(self, nc):
        self.nc = nc
        self.ops = {e: [] for e in ENGS}
        self.streams = {}
        self.bar = []

    def op(self, eng, fn, reads=(), writes=(), dma=False, stream=None, inc=None):
        o = Op()
        o.eng, o.fn, o.dma, o.stream = eng, fn, dma, stream
        o.signal = dma
        o.sem, o.val = None, 0
        o.inc = inc if inc is not None else (16 if dma else 1)
        deps = list(self.bar)
        if any(b.excl for b in reads):
            writes = list(writes) + [b for b in reads if b.excl and b not in writes]
            reads = [b for b in reads if not b.excl]
        for b in reads:
            deps.extend(b.writers)
        for b in writes:
            deps.extend(b.writers)
            deps.extend(b.readers)
        best = {}
        for d in deps:
            if (not d.dma) and d.eng == "pe" and eng == "pe" and not dma:
                continue
            key = ("s", d.stream) if d.dma else ("e", d.eng)
            if key not in best or best[key].idx < d.idx:
                best[key] = d
        o.deps = list(best.values())
        for d in o.deps:
            d.signal = True
        if dma:
            lst = self.streams.setdefault(stream, [])
            o.idx = len(lst)
            lst.append(o)
        else:
            o.idx = len(self.ops[eng])
        self.ops[eng].append(o)
        for b in reads:
            b.readers.append(o)
        for b in writes:
            b.writers = [o]
            b.readers = []
        return o

    def barrier(self):
        bar = []
        for e in ENGS:
            comp = [o for o in self.ops[e] if not o.dma]
            if comp:
                bar.append(comp[-1])
        for sname, lst in self.streams.items():
            if lst and not str(sname).startswith("cc"):
                bar.append(lst[-1])
        self.bar = bar

    def emit(self, final_eng="sp"):
        nc = self.nc
        semreq = []

        def newsem(name):
            semreq.append(name)
            return len(semreq) - 1

        for e in ENGS:
            cur, cnt = None, 0
            for o in self.ops[e]:
                if o.dma or not o.signal:
                    continue
                if cur is None or cnt >= MAXV:
                    cur, cnt = newsem(f"m{e}{len(semreq)}"), 0
                cnt += 1
                o.sem, o.val = cur, cnt
        for sname, lst in self.streams.items():
            cur, cnt = None, 0
            for o in lst:
                if cur is None or cnt >= MAXV * 16:
                    cur, cnt = newsem(f"d{len(semreq)}"), 0
                cnt += o.inc
                o.sem, o.val = cur, cnt
        print("semaphores requested:", len(semreq), flush=True)
        with contextlib.ExitStack() as st:
            sems = [st.enter_context(nc.semaphore(n)) for n in semreq]
            block = st.enter_context(nc.Block())
            handles = {"pe": "tensor", "act": "scalar", "dve": "vector", "pool": "gpsimd", "sp": "sync"}
            finals = [lst[-1] for lst in self.streams.values() if lst]

            def make(e):
                def body(eh):
                    waited = {}
                    for o in self.ops[e]:
                        for d in o.deps:
                            if waited.get(d.sem, 0) < d.val:
                                eh.wait_ge(sems[d.sem], d.val)
                                waited[d.sem] = d.val
                        ins = o.fn(eh)
                        if o.signal:
                            ins.then_inc(sems[o.sem], o.inc)
                    if e == final_eng:
                        for d in finals:
                            if waited.get(d.sem, 0) < d.val:
                                eh.wait_ge(sems[d.sem], d.val)
                                waited[d.sem] = d.val
                return body

            for e in ENGS:
                getattr(block, handles[e])(make(e))


NTOK = 8192
SEQ_OWN = 4096
NLOC = 5120
G = 512


def build_program():
    nc = bass.Bass("TRN2", target_bir_lowering=False)
    S = Sched(nc)
    scratch_kind = "ExternalOutput" if DEBUG else "Internal"

    def din(name, shape, dt=F32):
        return nc.dram_tensor(name, list(shape), dt, kind="ExternalInput").ap()

    def dscr(name, shape, dt, dbg=True):
        return nc.dram_tensor(name, list(shape), dt, kind=(scratch_kind if dbg else "Internal")).ap()

    x_own = din("x_own", [NTOK, 1024])
    x_halo = din("x_halo", [1024, 1024])
    w_in_e = din("w_in_e", [1024, 3328])
    w_out_e = din("w_out_e", [1024, 1024])
    w1x = din("w1x", [1024, 1728])
    wuq2 = din("wuq2", [384, 3072])
    wukv2 = din("wukv2", [256, 2048])
    w_out_o = din("w_out_o", [1024, 1024])
    norm_e = din("norm_e", [1, 1024])
    norm_o = din("norm_o", [1, 1024])
    norm_f = din("norm_f", [1, 1024])
    qnorm_b = din("qnorm_b", [1, 64])
    knorm_b = din("knorm_b", [1, 64])
    qlat_g = din("qlat_g", [1, 384])
    kvlat_g = din("kvlat_g", [1, 256])
    cs0 = din("cs0", [NTOK, 128])
    c1 = din("c1", [96, NTOK])
    s1 = din("s1", [96, NTOK])
    ck = din("ck", [32, NTOK])
    sk = din("sk", [32, NTOK])
    bm_int = din("bm_int", [128, 8 * 5 * 128])
    bm_edge = din("bm_edge", [8, 128, 8 * 7 * 128])
    ident_in = din("ident", [128, 128])
    selA_in = din("selA", [128, 64])
    selB_in = din("selB", [128, 512])
    y_out = nc.dram_tensor("y_out", [NTOK, 1024], F32, kind="ExternalOutput").ap()

    qaT_d = dscr("qaT_d", [512, NTOK], BF16)
    kaT_d = dscr("kaT_d", [2, 512, NLOC], BF16)
    va_d = dscr("va_d", [2, NLOC, 520], BF16)
    qbT_d = dscr("qbT_d", [512, NTOK], BF16)
    kbT_p = dscr("kbT_p", [128, SEQ_OWN], BF16)
    vb_p = dscr("vb_p", [SEQ_OWN, 130], BF16)
    kbT_c = dscr("kbT_c", [128, SEQ_OWN], BF16, dbg=False)
    vb_c = [dscr(f"vb_c{g}", [128, 2080], BF16, dbg=False) for g in range(2)]
    kbT_g = dscr("kbT_g", [512, SEQ_OWN], BF16, dbg=False)
    vb_g = [dscr(f"vb_g{g}", [512, 2080], BF16, dbg=False) for g in range(2)]
    gT_d = [dscr(f"gT{l}_d", [1024, NTOK], F32) for l in range(2)]
    mgT_d = [dscr(f"mgT{l}_d", [1024, NTOK], BF16) for l in range(2)]
    x1_d = dscr("x1_d", [NTOK, 1024], F32)
    q1T_d = dscr("q1T_d", [16, 96, NTOK], BF16)
    k1n_p = dscr("k1n_p", [1024, SEQ_OWN], BF16)
    k1r_p = dscr("k1r_p", [32, SEQ_OWN], BF16)
    v1_p = dscr("v1_p", [SEQ_OWN, 1040], BF16)
    k1n_c = [dscr(f"k1n_c{i}", [128, SEQ_OWN], BF16, dbg=False) for i in range(8)]
    k1r_c = dscr("k1r_c", [32, SEQ_OWN], BF16, dbg=False)
    v1_c = [dscr(f"v1_c{h}", [128, 2080], BF16, dbg=False) for h in range(16)]
    k1n_g = [dscr(f"k1n_g{i}", [512, SEQ_OWN], BF16, dbg=False) for i in range(8)]
    k1r_g = dscr("k1r_g", [128, SEQ_OWN], BF16, dbg=False)
    v1_g = [dscr(f"v1_g{h}", [512, 2080], BF16, dbg=False) for h in range(16)]

    def vview(ap):
        return ap.rearrange("p (a c) -> (p a) c", c=65)

    D = {}

    def db(name):
        if name not in D:
            D[name] = Buf(name)
        return D[name]

    GROUPS4 = [[0, 1, 2, 3], [4, 5, 6, 7]]

    def LOAD(out, in_, stream, reads=(), writes=(), q="sp"):
        return S.op(q, lambda e: e.dma_start(out=out, in_=in_), reads=reads, writes=writes, dma=True, stream=stream)

    def STORE(out, in_, stream, reads=(), writes=()):
        return S.op("sp", lambda e: e.dma_start(out=out, in_=in_), reads=reads, writes=writes, dma=True, stream=stream)

    def MM(out, lhsT, rhs, start, stop, reads, writes):
        return S.op("pe", lambda e: e.matmul(out, lhsT=lhsT, rhs=rhs, start=start, stop=stop), reads=reads, writes=writes)

    def TR(out, in_, ident, reads, writes):
        return S.op("pe", lambda e: e.transpose(out=out, in_=in_, identity=ident), reads=reads, writes=writes)

    def ACT(out, in_, func, reads, writes, scale=1.0, accum=None):
        if accum is None:
            return S.op("act", lambda e: e.activation(out=out, in_=in_, func=func, scale=scale), reads=reads, writes=writes)
        return S.op("act", lambda e: e.activation(out=out, in_=in_, func=func, scale=scale, accum_out=accum), reads=reads, writes=writes)

    def DVE(fn, reads, writes):
        return S.op("dve", fn, reads=reads, writes=writes)

    def COPY(eng, out, in_, reads, writes):
        if eng == "act":
            return S.op("act", lambda e: e.copy(out=out, in_=in_), reads=reads, writes=writes)
        return S.op(eng, lambda e: e.tensor_copy(out=out, in_=in_), reads=reads, writes=writes)

    def TT(out, in0, in1, op, reads, writes, eng="dve"):
        return S.op(eng, lambda e: e.tensor_tensor(out=out, in0=in0, in1=in1, op=op), reads=reads, writes=writes)

    def TS(out, in0, s1_, s2_, op0, op1, reads, writes):
        if op1 is None:
            assert op0 == ALU.pow and s1_ == -0.5
            S.op("act", lambda e: e.activation(out=out, in_=in0, func=AF.Sqrt), reads=reads, writes=writes)
            return S.op("dve", lambda e: e.reciprocal(out=out, in_=out), reads=writes, writes=writes)
        return S.op("dve", lambda e: e.tensor_scalar(out=out, in0=in0, scalar1=s1_, scalar2=s2_, op0=op0, op1=op1), reads=reads, writes=writes)

    def STT(out, in0, scalar, in1, op0, op1, reads, writes):
        return S.op("dve", lambda e: e.scalar_tensor_tensor(out=out, in0=in0, scalar=scalar, in1=in1, op0=op0, op1=op1), reads=reads, writes=writes)

    def MEMSET(ap, val, writes, eng="dve"):
        return S.op(eng, lambda e: e.memset(ap, val), writes=writes)

    class T:
        def __init__(self, t, name, excl=False):
            self.t = t
            self.b = Buf(name, excl)

    with contextlib.ExitStack() as top:
        uid = [0]

        def sbuf(st, name, shape, dt):
            uid[0] += 1
            name = f"{name}_{uid[0]}"
            return T(st.enter_context(nc.sbuf_tensor(name, list(shape), dt)), name)

        def psum(st, name, shape, dt):
            uid[0] += 1
            name = f"{name}_{uid[0]}"
            return T(st.enter_context(nc.psum_tensor(name, list(shape), dt)), name, excl=True)

        ident_f = sbuf(top, "ident_f", [128, 128], F32)
        ident_b = sbuf(top, "ident_b", [128, 128], BF16)
        LOAD(ident_f.t[:], ident_in, "c_ident", writes=[ident_f.b])
        COPY("dve", ident_b.t[:], ident_f.t[:], [ident_f.b], [ident_b.b])
        ones_f = sbuf(top, "ones_f", [128, 64], F32)
        MEMSET(ones_f.t[:], 1.0, [ones_f.b])
        selA = sbuf(top, "selA", [128, 8, 8], F32)
        selB = sbuf(top, "selB", [128, 8, 64], F32)
        LOAD(selA.t[:].rearrange("p h m -> p (h m)"), selA_in, "c_selA", writes=[selA.b])
        LOAD(selB.t[:].rearrange("p h m -> p (h m)"), selB_in, "c_selB", writes=[selB.b])

        def bcast_load(st, name, src, n):
            t = sbuf(st, name, [128, n], F32)
            LOAD(t.t[:], src.broadcast_to([128, n]), "c_" + name, writes=[t.b])
            return t

        def load_weight(st, dst, src, nk, ncols, stage):
            for k in range(nk):
                sg = stage[k % 2]
                LOAD(sg.t[:, 0:ncols], src[k * 128:(k + 1) * 128, :], "wst" + sg.b.name, writes=[sg.b])
                COPY("act" if k % 2 else "dve", dst.t[:, k, :], sg.t[:, 0:ncols], [sg.b], [dst.b])

        def norm_part(xg, gvec, hb, ss, rstd, junk, nsub):
            MEMSET(ss.t[:], 0.0, [ss.b])
            for s in range(nsub):
                ACT(junk.t[:], xg.t[:, s, :], AF.Square, [xg.b, ss.b], [junk.b, ss.b], accum=ss.t[:, s:s + 1])
            TS(rstd.t[:], ss.t[:], 1.0 / 1024, EPS, ALU.mult, ALU.add, [ss.b], [rstd.b])
            TS(rstd.t[:], rstd.t[:], -0.5, None, ALU.pow, None, [rstd.b], [rstd.b])
            for s in range(nsub):
                STT(hb.t[:, s, :], xg.t[:, s, :], rstd.t[:, s:s + 1], gvec.t[:], ALU.mult, ALU.mult,
                    [xg.b, rstd.b, gvec.b], [hb.b])

        def transpose_part(hb, hT, psT, nsub):
            for s in range(nsub):
                p = psT[s % 2]
                for kc in range(8):
                    TR(p.t[:, kc, :], hb.t[:, s, kc * 128:(kc + 1) * 128], ident_b.t[:], [hb.b, ident_b.b], [p.b])
                COPY("act" if s % 2 else "dve", hT.t[:, :, s * 128:(s + 1) * 128], p.t[:], [p.b], [hT.b])

        with contextlib.ExitStack() as ph:
            w0 = sbuf(ph, "w0", [128, 8, 3328], BF16)
            with contextlib.ExitStack() as wst:
                stage = [sbuf(wst, f"stg{i}", [128, 3328], F32) for i in range(2)]
                load_weight(wst, w0, w_in_e, 8, 3328, stage)
                S.barrier()
            gE = bcast_load(ph, "gE", norm_e, 1024)
            gq = bcast_load(ph, "gq", qnorm_b, 64)
            gk = bcast_load(ph, "gk", knorm_b, 64)
            xg = [sbuf(ph, f"xg{i}", [128, 4, 1024], F32) for i in range(2)]
            cst = [sbuf(ph, f"cst{i}", [128, 4, 128], F32) for i in range(2)]
            hb = sbuf(ph, "hb", [128, 4, 1024], BF16)
            hT = sbuf(ph, "hT", [128, 8, 512], BF16)
            junk = sbuf(ph, "junk", [128, 1024], BF16)
            ss = sbuf(ph, "ss", [128, 4], F32)
            rstd = sbuf(ph, "rstd", [128, 4], F32)
            fm_st = [sbuf(ph, f"fm_st{i}", [128, 8, 512], BF16) for i in range(2)]
            g_st = sbuf(ph, "g_st", [128, 8, 512], F32)
            va_st = sbuf(ph, "va_st", [128, 4, 8, 65], BF16)
            vb_st = sbuf(ph, "vb_st", [128, 4, 2, 65], BF16)
            sq = sbuf(ph, "sq", [128, 640], F32)
            ssq = sbuf(ph, "ssq", [128, 10], F32)
            qn = sbuf(ph, "qn", [128, 10, 64], F32)
            tA = sbuf(ph, "tA", [128, 10, 64], F32)
            tB = sbuf(ph, "tB", [128, 10, 64], F32)
            qrs = [sbuf(ph, f"qr{i}", [128, 10, 64], BF16) for i in range(2)]
            qbT_st = sbuf(ph, "qbT_st", [128, 4, 512], BF16)
            kbT_st = sbuf(ph, "kbT_st", [128, 512], BF16)
            psT = [psum(ph, f"psT{i}", [128, 8, 128], BF16) for i in range(2)]
            psF = [psum(ph, f"psF{i}", [128, 512], F32) for i in range(2)]
            psK = [psum(ph, f"psK{i}", [128, 1024], F32) for i in range(2)]
            MEMSET(va_st.t[:], 1.0, [va_st.b])
            MEMSET(vb_st.t[:], 1.0, [vb_st.b])

            def p0_loads(gi, src, seq, own_idx, halo_col):
                xb = xg[gi % 2]
                LOAD(xb.t[:], src.rearrange("(s p) d -> p s d", p=128), "xg" + xb.b.name, writes=[xb.b])
                if own_idx is not None:
                    t0 = own_idx * G
                    cb = cst[gi % 2]
                    LOAD(cb.t[:], cs0[t0:t0 + G, :].rearrange("(s p) c -> p s c", p=128), "cs" + cb.b.name, writes=[cb.b])

            def p0_group(gi, src, seq, own_idx, halo_col, mid_hook, late_hook):
                xb = xg[gi % 2]
                own = own_idx is not None
                if own:
                    t0 = own_idx * G
                    cb = cst[gi % 2]
                    loc = 512 + (own_idx % 8) * G
                else:
                    loc = halo_col
                if KSUB <= 0:
                    return
                if KSUB <= 1:
                    return
                fm = fm_st[gi % 2]
                chunks = ([0, 1, 2, 3] if own else []) + [4, 5, 6, 7]
                for j, cc in enumerate(chunks):
                    p = psF[j % 2]
                    for kc in range(8):
                        MM(p.t[:], w0.t[:, kc, cc * 128:(cc + 1) * 128], hT.t[:, kc, :], kc == 0, kc == 7, [w0.b, hT.b], [p.b])
                    COPY("act" if j % 2 else "dve", fm.t[:, cc, :], p.t[:], [p.b], [fm.b])
                if own:
                    STORE(qaT_d[:, t0:t0 + G].rearrange("(c p) t -> p c t", p=128), fm.t[:, 0:4, :], "st_qa", [fm.b], [db("qaT")])
                STORE(kaT_d[seq, :, loc:loc + G].rearrange("(c p) t -> p c t", p=128), fm.t[:, 4:8, :], "st_ka", [fm.b], [db("kaT")])
                if KSUB <= 2:
                    return
                if own:
                    for j in range(8):
                        cc = 18 + j
                        p = psF[j % 2]
                        for kc in range(8):
                            MM(p.t[:], w0.t[:, kc, cc * 128:(cc + 1) * 128], hT.t[:, kc, :], kc == 0, kc == 7, [w0.b, hT.b], [p.b])
                        ACT(g_st.t[:, j, :], p.t[:], AF.Silu, [p.b], [g_st.b])
                    STORE(gT_d[0][:, t0:t0 + G].rearrange("(c p) t -> p c t", p=128), g_st.t[:], "st_g0", [g_st.b], [db("gT0")])
                mid_hook()
                if KSUB <= 3:
                    return
                for s in range(4):
                    p = psF[s % 2]
                    for kc in range(8):
                        MM(p.t[:], hT.t[:, kc, s * 128:(s + 1) * 128], w0.t[:, kc, 1024:1536], kc == 0, kc == 7, [w0.b, hT.b], [p.b])
                    COPY("act", va_st.t[:, s, :, 0:64], p.t[:].rearrange("p (h d) -> p h d", d=64), [p.b], [va_st.b])
                STORE(va_d[seq, loc:loc + G, :].rearrange("(s p) c -> p s c", p=128), va_st.t[:].rearrange("p s h c -> p s (h c)"),
                      "st_va", [va_st.b], [db("va")])
                if not own or KSUB <= 4:
                    late_hook()
                    return
                def tm_mm(s):
                    p = psK[s % 2]
                    for kc in range(8):
                        MM(p.t[:, 0:512], hT.t[:, kc, s * 128:(s + 1) * 128], w0.t[:, kc, 1536:2048], kc == 0, kc == 7, [w0.b, hT.b], [p.b])
                    for kc in range(8):
                        MM(p.t[:, 512:768], hT.t[:, kc, s * 128:(s + 1) * 128], w0.t[:, kc, 2048:2304], kc == 0, kc == 7, [w0.b, hT.b], [p.b])

                def tm_chain(s):
                    p = psK[s % 2]
                    qr = qrs[s % 2]
                    COPY("act", vb_st.t[:, s, :, 0:64], p.t[:, 640:768].rearrange("p (h d) -> p h d", d=64), [p.b], [vb_st.b])
                    p3 = p.t[:, 0:640].rearrange("p (h d) -> p h d", d=64)
                    ACT(sq.t[:], p.t[:, 0:640], AF.Square, [p.b], [sq.b])
                    DVE(lambda e: e.reduce_sum(out=ssq.t[:], in_=sq.t[:].rearrange("p (h d) -> p h d", d=64), axis=AX.X), [sq.b], [ssq.b])
                    TS(ssq.t[:], ssq.t[:], 1.0 / 64, EPS, ALU.mult, ALU.add, [ssq.b], [ssq.b])
                    TS(ssq.t[:], ssq.t[:], -0.5, None, ALU.pow, None, [ssq.b], [ssq.b])
                    TT(qn.t[:], p3, ssq.t[:].unsqueeze(2).broadcast_to([128, 10, 64]), ALU.mult, [p.b, ssq.b], [qn.b])
                    TT(qn.t[:, 0:8, :], qn.t[:, 0:8, :], gq.t[:].unsqueeze(1).broadcast_to([128, 8, 64]), ALU.mult, [qn.b, gq.b], [qn.b])
                    TT(qn.t[:, 8:10, :], qn.t[:, 8:10, :], gk.t[:].unsqueeze(1).broadcast_to([128, 2, 64]), ALU.mult, [qn.b, gk.b], [qn.b])
                    cosb = cb.t[:, s, 0:64].unsqueeze(1).broadcast_to([128, 10, 64])
                    sinb = cb.t[:, s, 64:128].unsqueeze(1).broadcast_to([128, 10, 64])
                    TT(tA.t[:], qn.t[:], cosb, ALU.mult, [qn.b, cb.b], [tA.b])
                    TT(tB.t[:], qn.t[:], sinb, ALU.mult, [qn.b, cb.b], [tB.b])
                    TT(qr.t[:, :, 0:32], tA.t[:, :, 0:32], tB.t[:, :, 32:64], ALU.subtract, [tA.b, tB.b], [qr.b])
                    TT(qr.t[:, :, 32:64], tB.t[:, :, 0:32], tA.t[:, :, 32:64], ALU.add, [tA.b, tB.b], [qr.b])

                def tm_tr(s):
                    qr = qrs[s % 2]
                    pt = psT[s % 2]
                    qr2 = qr.t[:].rearrange("p h d -> p (h d)")
                    for c in range(5):
                        TR(pt.t[:, c, :], qr2[:, c * 128:(c + 1) * 128], ident_b.t[:], [qr.b, ident_b.b], [pt.b])
                    COPY("dve", qbT_st.t[:, :, s * 128:(s + 1) * 128], pt.t[:, 0:4, :], [pt.b], [qbT_st.b])
                    COPY("act", kbT_st.t[:, s * 128:(s + 1) * 128], pt.t[:, 4, :], [pt.b], [kbT_st.b])

                tm_mm(0)
                tm_chain(0)
                tm_mm(1)
                tm_chain(1)
                tm_mm(2)
                tm_tr(0)
                tm_chain(2)
                tm_mm(3)
                tm_tr(1)
                late_hook()
                tm_chain(3)

                def tail():
                    tm_tr(2)
                    tm_tr(3)
                    STORE(qbT_d[:, t0:t0 + G].rearrange("(c p) t -> p c t", p=128), qbT_st.t[:], "st_qb", [qbT_st.b], [db("qbT")])
                    tl = (own_idx % 8) * G
                    kdst = kbT_p if seq == 0 else kbT_c
                    STORE(kdst[:, tl:tl + G], kbT_st.t[:], "st_kb", [kbT_st.b], [db("kbT%d" % seq)])
                    if seq == 0:
                        STORE(vb_p[tl:tl + G, :].rearrange("(s p) c -> p s c", p=128), vb_st.t[:].rearrange("p s h c -> p s (h c)"),
                              "st_vb", [vb_st.b], [db("vb0")])
                    else:
                        for g2 in range(2):
                            STORE(vview(vb_c[g2])[tl:tl + G, :].rearrange("(s p) c -> p s c", p=128), vb_st.t[:, :, g2, :],
                                  "st_vb", [vb_st.b], [db("vb1")])
                return tail

            jobs = []
            for hgi in range(min(2, KNG)):
                jobs.append((x_halo[hgi * G:(hgi + 1) * G, :], 1, None, 0 if hgi == 0 else 4608))
            for g in range(8, min(16, 8 + KNG)):
                jobs.append((x_own[g * G:(g + 1) * G, :], 1, g, None))
            n_sample_jobs = len(jobs)
            for g in range(0, min(8, KNG)):
                jobs.append((x_own[g * G:(g + 1) * G, :], 0, g, None))
            p0_loads(0, *jobs[0])
            norm_part(xg[0], gE, hb, ss, rstd, junk, 4)
            transpose_part(hb, hT, psT, 4)
            for gi, job in enumerate(jobs):
                nxt = gi + 1 < len(jobs)
                if nxt:
                    p0_loads(gi + 1, *jobs[gi + 1])
                tail = p0_group(gi, *job, (lambda gi=gi: norm_part(xg[(gi + 1) % 2], gE, hb, ss, rstd, junk, 4)) if nxt else (lambda: None),
                                (lambda: transpose_part(hb, hT, psT, 4)) if nxt else (lambda: None))
                if tail is not None:
                    tail()
                if gi == n_sample_jobs - 1:
                    if "a" not in NOCC:
                        S.op("pool", lambda e: e.collective_compute("AllGather", ALU.bypass, replica_groups=GROUPS4, ins=[kbT_c], outs=[kbT_g]),
                             reads=[db("kbT1")], writes=[db("ccg0")], dma=True, stream="cc0", inc=1)
                    if "b" not in NOCC:
                        for g2 in range(2):
                            S.op("pool", (lambda s_, d_: (lambda e: e.collective_compute("AllGather", ALU.bypass, replica_groups=GROUPS4, ins=[s_], outs=[d_])))(vb_c[g2], vb_g[g2]),
                                 reads=[db("vb1")], writes=[db("ccg0")], dma=True, stream="cc0", inc=1)
            S.barrier()
        if STOP <= 0:
            S.emit()
            return nc

        with contextlib.ExitStack() as ph:
            kaT = sbuf(ph, "kaT", [128, 4, NLOC], BF16)
            va = sbuf(ph, "va", [128, 40, 520], BF16)
            qaT = sbuf(ph, "qaT", [128, 4, SEQ_OWN], BF16)
            bmi = sbuf(ph, "bmi", [128, 8 * 5 * 128], F32)
            bme = sbuf(ph, "bme", [128, 8 * 7 * 128], F32)
            s_sb = [sbuf(ph, f"s_sb{i}", [128, 896], F32) for i in range(3)]
            pT = [sbuf(ph, f"pT{i}", [128, 896], BF16) for i in range(4)]
            oA = [sbuf(ph, f"oA{i}", [65, 8, 128], F32) for i in range(2)]
            gtA = [sbuf(ph, f"gtA{i}", [64, 8, 128], F32) for i in range(2)]
            mgA = [sbuf(ph, f"mgA{i}", [64, 8, 128], BF16) for i in range(2)]
            psS = [psum(ph, f"psS{i}", [128, 1024], F32) for i in range(3)]
            psO = psum(ph, "psO", [128, 4, 128], F32)
            psB = psum(ph, "psB", [128, 4, 128], F32)
            rs8 = sbuf(ph, "rs8", [128, 128], F32)
            LOAD(bmi.t[:], bm_int, "bmi", writes=[bmi.b])
            it = 0
            pend = []
            deferred1 = []

            def tick1():
                for d in list(deferred1):
                    d[0] -= 1
                    if d[0] <= 0:
                        deferred1.remove(d)
                        d[1]()

            def flush_pend(keep=0):
                while len(pend) > keep:
                    pend.pop(0)()

            def mk_pv1(pt, nkt, kt0, h, qt, seq):
                def f():
                    hl, hf = h % 4, h // 4
                    for k in range(nkt):
                        MM(psO.t[0:65, hl, :], va.t[:, kt0 + k, h * 65:(h + 1) * 65], pt.t[:, k * 128:(k + 1) * 128],
                           k == 0, k == nkt - 1, [va.b, pt.b], [psO.b])
                    if hl == 3:
                        ob, gt, mgo = oA[qt % 2], gtA[qt % 2], mgA[qt % 2]
                        t0 = seq * SEQ_OWN + qt * 128
                        hs = slice(hf * 4, hf * 4 + 4)
                        if hf == 0:
                            LOAD(gt.t[:], gT_d[0][0:512, t0:t0 + 128].rearrange("(h d) t -> d h t", d=64), "ld_gt" + gt.b.name, [db("gT0")], [gt.b])
                        COPY("act", ob.t[:, hs, :], psO.t[0:65, :, :], [psO.b], [ob.b])
                        TT(ob.t[0:64, hs, :], ob.t[0:64, hs, :], gt.t[:, hs, :], ALU.mult, [ob.b, gt.b], [ob.b], eng="pool")

                        def post_b():
                            for hh in range(4):
                                MM(psB.t[0:64, hh, :], selB.t[64:68, hh, :], rs8.t[64:68, :], True, True, [selB.b, rs8.b], [psB.b])
                            TT(mgo.t[:, hs, :], ob.t[0:64, hs, :], psB.t[0:64, :, :], ALU.mult, [ob.b, psB.b], [mgo.b])
                            if hf == 1:
                                STORE(mgT_d[0][0:512, t0:t0 + 128].rearrange("(h d) t -> d h t", d=64), mgo.t[:], "st_oA", [mgo.b], [db("mgT0")])

                        def post_a():
                            for hh in range(4):
                                MM(psB.t[64:72, 0, :], selA.t[64:65, hh, :], ob.t[64:65, hf * 4 + hh, :], hh == 0, hh == 3, [selA.b, ob.b], [psB.b])
                            DVE(lambda e: e.reciprocal(out=rs8.t[64:68, :], in_=psB.t[64:68, 0, :]), [psB.b], [rs8.b])
                            deferred1.append([2, post_b])
                        deferred1.append([2, post_a])
                return f

            for seq in range(2):
                flush_pend()
                if seq == 0:
                    MEMSET(kaT.t[:], 0.0, [kaT.b])
                    MEMSET(va.t[:], 0.0, [va.b], eng="pool")
                    LOAD(kaT.t[:, :, 512:4608], kaT_d[0, :, 512:4608].rearrange("(c p) t -> p c t", p=128), "ld_ka", [db("kaT")], [kaT.b])
                    LOAD(va.t[:, 4:36, :], va_d[0, 512:4608, :].rearrange("(t p) c -> p t c", p=128), "ld_va", [db("va")], [va.b])
                else:
                    LOAD(kaT.t[:], kaT_d[1].rearrange("(c p) t -> p c t", p=128), "ld_ka", [db("kaT")], [kaT.b])
                    LOAD(va.t[:], va_d[1].rearrange("(t p) c -> p t c", p=128), "ld_va", [db("va")], [va.b])
                LOAD(qaT.t[:], qaT_d[:, seq * SEQ_OWN:(seq + 1) * SEQ_OWN].rearrange("(c p) t -> p c t", p=128), "ld_qa", [db("qaT")], [qaT.b])
                for qt in (list(range(32)) if KQT >= 32 else [0, 1, 5, 30, 31][:KQT]):
                    edge = qt < 2 or qt >= 30
                    if edge:
                        ei = seq * 4 + (qt if qt < 2 else qt - 28)
                        LOAD(bme.t[:], bm_edge[ei], "bme", writes=[bme.b])
                        nkt, kt0, bm = 7, qt + 1, bme
                    else:
                        nkt, kt0, bm = 5, qt + 2, bmi
                    n = nkt * 128
                    for h in range(8):
                        c, base = h // 2, (h % 2) * 64
                        ps = psS[it % 3]
                        ssb = s_sb[it % 3]
                        pt = pT[it % 4]
                        it += 1
                        for k in range(nkt):
                            MM(ps.t[:, k * 128:(k + 1) * 128], kaT.t[base:base + 64, c, (kt0 + k) * 128:(kt0 + k + 1) * 128],
                               qaT.t[base:base + 64, c, qt * 128:(qt + 1) * 128], True, True, [kaT.b, qaT.b], [ps.b])
                        STT(ssb.t[:, 0:n], ps.t[:, 0:n], 0.125, bm.t[:, h * n:(h + 1) * n], ALU.mult, ALU.add, [ps.b, bm.b], [ssb.b])
                        ACT(pt.t[:, 0:n], ssb.t[:, 0:n], AF.Exp, [ssb.b], [pt.b])
                        pend.append(mk_pv1(pt, nkt, kt0, h, qt, seq))
                        flush_pend(keep=2)
                        tick1()
            flush_pend()
            while deferred1:
                deferred1.pop(0)[1]()
            S.barrier()
        if STOP <= 1:
            S.emit()
            return nc

        def dense_phase(layer, dk, scale, nkv, qper, k_loader, v_loader, q_loader, head_base, KG=3, rowpack=False):
            with contextlib.ExitStack() as ph:
                kT = [sbuf(ph, f"kT{i}", [128, 16384], BF16) for i in range(2)]
                vv = [sbuf(ph, f"vv{i}", [128, 128, 65], BF16) for i in range(2)]
                qT = [sbuf(ph, f"qT{i}", [128, SEQ_OWN], BF16) for i in range(2)]
                pT = [sbuf(ph, f"pT{i}", [128, KG * 512], BF16) for i in range(3)]
                ost = [sbuf(ph, f"ost{i}", [65, 512], F32) for i in range(2)]
                gtD = [sbuf(ph, f"gtD{i}", [64, 512], F32) for i in range(2)]
                mgD = [sbuf(ph, f"mgD{i}", [64, 512], BF16) for i in range(2)]
                psS = [psum(ph, f"psS{i}", [128, KG * 512], F32) for i in range(2)]
                psO = [psum(ph, f"psO{i}", [128, 512], F32) for i in range(1)]
                psB = psum(ph, "psB", [128, 512], F32)
                it = 0
                qbi = 0
                pending = None

                def mk_pv(po, ob, gt, mgo, vb_, pt, k0, sz, nkt, seq, h, qb):
                    def f():
                        for k in range(sz):
                            kt = k0 + k
                            MM(po.t[0:65, :], vb_.t[:, kt, :], pt.t[:, k * 512:(k + 1) * 512],
                               kt == 0, kt == nkt - 1, [vb_.b, pt.b], [po.b])
                        if k0 + sz == nkt:
                            t0 = seq * SEQ_OWN + qb * 512
                            f0 = (head_base + h) * 64
                            COPY("dve", ob.t[:], po.t[0:65, :], [po.b], [ob.b])
                            DVE(lambda e: e.reciprocal(out=ob.t[64:65, :], in_=ob.t[64:65, :]), [ob.b], [ob.b])
                            TT(ob.t[0:64, :], ob.t[0:64, :], gt.t[:], ALU.mult, [ob.b, gt.b], [ob.b])

                            def post():
                                MM(psB.t[0:64, :], ones_f.t[64:65, 0:64], ob.t[64:65, :], True, True, [ones_f.b, ob.b], [psB.b])
                                TT(mgo.t[:], ob.t[0:64, :], psB.t[0:64, :], ALU.mult, [ob.b, psB.b], [mgo.b])
                                STORE(mgT_d[layer][f0:f0 + 64, t0:t0 + 512], mgo.t[:], "st_oD", [mgo.b], [db("mgT%d" % layer)])
                            deferred.append([6, post])
                    return f

                deferred = []

                def tick():
                    for d in list(deferred):
                        d[0] -= 1
                        if d[0] <= 0:
                            deferred.remove(d)
                            d[1]()

                jobs = []
                kvi = 0
                for seq in range(2):
                    for g in range(nkv):
                        for hq in range(qper):
                            jobs.append((seq, g, hq, kvi))
                        kvi += 1

                def job_loads(j):
                    seq, g, hq, kvi_ = jobs[j]
                    if hq == 0:
                        k_loader(seq, g, kT[kvi_ % 2])
                        v_loader(seq, g, vv[kvi_ % 2])
                    q_loader(seq, g * qper + hq, qT[j % 2])

                job_loads(0)
                for j, (seq, g, hq, kvi_) in enumerate(jobs):
                    nk = SEQ_OWN if seq == 0 else 4 * SEQ_OWN
                    nkt = nk // 128
                    kb_, vb_, qb_ = kT[kvi_ % 2], vv[kvi_ % 2], qT[j % 2]
                    h = g * qper + hq
                    for qb in range(8):
                        if qb == 2 and j + 1 < len(jobs):
                            job_loads(j + 1)
                        po = psO[0]
                        ob, gt, mgo = ost[qbi % 2], gtD[qbi % 2], mgD[qbi % 2]
                        qbi += 1
                        t0 = seq * SEQ_OWN + qb * 512
                        f0 = (head_base + h) * 64
                        LOAD(gt.t[:], gT_d[layer][f0:f0 + 64, t0:t0 + 512], "ld_gt" + gt.b.name, [db("gT%d" % layer)], [gt.b])
                        for k0 in range(0, nkt, KG):
                            sz = min(KG, nkt - k0)
                            ps = psS[it % 2]
                            pt = pT[it % 3]
                            it += 1
                            for k in range(sz):
                                kt = k0 + k
                                r0 = (kt % 2) * 64 if rowpack else 0
                                MM(ps.t[:, k * 512:(k + 1) * 512], kb_.t[r0:r0 + dk, kt * 128:(kt + 1) * 128],
                                   qb_.t[r0:r0 + dk, qb * 512:(qb + 1) * 512], True, True, [kb_.b, qb_.b], [ps.b])
                            ACT(pt.t[:, 0:sz * 512], ps.t[:, 0:sz * 512], AF.Exp, [ps.b], [pt.b], scale=scale)
                            if pending is not None:
                                pending()
                            pending = mk_pv(po, ob, gt, mgo, vb_, pt, k0, sz, nkt, seq, h, qb)
                            tick()
                if pending is not None:
                    pending()
                for d in list(deferred):
                    d[1]()
                S.barrier()

        def k0_loader(seq, g, kb_):
            for r0 in (0, 64):
                if seq == 0:
                    LOAD(kb_.t[r0:r0 + 64, 0:SEQ_OWN], kbT_p[g * 64:(g + 1) * 64, :], "ld_k" + kb_.b.name, [db("kbT0")], [kb_.b])
                else:
                    LOAD(kb_.t[r0:r0 + 64, :].rearrange("p (r t) -> p r t", r=4),
                         kbT_g.rearrange("(r p) t -> p r t", p=128)[g * 64:(g + 1) * 64], "ld_k" + kb_.b.name, [db("ccg0")], [kb_.b])

        def v0_loader(seq, g, vb_):
            if seq == 0:
                LOAD(vb_.t[:, 0:32, :], vb_p[:, g * 65:(g + 1) * 65].rearrange("(t p) c -> p t c", p=128), "ld_v" + vb_.b.name, [db("vb0")], [vb_.b])
            else:
                LOAD(vb_.t[:], vview(vb_g[g]).rearrange("(t p) c -> p t c", p=128), "ld_v" + vb_.b.name, [db("ccg0")], [vb_.b])

        def q0_loader(seq, h, qb_):
            for r0 in (0, 64):
                LOAD(qb_.t[r0:r0 + 64, :], qbT_d[h * 64:(h + 1) * 64, seq * SEQ_OWN:(seq + 1) * SEQ_OWN], "ld_q" + qb_.b.name, [db("qbT")], [qb_.b])

        dense_phase(0, 64, 0.125, 2, 4, k0_loader, v0_loader, q0_loader, 8, rowpack=True)
        if STOP <= 2:
            S.emit()
            return nc

        def outproj_phase(layer, w_src, x_src, x_buf, final):
            with contextlib.ExitStack() as ph:
                wo = sbuf(ph, "wo", [128, 8, 1024], BF16)
                with contextlib.ExitStack() as wst:
                    stage = [sbuf(wst, f"stg{i}", [128, 1024], F32) for i in range(2)]
                    load_weight(wst, wo, w_src, 8, 1024, stage)
                    S.barrier()
                gF = bcast_load(ph, "gF", norm_f, 1024)
                mgs = [sbuf(ph, f"mg{i}", [128, 8, 512], BF16) for i in range(2)]
                xg = [sbuf(ph, f"xg{i}", [128, 4, 1024], F32) for i in range(2)]
                yo = [sbuf(ph, f"yo{i}", [128, 4, 1024], F32) for i in range(2)]
                junk = sbuf(ph, "junk", [128, 1024], BF16)
                ss = sbuf(ph, "ss", [128, 4], F32)
                rstd = sbuf(ph, "rstd", [128, 4], F32)
                psY = [psum(ph, f"psY{i}", [128, 512], F32) for i in range(6)]

                def loads(g):
                    t0 = g * G
                    LOAD(xg[g % 2].t[:], x_src[t0:t0 + G, :].rearrange("(s p) d -> p s d", p=128), "ld_x" + xg[g % 2].b.name, x_buf, [xg[g % 2].b])
                    LOAD(mgs[g % 2].t[:], mgT_d[layer][:, t0:t0 + G].rearrange("(c p) t -> p c t", p=128), "ld_mg" + mgs[g % 2].b.name,
                         [db("mgT%d" % layer)], [mgs[g % 2].b])

                loads(0)
                for g in range(16):
                    t0 = g * G
                    if g + 1 < 16:
                        loads(g + 1)
                    xb, mg = xg[g % 2], mgs[g % 2]
                    for s_ in range(4):
                        ps2 = [psY[(s_ * 2 + n) % 6] for n in range(2)]
                        for c in range(8):
                            for n in range(2):
                                MM(ps2[n].t[:], mg.t[:, c, s_ * 128:(s_ + 1) * 128], wo.t[:, c, n * 512:(n + 1) * 512], c == 0, c == 7, [mg.b, wo.b], [ps2[n].b])
                        for n in range(2):
                            TT(xb.t[:, s_, n * 512:(n + 1) * 512], ps2[n].t[:], xb.t[:, s_, n * 512:(n + 1) * 512], ALU.add, [ps2[n].b, xb.b], [xb.b])
                    if not final:
                        STORE(x1_d[t0:t0 + G, :].rearrange("(s p) d -> p s d", p=128), xb.t[:], "st_x1", [xb.b], [db("x1")])
                        continue
                    yb = yo[g % 2]
                    MEMSET(ss.t[:], 0.0, [ss.b])
                    for s_ in range(4):
                        ACT(junk.t[:], xb.t[:, s_, :], AF.Square, [xb.b, ss.b], [junk.b, ss.b], accum=ss.t[:, s_:s_ + 1])
                    TS(rstd.t[:], ss.t[:], 1.0 / 1024, EPS, ALU.mult, ALU.add, [ss.b], [rstd.b])
                    TS(rstd.t[:], rstd.t[:], -0.5, None, ALU.pow, None, [rstd.b], [rstd.b])
                    for s_ in range(4):
                        STT(yb.t[:, s_, :], xb.t[:, s_, :], rstd.t[:, s_:s_ + 1], gF.t[:], ALU.mult, ALU.mult, [xb.b, rstd.b, gF.b], [yb.b])
                    STORE(y_out[t0:t0 + G, :].rearrange("(s p) d -> p s d", p=128), yb.t[:], "st_y", [yb.b], [db("y")])
                S.barrier()

        outproj_phase(0, w_out_e, x_own, [], False)
        if STOP <= 3:
            S.emit()
            return nc

        with contextlib.ExitStack() as ph:
            w1 = sbuf(ph, "w1", [128, 8, 1728], BF16)
            wq = sbuf(ph, "wq", [128, 3, 3072], BF16)
            wkv = sbuf(ph, "wkv", [128, 2, 2048], BF16)
            with contextlib.ExitStack() as wst:
                stage = [sbuf(wst, f"stg{i}", [128, 3072], F32) for i in range(2)]
                load_weight(wst, w1, w1x, 8, 1728, stage)
                load_weight(wst, wq, wuq2, 3, 3072, stage)
                load_weight(wst, wkv, wukv2, 2, 2048, stage)
                S.barrier()
            gO = bcast_load(ph, "gO", norm_o, 1024)
            gql = bcast_load(ph, "gql", qlat_g, 384)
            gkl = bcast_load(ph, "gkl", kvlat_g, 256)
            x1s = [sbuf(ph, f"x1s{i}", [128, 4, 1024], F32) for i in range(2)]
            hb = sbuf(ph, "hb", [128, 4, 1024], BF16)
            hT = sbuf(ph, "hT", [128, 8, 512], BF16)
            junk = sbuf(ph, "junk", [128, 1024], BF16)
            ss = sbuf(ph, "ss", [128, 4], F32)
            rstd = sbuf(ph, "rstd", [128, 4], F32)
            ss2 = sbuf(ph, "ss2", [128, 2], F32)
            g_st = sbuf(ph, "g_st", [128, 8, 512], F32)
            lats = [sbuf(ph, f"lat{i}", [128, 640], BF16) for i in range(2)]
            latT = sbuf(ph, "latT", [128, 5, 512], BF16)
            c1ts = [sbuf(ph, f"c1t{i}", [96, 512], F32) for i in range(2)]
            s1ts = [sbuf(ph, f"s1t{i}", [96, 512], F32) for i in range(2)]
            ckts = [sbuf(ph, f"ckt{i}", [32, 512], F32) for i in range(2)]
            skts = [sbuf(ph, f"skt{i}", [32, 512], F32) for i in range(2)]
            ta = sbuf(ph, "ta", [96, 512], F32)
            tb = sbuf(ph, "tb", [96, 512], F32)
            q_st = [sbuf(ph, f"q_st{i}", [96, 512], BF16) for i in range(4)]
            kr_st = sbuf(ph, "kr_st", [32, 512], BF16)
            kn_sts = [sbuf(ph, f"kn_st{i}", [128, 8, 512], BF16) for i in range(2)]
            v_sts = [sbuf(ph, f"v_st{i}", [128, 4, 16, 65], BF16) for i in range(2)]
            psT = [psum(ph, f"psT{i}", [128, 8, 128], BF16) for i in range(2)]
            psF = [psum(ph, f"psF{i}", [128, 512], F32) for i in range(2)]
            psL = [psum(ph, f"psL{i}", [128, 1024], F32) for i in range(2)]
            psY = psF
            for v_st in v_sts:
                MEMSET(v_st.t[:], 1.0, [v_st.b])

            def p3_loads(gi, g):
                t0 = g * G
                x1 = x1s[gi % 2]
                LOAD(x1.t[:], x1_d[t0:t0 + G, :].rearrange("(s p) d -> p s d", p=128), "ld_x" + x1.b.name, [db("x1")], [x1.b], q="pool")
                LOAD(c1ts[gi % 2].t[:], c1[:, t0:t0 + G], "ld_c1%d" % (gi % 2), writes=[c1ts[gi % 2].b], q="pool")
                LOAD(s1ts[gi % 2].t[:], s1[:, t0:t0 + G], "ld_s1%d" % (gi % 2), writes=[s1ts[gi % 2].b], q="pool")
                LOAD(ckts[gi % 2].t[:], ck[:, t0:t0 + G], "ld_ck%d" % (gi % 2), writes=[ckts[gi % 2].b], q="pool")
                LOAD(skts[gi % 2].t[:], sk[:, t0:t0 + G], "ld_sk%d" % (gi % 2), writes=[skts[gi % 2].b], q="pool")

            def p3_group(gi, g, mid_hook):
                t0 = g * G
                seq = g // 8
                tl = (g % 8) * G
                x1 = x1s[gi % 2]
                c1t, s1t, ckt, skt = c1ts[gi % 2], s1ts[gi % 2], ckts[gi % 2], skts[gi % 2]
                for j in range(8):
                    p = psF[j % 2]
                    c0 = 672 + j * 128
                    for kc in range(8):
                        MM(p.t[:], w1.t[:, kc, c0:c0 + 128], hT.t[:, kc, :], kc == 0, kc == 7, [w1.b, hT.b], [p.b])
                    ACT(g_st.t[:, j, :], p.t[:], AF.Silu, [p.b], [g_st.b])
                STORE(gT_d[1][:, t0:t0 + G].rearrange("(c p) t -> p c t", p=128), g_st.t[:], "st_g1", [g_st.b], [db("gT1")])
                pa, pb = psF[0], psF[1]
                for kc in range(8):
                    MM(pa.t[0:32, :], w1.t[:, kc, 640:672], hT.t[:, kc, :], kc == 0, kc == 7, [w1.b, hT.b], [pa.b])
                for kc in range(8):
                    MM(pb.t[0:32, :], w1.t[:, kc, 1696:1728], hT.t[:, kc, :], kc == 0, kc == 7, [w1.b, hT.b], [pb.b])
                TT(ta.t[0:32, :], pa.t[0:32, :], ckt.t[:], ALU.mult, [pa.b, ckt.b], [ta.b])
                TT(tb.t[0:32, :], pb.t[0:32, :], skt.t[:], ALU.mult, [pb.b, skt.b], [tb.b])
                TT(kr_st.t[:], ta.t[0:32, :], tb.t[0:32, :], ALU.add, [ta.b, tb.b], [kr_st.b])
                STORE((k1r_p if seq == 0 else k1r_c)[:, tl:tl + G], kr_st.t[:], "st_kr", [kr_st.b], [db("k1r%d" % seq)])
                def lat_mm(s):
                    p = psL[s % 2]
                    for kc in range(8):
                        MM(p.t[:, 0:384], hT.t[:, kc, s * 128:(s + 1) * 128], w1.t[:, kc, 0:384], kc == 0, kc == 7, [w1.b, hT.b], [p.b])
                    for kc in range(8):
                        MM(p.t[:, 512:768], hT.t[:, kc, s * 128:(s + 1) * 128], w1.t[:, kc, 384:640], kc == 0, kc == 7, [w1.b, hT.b], [p.b])

                def lat_chain(s):
                    p = psL[s % 2]
                    lat = lats[s % 2]
                    MEMSET(ss2.t[:], 0.0, [ss2.b])
                    ACT(junk.t[:, 0:384], p.t[:, 0:384], AF.Square, [p.b, ss2.b], [junk.b, ss2.b], accum=ss2.t[:, 0:1])
                    ACT(junk.t[:, 384:640], p.t[:, 512:768], AF.Square, [p.b, ss2.b], [junk.b, ss2.b], accum=ss2.t[:, 1:2])
                    TS(ss2.t[:, 0:1], ss2.t[:, 0:1], 1.0 / 384, EPS, ALU.mult, ALU.add, [ss2.b], [ss2.b])
                    TS(ss2.t[:, 1:2], ss2.t[:, 1:2], 1.0 / 256, EPS, ALU.mult, ALU.add, [ss2.b], [ss2.b])
                    TS(ss2.t[:], ss2.t[:], -0.5, None, ALU.pow, None, [ss2.b], [ss2.b])
                    STT(lat.t[:, 0:384], p.t[:, 0:384], ss2.t[:, 0:1], gql.t[:], ALU.mult, ALU.mult, [p.b, ss2.b, gql.b], [lat.b])
                    STT(lat.t[:, 384:640], p.t[:, 512:768], ss2.t[:, 1:2], gkl.t[:], ALU.mult, ALU.mult, [p.b, ss2.b, gkl.b], [lat.b])

                def lat_tr(s):
                    lat = lats[s % 2]
                    pt = psT[s % 2]
                    for c in range(5):
                        TR(pt.t[:, c, :], lat.t[:, c * 128:(c + 1) * 128], ident_b.t[:], [lat.b, ident_b.b], [pt.b])
                    COPY("act", latT.t[:, :, s * 128:(s + 1) * 128], pt.t[:, 0:5, :], [pt.b], [latT.b])

                lat_mm(0)
                lat_chain(0)
                lat_mm(1)
                lat_chain(1)
                lat_mm(2)
                lat_tr(0)
                lat_chain(2)
                lat_mm(3)
                lat_tr(1)
                lat_chain(3)
                lat_tr(2)
                lat_tr(3)
                for h in range(16):
                    if h % 2 == 0:
                        pa_t, pb_t, pa_b, pb_b = psF[0].t[0:96, :], psF[1].t[0:96, :], psF[0].b, psF[1].b
                    else:
                        pa_t, pb_t, pa_b, pb_b = psL[0].t[0:96, 0:512], psL[1].t[0:96, 0:512], psL[0].b, psL[1].b
                    qs = q_st[h % 4]
                    for kc in range(3):
                        MM(pa_t, wq.t[:, kc, h * 96:(h + 1) * 96], latT.t[:, kc, :], kc == 0, kc == 2, [wq.b, latT.b], [pa_b])
                    for kc in range(3):
                        MM(pb_t, wq.t[:, kc, 1536 + h * 96:1536 + (h + 1) * 96], latT.t[:, kc, :], kc == 0, kc == 2, [wq.b, latT.b], [pb_b])
                    TT(ta.t[:], pa_t, c1t.t[:], ALU.mult, [pa_b, c1t.b], [ta.b])
                    TT(tb.t[:], pb_t, s1t.t[:], ALU.mult, [pb_b, s1t.b], [tb.b])
                    TT(qs.t[:], ta.t[:], tb.t[:], ALU.add, [ta.b, tb.b], [qs.b])
                    STORE(q1T_d[h, :, t0:t0 + G], qs.t[:], "st_q1", [qs.b], [db("q1T")])
                mid_hook()
                kn_st = kn_sts[gi % 2]
                v_st = v_sts[gi % 2]
                for c in range(8):
                    p = psY[c % 2]
                    for kc in range(2):
                        MM(p.t[:], wkv.t[:, kc, c * 128:(c + 1) * 128], latT.t[:, 3 + kc, :], kc == 0, kc == 1, [wkv.b, latT.b], [p.b])
                    COPY("act", kn_st.t[:, c, :], p.t[:], [p.b], [kn_st.b])
                if seq == 0:
                    STORE(k1n_p[:, tl:tl + G].rearrange("(c p) t -> p c t", p=128), kn_st.t[:], "st_kn", [kn_st.b], [db("k1n0")])
                else:
                    for c in range(8):
                        STORE(k1n_c[c][:, tl:tl + G], kn_st.t[:, c, :], "st_kn", [kn_st.b], [db("k1n1")])
                for s in range(4):
                    for n in range(2):
                        p = psY[(s * 2 + n) % 2]
                        for kc in range(2):
                            MM(p.t[:], latT.t[:, 3 + kc, s * 128:(s + 1) * 128], wkv.t[:, kc, 1024 + n * 512:1024 + (n + 1) * 512],
                               kc == 0, kc == 1, [wkv.b, latT.b], [p.b])
                        COPY("act", v_st.t[:, s, n * 8:(n + 1) * 8, 0:64], p.t[:].rearrange("p (h d) -> p h d", d=64), [p.b], [v_st.b])
                if seq == 0:
                    STORE(v1_p[tl:tl + G, :].rearrange("(s p) c -> p s c", p=128),
                          v_st.t[:].rearrange("p s h c -> p s (h c)"), "st_v1", [v_st.b], [db("v10")])
                else:
                    for h in range(16):
                        STORE(vview(v1_c[h])[tl:tl + G, :].rearrange("(s p) c -> p s c", p=128), v_st.t[:, :, h, :],
                              "st_v1", [v_st.b], [db("v11")])

            order = list(range(8, 16)) + list(range(0, 8))
            p3_loads(0, order[0])
            norm_part(x1s[0], gO, hb, ss, rstd, junk, 4)
            transpose_part(hb, hT, psT, 4)
            for gi, g in enumerate(order):
                nxt = gi + 1 < 16
                if nxt:
                    p3_loads(gi + 1, order[gi + 1])
                p3_group(gi, g, (lambda gi=gi: norm_part(x1s[(gi + 1) % 2], gO, hb, ss, rstd, junk, 4)) if nxt else (lambda: None))
                if nxt:
                    transpose_part(hb, hT, psT, 4)
            cc_list = [("k1r", k1r_c, k1r_g)] + [("k1n", k1n_c[i], k1n_g[i]) for i in range(8)] + [("v1", v1_c[h], v1_g[h]) for h in range(16)]
            for ci, (nm, src, dst) in enumerate(cc_list):
                S.op("pool", (lambda s_, d_: (lambda e: e.collective_compute("AllGather", ALU.bypass, replica_groups=GROUPS4, ins=[s_], outs=[d_])))(src, dst),
                     reads=[db(nm + "1")], writes=[db("ccg1")], dma=True, stream="cc1", inc=1)
            S.barrier()
        if STOP <= 4:
            S.emit()
            return nc

        def k1_loader(seq, h, kb_):
            if seq == 0:
                LOAD(kb_.t[0:64, 0:SEQ_OWN], k1n_p[h * 64:(h + 1) * 64, :], "ld_k" + kb_.b.name, [db("k1n0")], [kb_.b])
                LOAD(kb_.t[64:96, 0:SEQ_OWN], k1r_p[:, :], "ld_k" + kb_.b.name, [db("k1r0")], [kb_.b])
            else:
                LOAD(kb_.t[0:64, :].rearrange("p (r t) -> p r t", r=4),
                     k1n_g[h // 2].rearrange("(r f) t -> f r t", f=128)[(h % 2) * 64:(h % 2 + 1) * 64], "ld_k" + kb_.b.name, [db("ccg1")], [kb_.b])
                LOAD(kb_.t[64:96, :].rearrange("p (r t) -> p r t", r=4),
                     k1r_g.rearrange("(r f) t -> f r t", f=32), "ld_k" + kb_.b.name, [db("ccg1")], [kb_.b])

        def v1_loader(seq, h, vb_):
            if seq == 0:
                LOAD(vb_.t[:, 0:32, :], v1_p[:, h * 65:(h + 1) * 65].rearrange("(t p) c -> p t c", p=128), "ld_v" + vb_.b.name, [db("v10")], [vb_.b])
            else:
                LOAD(vb_.t[:], vview(v1_g[h]).rearrange("(t p) c -> p t c", p=128), "ld_v" + vb_.b.name, [db("ccg1")], [vb_.b])

        def q1_loader(seq, h, qb_):
            LOAD(qb_.t[0:96, :], q1T_d[h, :, seq * SEQ_OWN:(seq + 1) * SEQ_OWN], "ld_q" + qb_.b.name, [db("q1T")], [qb_.b])

        dense_phase(1, 96, 96 ** -0.5, 16, 1, k1_loader, v1_loader, q1_loader, 0)
        if STOP <= 5:
            S.emit()
            return nc

        outproj_phase(1, w_out_o, x1_d, [db("x1")], True)
        S.emit()
    return nc


def _na_table(r0, R, base_off, nkt, rpb):
    q = np.arange(128)
    qr = r0 + q // 64
    qc = q % 64
    k = np.arange(128)
    kr_in = k // 64
    kc = k % 64
    rs = np.clip(qr - 4, 0, R - 8)
    cs = np.clip(qc - 8, 0, 64 - 16)
    out = np.full((128, 8, nkt, 128), NEG, np.float32)
    for kt in range(nkt):
        kr = r0 + base_off + 2 * kt + kr_in
        valid = ((kr[:, None] >= rs[None, :]) & (kr[:, None] < rs[None, :] + 8) & (kr[:, None] >= 0) & (kr[:, None] < R)
                 & (kc[:, None] >= cs[None, :]) & (kc[:, None] < cs[None, :] + 16))
        ro = np.clip(kr[:, None] - qr[None, :] + 7, 0, 14)
        co = np.clip(kc[:, None] - qc[None, :] + 15, 0, 30)
        vals = rpb[:, ro, co]
        vals = np.where(valid[None], vals, np.float32(NEG))
        out[:, :, kt, :] = vals.transpose(1, 0, 2)
    return out


def _rope_tables(pos, rot_dim):
    nf = rot_dim // 4
    inv = (np.float32(10000.0) ** (-np.arange(nf, dtype=np.float32) / np.float32(nf))).astype(np.float32)
    row = (pos // 64).astype(np.float32)
    col = (pos % 64).astype(np.float32)
    ang = np.concatenate([row[:, None] * inv[None], col[:, None] * inv[None]], axis=-1).astype(np.float32)
    return np.cos(ang).astype(np.float32), np.sin(ang).astype(np.float32)


_PROG = None


def kernel(x_prompt, x_sample, norm_e, w_in_e, rpb_a, qnorm_b, knorm_b, w_out_e,
           norm_o, w_in_o, qlat_g, kvlat_g, w_uq, w_ukv, w_out_o, norm_f):
    global _PROG
    f = lambda a: np.ascontiguousarray(np.asarray(a, dtype=np.float32))
    x_prompt, x_sample = f(x_prompt), f(x_sample)
    w_in_e0, w_out_e0, w_in_o0, w_uq0, w_ukv0, w_out_o0 = f(w_in_e)[0], f(w_out_e)[0], f(w_in_o)[0], f(w_uq)[0], f(w_ukv)[0], f(w_out_o)[0]
    rpb = f(rpb_a)[0]
    kr = w_in_o0[:, 640:672]
    w1x = np.concatenate([w_in_o0, kr[:, 16:32], kr[:, 0:16]], axis=1)
    uq = w_uq0.reshape(384, 16, 96)
    uq_sw = np.concatenate([uq[:, :, 0:64], uq[:, :, 80:96], uq[:, :, 64:80]], axis=2)
    wuq2 = np.concatenate([uq.reshape(384, 1536), uq_sw.reshape(384, 1536)], axis=1)
    ukv = w_ukv0.reshape(256, 16, 128)
    wukv2 = np.concatenate([ukv[:, :, 0:64].reshape(256, 1024), ukv[:, :, 64:128].reshape(256, 1024)], axis=1)
    bm_int = _na_table(8, 64, -4, 5, rpb).reshape(128, -1)
    ident = np.eye(128, dtype=np.float32)
    selA_h = np.tile(np.eye(8, dtype=np.float32).reshape(1, 64), (128, 1))
    selB_h = np.zeros((128, 8, 64), np.float32)
    for hh in range(8):
        selB_h[64 + hh, hh, :] = 1.0
    selB_h = selB_h.reshape(128, 512)
    common = dict(w_in_e=w_in_e0, w_out_e=w_out_e0, w1x=f(w1x), wuq2=f(wuq2), wukv2=f(wukv2), w_out_o=w_out_o0,
                  norm_e=f(norm_e).reshape(1, 1024), norm_o=f(norm_o).reshape(1, 1024), norm_f=f(norm_f).reshape(1, 1024),
                  qnorm_b=f(qnorm_b).reshape(1, 64), knorm_b=f(knorm_b).reshape(1, 64),
                  qlat_g=f(qlat_g).reshape(1, 384), kvlat_g=f(kvlat_g).reshape(1, 256),
                  bm_int=f(bm_int), ident=ident, selA=selA_h, selB=selB_h)
    edge_p = [_na_table(r0, 64, -6, 7, rpb).reshape(128, -1) for r0 in (0, 2, 60, 62)]
    in_maps = []
    for c in range(8):
        sq, j = c // 4, c % 4
        xs = x_sample[sq]
        x_own = np.concatenate([x_prompt[c], xs[j * 4096:(j + 1) * 4096]], axis=0)
        halo = np.zeros((1024, 1024), np.float32)
        if j > 0:
            halo[0:512] = xs[j * 4096 - 512:j * 4096]
        if j < 3:
            halo[512:1024] = xs[(j + 1) * 4096:(j + 1) * 4096 + 512]
        pos = np.concatenate([np.arange(4096), j * 4096 + np.arange(4096)])
        cos0, sin0 = _rope_tables(pos, 64)
        cs0 = np.concatenate([cos0, cos0, sin0, sin0], axis=1)
        cos1, sin1 = _rope_tables(pos, 32)
        c1 = np.ones((96, 8192), np.float32)
        s1 = np.zeros((96, 8192), np.float32)
        c1[64:80] = cos1.T
        c1[80:96] = cos1.T
        s1[64:80] = -sin1.T
        s1[80:96] = sin1.T
        edge_s = [_na_table(64 * j + r0, 256, -6, 7, rpb).reshape(128, -1) for r0 in (0, 2, 60, 62)]
        m = dict(common)
        m.update(x_own=f(x_own), x_halo=halo, cs0=f(cs0), c1=c1, s1=s1, ck=f(c1[64:96]), sk=f(s1[64:96]),
                 bm_edge=f(np.stack(edge_p + edge_s, axis=0)))
        in_maps.append(m)
    if _PROG is None:
        _PROG = build_program()
    res = run_bass_kernel_spmd(_PROG, in_maps, core_ids=list(range(8)))
    if DEBUG:
        kernel.last = res.results
    y_prompt = np.stack([np.asarray(res.results[c]["y_out"])[0:4096] for c in range(8)], axis=0)
    y_sample = np.stack([np.concatenate([np.asarray(res.results[sq * 4 + j]["y_out"])[4096:8192] for j in range(4)], axis=0)
                         for sq in range(2)], axis=0)
    return (y_prompt.astype(np.float32), y_sample.astype(np.float32))
```

```python
import contextlib
import os
import numpy as np
import concourse.bass as bass
import concourse.mybir as mybir
from concourse.bass_utils import run_bass_kernel_spmd

F32 = mybir.dt.float32
BF16 = mybir.dt.bfloat16
AF = mybir.ActivationFunctionType
ALU = mybir.AluOpType
AX = mybir.AxisListType

ENGS = ("pe", "act", "dve", "pool", "sp")
MAXV = 8000
EPS = 1e-6
NEG = -30000.0
DEBUG = bool(int(os.environ.get("KDEBUG", "0")))
STOP = int(os.environ.get("KSTOP", "99"))
NOCC = os.environ.get("KNOCC", "")
KSUB = int(os.environ.get("KSUB", "99"))
KNG = int(os.environ.get("KNG", "99"))
KP1 = int(os.environ.get("KP1", "99"))
KQT = int(os.environ.get("KQT", "32"))


class Buf:
    __slots__ = ("name", "writers", "readers", "excl")

    def __init__(self, name="", excl=False):
        self.name = name
        self.writers = []
        self.readers = []
        self.excl = excl


class Op:
    __slots__ = ("eng", "fn", "deps", "dma", "stream", "idx", "signal", "sem", "val", "inc")


class Sched:
    def __init__(self, nc):
        self.nc = nc
        self.ops = {e: [] for e in ENGS}
        self.streams = {}
        self.bar = []

    def op(self, eng, fn, reads=(), writes=(), dma=False, stream=None, inc=None):
        o = Op()
        o.eng, o.fn, o.dma, o.stream = eng, fn, dma, stream
        o.signal = dma
        o.sem, o.val = None, 0
        o.inc = inc if inc is not None else (16 if dma else 1)
        deps = list(self.bar)
        if any(b.excl for b in reads):
            writes = list(writes) + [b for b in reads if b.excl and b not in writes]
            reads = [b for b in reads if not b.excl]
        for b in reads:
            deps.extend(b.writers)
        for b in writes:
            deps.extend(b.writers)
            deps.extend(b.readers)
        best = {}
        for d in deps:
            if (not d.dma) and d.eng == "pe" and eng == "pe" and not dma:
                continue
            key = ("s", d.stream) if d.dma else ("e", d.eng)
            if key not in best or best[key].idx < d.idx:
                best[key] = d
        o.deps = list(best.values())
        for d in o.deps:
            d.signal = True
        if dma:
            lst = self.streams.setdefault(stream, [])
            o.idx = len(lst)
            lst.append(o)
        else:
            o.idx = len(self.ops[eng])
        self.ops[eng].append(o)
        for b in reads:
            b.readers.append(o)
        for b in writes:
            b.writers = [o]
            b.readers = []
        return o

    def barrier(self):
        bar = []
        for e in ENGS:
            comp = [o for o in self.ops[e] if not o.dma]
            if comp:
                bar.append(comp[-1])
        for sname, lst in self.streams.items():
            if lst and not str(sname).startswith("cc"):
                bar.append(lst[-1])
        self.bar = bar

    def emit(self, final_eng="sp"):
        nc = self.nc
        semreq = []

        def newsem(name):
            semreq.append(name)
            return len(semreq) - 1

        for e in ENGS:
            cur, cnt = None, 0
            for o in self.ops[e]:
                if o.dma or not o.signal:
                    continue
                if cur is None or cnt >= MAXV:
                    cur, cnt = newsem(f"m{e}{len(semreq)}"), 0
                cnt += 1
                o.sem, o.val = cur, cnt
        for sname, lst in self.streams.items():
            cur, cnt = None, 0
            for o in lst:
                if cur is None or cnt >= MAXV * 16:
                    cur, cnt = newsem(f"d{len(semreq)}"), 0
                cnt += o.inc
                o.sem, o.val = cur, cnt
        print("semaphores requested:", len(semreq), flush=True)
        with contextlib.ExitStack() as st:
            sems = [st.enter_context(nc.semaphore(n)) for n in semreq]
            block = st.enter_context(nc.Block())
            handles = {"pe": "tensor", "act": "scalar", "dve": "vector", "pool": "gpsimd", "sp": "sync"}
            finals = [lst[-1] for lst in self.streams.values() if lst]

            def make(e):
                def body(eh):
                    waited = {}
                    for o in self.ops[e]:
                        for d in o.deps:
                            if waited.get(d.sem, 0) < d.val:
                                eh.wait_ge(sems[d.sem], d.val)
                                waited[d.sem] = d.val
                        ins = o.fn(eh)
                        if o.signal:
                            ins.then_inc(sems[o.sem], o.inc)
                    if e == final_eng:
                        for d in finals:
                            if waited.get(d.sem, 0) < d.val:
                                eh.wait_ge(sems[d.sem], d.val)
                                waited[d.sem] = d.val
                return body

            for e in ENGS:
                getattr(block, handles[e])(make(e))


NTOK = 8192
SEQ_OWN = 4096
NLOC = 5120
G = 512


def build_program():
    nc = bass.Bass("TRN2", target_bir_lowering=False)
    S = Sched(nc)
    scratch_kind = "ExternalOutput" if DEBUG else "Internal"

    def din(name, shape, dt=F32):
        return nc.dram_tensor(name, list(shape), dt, kind="ExternalInput").ap()

    def dscr(name, shape, dt, dbg=True):
        return nc.dram_tensor(name, list(shape), dt, kind=(scratch_kind if dbg else "Internal")).ap()

    x_own = din("x_own", [NTOK, 1024])
    x_halo = din("x_halo", [1024, 1024])
    w_in_e = din("w_in_e", [1024, 3328])
    w_out_e = din("w_out_e", [1024, 1024])
    w1x = din("w1x", [1024, 1728])
    wuq2 = din("wuq2", [384, 3072])
    wukv2 = din("wukv2", [256, 2048])
    w_out_o = din("w_out_o", [1024, 1024])
    norm_e = din("norm_e", [1, 1024])
    norm_o = din("norm_o", [1, 1024])
    norm_f = din("norm_f", [1, 1024])
    qnorm_b = din("qnorm_b", [1, 64])
    knorm_b = din("knorm_b", [1, 64])
    qlat_g = din("qlat_g", [1, 384])
    kvlat_g = din("kvlat_g", [1, 256])
    cs0 = din("cs0", [NTOK, 128])
    c1 = din("c1", [96, NTOK])
    s1 = din("s1", [96, NTOK])
    ck = din("ck", [32, NTOK])
    sk = din("sk", [32, NTOK])
    bm_int = din("bm_int", [128, 8 * 5 * 128])
    bm_edge = din("bm_edge", [8, 128, 8 * 7 * 128])
    ident_in = din("ident", [128, 128])
    selA_in = din("selA", [128, 64])
    selB_in = din("selB", [128, 512])
    y_out = nc.dram_tensor("y_out", [NTOK, 1024], F32, kind="ExternalOutput").ap()

    qaT_d = dscr("qaT_d", [512, NTOK], BF16)
    kaT_d = dscr("kaT_d", [2, 512, NLOC], BF16)
    va_d = dscr("va_d", [2, NLOC, 520], BF16)
    qbT_d = dscr("qbT_d", [512, NTOK], BF16)
    kbT_p = dscr("kbT_p", [128, SEQ_OWN], BF16)
    vb_p = dscr("vb_p", [SEQ_OWN, 130], BF16)
    kbT_c = dscr("kbT_c", [128, SEQ_OWN], BF16, dbg=False)
    vb_c = [dscr(f"vb_c{g}", [128, 2080], BF16, dbg=False) for g in range(2)]
    kbT_g = dscr("kbT_g", [512, SEQ_OWN], BF16, dbg=False)
    vb_g = [dscr(f"vb_g{g}", [512, 2080], BF16, dbg=False) for g in range(2)]
    gT_d = [dscr(f"gT{l}_d", [1024, NTOK], F32) for l in range(2)]
    mgT_d = [dscr(f"mgT{l}_d", [1024, NTOK], BF16) for l in range(2)]
    x1_d = dscr("x1_d", [NTOK, 1024], F32)
    q1T_d = dscr("q1T_d", [16, 96, NTOK], BF16)
    k1n_p = dscr("k1n_p", [1024, SEQ_OWN], BF16)
    k1r_p = dscr("k1r_p", [32, SEQ_OWN], BF16)
    v1_p = dscr("v1_p", [SEQ_OWN, 1040], BF16)
    k1n_c = [dscr(f"k1n_c{i}", [128, SEQ_OWN], BF16, dbg=False) for i in range(8)]
    k1r_c = dscr("k1r_c", [32, SEQ_OWN], BF16, dbg=False)
    v1_c = [dscr(f"v1_c{h}", [128, 2080], BF16, dbg=False) for h in range(16)]
    k1n_g = [dscr(f"k1n_g{i}", [512, SEQ_OWN], BF16, dbg=False) for i in range(8)]
    k1r_g = dscr("k1r_g", [128, SEQ_OWN], BF16, dbg=False)
    v1_g = [dscr(f"v1_g{h}", [512, 2080], BF16, dbg=False) for h in range(16)]

    def vview(ap):
        return ap.rearrange("p (a c) -> (p a) c", c=65)

    D = {}

    def db(name):
        if name not in D:
            D[name] = Buf(name)
        return D[name]

    GROUPS4 = [[0, 1, 2, 3], [4, 5, 6, 7]]

    def LOAD(out, in_, stream, reads=(), writes=(), q="sp"):
        return S.op(q, lambda e: e.dma_start(out=out, in_=in_), reads=reads, writes=writes, dma=True, stream=stream)

    def STORE(out, in_, stream, reads=(), writes=()):
        return S.op("sp", lambda e: e.dma_start(out=out, in_=in_), reads=reads, writes=writes, dma=True, stream=stream)

    def MM(out, lhsT, rhs, start, stop, reads, writes):
        return S.op("pe", lambda e: e.matmul(out, lhsT=lhsT, rhs=rhs, start=start, stop=stop), reads=reads, writes=writes)

    def TR(out, in_, ident, reads, writes):
        return S.op("pe", lambda e: e.transpose(out=out, in_=in_, identity=ident), reads=reads, writes=writes)

    def ACT(out, in_, func, reads, writes, scale=1.0, accum=None):
        if accum is None:
            return S.op("act", lambda e: e.activation(out=out, in_=in_, func=func, scale=scale), reads=reads, writes=writes)
        return S.op("act", lambda e: e.activation(out=out, in_=in_, func=func, scale=scale, accum_out=accum), reads=reads, writes=writes)

    def DVE(fn, reads, writes):
        return S.op("dve", fn, reads=reads, writes=writes)

    def COPY(eng, out, in_, reads, writes):
        if eng == "act":
            return S.op("act", lambda e: e.copy(out=out, in_=in_), reads=reads, writes=writes)
        return S.op(eng, lambda e: e.tensor_copy(out=out, in_=in_), reads=reads, writes=writes)

    def TT(out, in0, in1, op, reads, writes, eng="dve"):
        return S.op(eng, lambda e: e.tensor_tensor(out=out, in0=in0, in1=in1, op=op), reads=reads, writes=writes)

    def TS(out, in0, s1_, s2_, op0, op1, reads, writes):
        if op1 is None:
            assert op0 == ALU.pow and s1_ == -0.5
            S.op("act", lambda e: e.activation(out=out, in_=in0, func=AF.Sqrt), reads=reads, writes=writes)
            return S.op("dve", lambda e: e.reciprocal(out=out, in_=out), reads=writes, writes=writes)
        return S.op("dve", lambda e: e.tensor_scalar(out=out, in0=in0, scalar1=s1_, scalar2=s2_, op0=op0, op1=op1), reads=reads, writes=writes)

    def STT(out, in0, scalar, in1, op0, op1, reads, writes):
        return S.op("dve", lambda e: e.scalar_tensor_tensor(out=out, in0=in0, scalar=scalar, in1=in1, op0=op0, op1=op1), reads=reads, writes=writes)

    def MEMSET(ap, val, writes, eng="dve"):
        return S.op(eng, lambda e: e.memset(ap, val), writes=writes)

    class T:
        def __init__(self, t, name, excl=False):
            self.t = t
            self.b = Buf(name, excl)

    with contextlib.ExitStack() as top:
        uid = [0]

        def sbuf(st, name, shape, dt):
            uid[0] += 1
            name = f"{name}_{uid[0]}"
            return T(st.enter_context(nc.sbuf_tensor(name, list(shape), dt)), name)

        def psum(st, name, shape, dt):
            uid[0] += 1
            name = f"{name}_{uid[0]}"
            return T(st.enter_context(nc.psum_tensor(name, list(shape), dt)), name, excl=True)

        ident_f = sbuf(top, "ident_f", [128, 128], F32)
        ident_b = sbuf(top, "ident_b", [128, 128], BF16)
        LOAD(ident_f.t[:], ident_in, "c_ident", writes=[ident_f.b])
        COPY("dve", ident_b.t[:], ident_f.t[:], [ident_f.b], [ident_b.b])
        ones_f = sbuf(top, "ones_f", [128, 64], F32)
        MEMSET(ones_f.t[:], 1.0, [ones_f.b])
        selA = sbuf(top, "selA", [128, 8, 8], F32)
        selB = sbuf(top, "selB", [128, 8, 64], F32)
        LOAD(selA.t[:].rearrange("p h m -> p (h m)"), selA_in, "c_selA", writes=[selA.b])
        LOAD(selB.t[:].rearrange("p h m -> p (h m)"), selB_in, "c_selB", writes=[selB.b])

        def bcast_load(st, name, src, n):
            t = sbuf(st, name, [128, n], F32)
            LOAD(t.t[:], src.broadcast_to([128, n]), "c_" + name, writes=[t.b])
            return t

        def load_weight(st, dst, src, nk, ncols, stage):
            for k in range(nk):
                sg = stage[k % 2]
                LOAD(sg.t[:, 0:ncols], src[k * 128:(k + 1) * 128, :], "wst" + sg.b.name, writes=[sg.b])
                COPY("act" if k % 2 else "dve", dst.t[:, k, :], sg.t[:, 0:ncols], [sg.b], [dst.b])

        def norm_part(xg, gvec, hb, ss, rstd, junk, nsub):
            MEMSET(ss.t[:], 0.0, [ss.b])
            for s in range(nsub):
                ACT(junk.t[:], xg.t[:, s, :], AF.Square, [xg.b, ss.b], [junk.b, ss.b], accum=ss.t[:, s:s + 1])
            TS(rstd.t[:], ss.t[:], 1.0 / 1024, EPS, ALU.mult, ALU.add, [ss.b], [rstd.b])
            TS(rstd.t[:], rstd.t[:], -0.5, None, ALU.pow, None, [rstd.b], [rstd.b])
            for s in range(nsub):
                STT(hb.t[:, s, :], xg.t[:, s, :], rstd.t[:, s:s + 1], gvec.t[:], ALU.mult, ALU.mult,
                    [xg.b, rstd.b, gvec.b], [hb.b])

        def transpose_part(hb, hT, psT, nsub):
            for s in range(nsub):
                p = psT[s % 2]
                for kc in range(8):
                    TR(p.t[:, kc, :], hb.t[:, s, kc * 128:(kc + 1) * 128], ident_b.t[:], [hb.b, ident_b.b], [p.b])
                COPY("act" if s % 2 else "dve", hT.t[:, :, s * 128:(s + 1) * 128], p.t[:], [p.b], [hT.b])

        with contextlib.ExitStack() as ph:
            w0 = sbuf(ph, "w0", [128, 8, 3328], BF16)
            with contextlib.ExitStack() as wst:
                stage = [sbuf(wst, f"stg{i}", [128, 3328], F32) for i in range(2)]
                load_weight(wst, w0, w_in_e, 8, 3328, stage)
                S.barrier()
            gE = bcast_load(ph, "gE", norm_e, 1024)
            gq = bcast_load(ph, "gq", qnorm_b, 64)
            gk = bcast_load(ph, "gk", knorm_b, 64)
            xg = [sbuf(ph, f"xg{i}", [128, 4, 1024], F32) for i in range(2)]
            cst = [sbuf(ph, f"cst{i}", [128, 4, 128], F32) for i in range(2)]
            hb = sbuf(ph, "hb", [128, 4, 1024], BF16)
            hT = sbuf(ph, "hT", [128, 8, 512], BF16)
            junk = sbuf(ph, "junk", [128, 1024], BF16)
            ss = sbuf(ph, "ss", [128, 4], F32)
            rstd = sbuf(ph, "rstd", [128, 4], F32)
            fm_st = [sbuf(ph, f"fm_st{i}", [128, 8, 512], BF16) for i in range(2)]
            g_st = sbuf(ph, "g_st", [128, 8, 512], F32)
            va_st = sbuf(ph, "va_st", [128, 4, 8, 65], BF16)
            vb_st = sbuf(ph, "vb_st", [128, 4, 2, 65], BF16)
            sq = sbuf(ph, "sq", [128, 640], F32)
            ssq = sbuf(ph, "ssq", [128, 10], F32)
            qn = sbuf(ph, "qn", [128, 10, 64], F32)
            tA = sbuf(ph, "tA", [128, 10, 64], F32)
            tB = sbuf(ph, "tB", [128, 10, 64], F32)
            qrs = [sbuf(ph, f"qr{i}", [128, 10, 64], BF16) for i in range(2)]
            qbT_st = sbuf(ph, "qbT_st", [128, 4, 512], BF16)
            kbT_st = sbuf(ph, "kbT_st", [128, 512], BF16)
            psT = [psum(ph, f"psT{i}", [128, 8, 128], BF16) for i in range(2)]
            psF = [psum(ph, f"psF{i}", [128, 512], F32) for i in range(2)]
            psK = [psum(ph, f"psK{i}", [128, 1024], F32) for i in range(2)]
            MEMSET(va_st.t[:], 1.0, [va_st.b])
            MEMSET(vb_st.t[:], 1.0, [vb_st.b])

            def p0_loads(gi, src, seq, own_idx, halo_col):
                xb = xg[gi % 2]
                LOAD(xb.t[:], src.rearrange("(s p) d -> p s d", p=128), "xg" + xb.b.name, writes=[xb.b], q="act")
                if own_idx is not None:
                    t0 = own_idx * G
                    cb = cst[gi % 2]
                    LOAD(cb.t[:], cs0[t0:t0 + G, :].rearrange("(s p) c -> p s c", p=128), "cs" + cb.b.name, writes=[cb.b], q="act")

            def p0_group(gi, src, seq, own_idx, halo_col, mid_hook, late_hook):
                xb = xg[gi % 2]
                own = own_idx is not None
                if own:
                    t0 = own_idx * G
                    cb = cst[gi % 2]
                    loc = 512 + (own_idx % 8) * G
                else:
                    loc = halo_col
                if KSUB <= 0:
                    return
                if KSUB <= 1:
                    return
                fm = fm_st[gi % 2]
                chunks = ([0, 1, 2, 3] if own else []) + [4, 5, 6, 7]
                for j, cc in enumerate(chunks):
                    p = psF[j % 2]
                    for kc in range(8):
                        MM(p.t[:], w0.t[:, kc, cc * 128:(cc + 1) * 128], hT.t[:, kc, :], kc == 0, kc == 7, [w0.b, hT.b], [p.b])
                    COPY("act" if j % 2 else "dve", fm.t[:, cc, :], p.t[:], [p.b], [fm.b])
                if own:
                    STORE(qaT_d[:, t0:t0 + G].rearrange("(c p) t -> p c t", p=128), fm.t[:, 0:4, :], "st_qa", [fm.b], [db("qaT")])
                STORE(kaT_d[seq, :, loc:loc + G].rearrange("(c p) t -> p c t", p=128), fm.t[:, 4:8, :], "st_ka", [fm.b], [db("kaT")])
                if KSUB <= 2:
                    return
                if own:
                    for j in range(8):
                        cc = 18 + j
                        p = psF[j % 2]
                        for kc in range(8):
                            MM(p.t[:], w0.t[:, kc, cc * 128:(cc + 1) * 128], hT.t[:, kc, :], kc == 0, kc == 7, [w0.b, hT.b], [p.b])
                        ACT(g_st.t[:, j, :], p.t[:], AF.Silu, [p.b], [g_st.b])
                    STORE(gT_d[0][:, t0:t0 + G].rearrange("(c p) t -> p c t", p=128), g_st.t[:], "st_g0", [g_st.b], [db("gT0")])
                mid_hook()
                if KSUB <= 3:
                    return
                for s in range(4):
                    p = psF[s % 2]
                    for kc in range(8):
                        MM(p.t[:], hT.t[:, kc, s * 128:(s + 1) * 128], w0.t[:, kc, 1024:1536], kc == 0, kc == 7, [w0.b, hT.b], [p.b])
                    COPY("act", va_st.t[:, s, :, 0:64], p.t[:].rearrange("p (h d) -> p h d", d=64), [p.b], [va_st.b])
                STORE(va_d[seq, loc:loc + G, :].rearrange("(s p) c -> p s c", p=128), va_st.t[:].rearrange("p s h c -> p s (h c)"),
                      "st_va", [va_st.b], [db("va")])
                if not own or KSUB <= 4:
                    late_hook()
                    return
                def tm_mm(s):
                    p = psK[s % 2]
                    for kc in range(8):
                        MM(p.t[:, 0:512], hT.t[:, kc, s * 128:(s + 1) * 128], w0.t[:, kc, 1536:2048], kc == 0, kc == 7, [w0.b, hT.b], [p.b])
                    for kc in range(8):
                        MM(p.t[:, 512:768], hT.t[:, kc, s * 128:(s + 1) * 128], w0.t[:, kc, 2048:2304], kc == 0, kc == 7, [w0.b, hT.b], [p.b])

                def tm_chain(s):
                    p = psK[s % 2]
                    qr = qrs[s % 2]
                    COPY("act", vb_st.t[:, s, :, 0:64], p.t[:, 640:768].rearrange("p (h d) -> p h d", d=64), [p.b], [vb_st.b])
                    p3 = p.t[:, 0:640].rearrange("p (h d) -> p h d", d=64)
                    ACT(sq.t[:], p.t[:, 0:640], AF.Square, [p.b], [sq.b])
                    DVE(lambda e: e.reduce_sum(out=ssq.t[:], in_=sq.t[:].rearrange("p (h d) -> p h d", d=64), axis=AX.X), [sq.b], [ssq.b])
                    TS(ssq.t[:], ssq.t[:], 1.0 / 64, EPS, ALU.mult, ALU.add, [ssq.b], [ssq.b])
                    TS(ssq.t[:], ssq.t[:], -0.5, None, ALU.pow, None, [ssq.b], [ssq.b])
                    TT(qn.t[:], p3, ssq.t[:].unsqueeze(2).broadcast_to([128, 10, 64]), ALU.mult, [p.b, ssq.b], [qn.b])
                    TT(qn.t[:, 0:8, :], qn.t[:, 0:8, :], gq.t[:].unsqueeze(1).broadcast_to([128, 8, 64]), ALU.mult, [qn.b, gq.b], [qn.b])
                    TT(qn.t[:, 8:10, :], qn.t[:, 8:10, :], gk.t[:].unsqueeze(1).broadcast_to([128, 2, 64]), ALU.mult, [qn.b, gk.b], [qn.b])
                    cosb = cb.t[:, s, 0:64].unsqueeze(1).broadcast_to([128, 10, 64])
                    sinb = cb.t[:, s, 64:128].unsqueeze(1).broadcast_to([128, 10, 64])
                    TT(tA.t[:], qn.t[:], cosb, ALU.mult, [qn.b, cb.b], [tA.b])
                    TT(tB.t[:], qn.t[:], sinb, ALU.mult, [qn.b, cb.b], [tB.b])
                    TT(qr.t[:, :, 0:32], tA.t[:, :, 0:32], tB.t[:, :, 32:64], ALU.subtract, [tA.b, tB.b], [qr.b])
                    TT(qr.t[:, :, 32:64], tB.t[:, :, 0:32], tA.t[:, :, 32:64], ALU.add, [tA.b, tB.b], [qr.b])

                def tm_tr(s):
                    qr = qrs[s % 2]
                    pt = psT[s % 2]
                    qr2 = qr.t[:].rearrange("p h d -> p (h d)")
                    for c in range(5):
                        TR(pt.t[:, c, :], qr2[:, c * 128:(c + 1) * 128], ident_b.t[:], [qr.b, ident_b.b], [pt.b])
                    COPY("dve", qbT_st.t[:, :, s * 128:(s + 1) * 128], pt.t[:, 0:4, :], [pt.b], [qbT_st.b])
                    COPY("act", kbT_st.t[:, s * 128:(s + 1) * 128], pt.t[:, 4, :], [pt.b], [kbT_st.b])

                tm_mm(0)
                tm_chain(0)
                tm_mm(1)
                tm_chain(1)
                tm_mm(2)
                tm_tr(0)
                tm_chain(2)
                tm_mm(3)
                tm_tr(1)
                late_hook()
                tm_chain(3)

                def tail():
                    tm_tr(2)
                    tm_tr(3)
                    STORE(qbT_d[:, t0:t0 + G].rearrange("(c p) t -> p c t", p=128), qbT_st.t[:], "st_qb", [qbT_st.b], [db("qbT")])
                    tl = (own_idx % 8) * G
                    kdst = kbT_p if seq == 0 else kbT_c
                    STORE(kdst[:, tl:tl + G], kbT_st.t[:], "st_kb", [kbT_st.b], [db("kbT%d" % seq)])
                    if seq == 0:
                        STORE(vb_p[tl:tl + G, :].rearrange("(s p) c -> p s c", p=128), vb_st.t[:].rearrange("p s h c -> p s (h c)"),
                              "st_vb", [vb_st.b], [db("vb0")])
                    else:
                        for g2 in range(2):
                            STORE(vview(vb_c[g2])[tl:tl + G, :].rearrange("(s p) c -> p s c", p=128), vb_st.t[:, :, g2, :],
                                  "st_vb", [vb_st.b], [db("vb1")])
                return tail

            jobs = []
            for hgi in range(min(2, KNG)):
                jobs.append((x_halo[hgi * G:(hgi + 1) * G, :], 1, None, 0 if hgi == 0 else 4608))
            for g in range(8, min(16, 8 + KNG)):
                jobs.append((x_own[g * G:(g + 1) * G, :], 1, g, None))
            n_sample_jobs = len(jobs)
            for g in range(0, min(8, KNG)):
                jobs.append((x_own[g * G:(g + 1) * G, :], 0, g, None))
            p0_loads(0, *jobs[0])
            norm_part(xg[0], gE, hb, ss, rstd, junk, 4)
            transpose_part(hb, hT, psT, 4)
            for gi, job in enumerate(jobs):
                nxt = gi + 1 < len(jobs)
                if nxt:
                    p0_loads(gi + 1, *jobs[gi + 1])
                tail = p0_group(gi, *job, (lambda gi=gi: norm_part(xg[(gi + 1) % 2], gE, hb, ss, rstd, junk, 4)) if nxt else (lambda: None),
                                (lambda: transpose_part(hb, hT, psT, 4)) if nxt else (lambda: None))
                if tail is not None:
                    tail()
                if gi == n_sample_jobs - 1:
                    if "a" not in NOCC:
                        S.op("pool", lambda e: e.collective_compute("AllGather", ALU.bypass, replica_groups=GROUPS4, ins=[kbT_c], outs=[kbT_g]),
                             reads=[db("kbT1")], writes=[db("ccg0")], dma=True, stream="cc0", inc=1)
                    if "b" not in NOCC:
                        for g2 in range(2):
                            S.op("pool", (lambda s_, d_: (lambda e: e.collective_compute("AllGather", ALU.bypass, replica_groups=GROUPS4, ins=[s_], outs=[d_])))(vb_c[g2], vb_g[g2]),
                                 reads=[db("vb1")], writes=[db("ccg0")], dma=True, stream="cc0", inc=1)
            S.barrier()
        if STOP <= 0:
            S.emit()
            return nc

        with contextlib.ExitStack() as ph:
            kaT = sbuf(ph, "kaT", [128, 4, NLOC], BF16)
            va = sbuf(ph, "va", [128, 40, 520], BF16)
            qaT = sbuf(ph, "qaT", [128, 4, SEQ_OWN], BF16)
            bmi = sbuf(ph, "bmi", [128, 8 * 5 * 128], F32)
            bme = sbuf(ph, "bme", [128, 8 * 7 * 128], F32)
            s_sb = [sbuf(ph, f"s_sb{i}", [128, 896], F32) for i in range(3)]
            pT = [sbuf(ph, f"pT{i}", [128, 896], BF16) for i in range(4)]
            oA = [sbuf(ph, f"oA{i}", [65, 8, 128], F32) for i in range(2)]
            gtA = [sbuf(ph, f"gtA{i}", [64, 8, 128], F32) for i in range(2)]
            mgA = [sbuf(ph, f"mgA{i}", [64, 8, 128], BF16) for i in range(2)]
            psS = [psum(ph, f"psS{i}", [128, 1024], F32) for i in range(3)]
            psO = psum(ph, "psO", [128, 4, 128], F32)
            psB = psum(ph, "psB", [128, 4, 128], F32)
            rs8 = sbuf(ph, "rs8", [128, 128], F32)
            LOAD(bmi.t[:], bm_int, "bmi", writes=[bmi.b])
            it = 0
            pend = []
            deferred1 = []

            def tick1():
                for d in list(deferred1):
                    d[0] -= 1
                    if d[0] <= 0:
                        deferred1.remove(d)
                        d[1]()

            def flush_pend(keep=0):
                while len(pend) > keep:
                    pend.pop(0)()

            def mk_pv1(pt, nkt, kt0, h, qt, seq):
                def f():
                    hl, hf = h % 4, h // 4
                    for k in range(nkt):
                        MM(psO.t[0:65, hl, :], va.t[:, kt0 + k, h * 65:(h + 1) * 65], pt.t[:, k * 128:(k + 1) * 128],
                           k == 0, k == nkt - 1, [va.b, pt.b], [psO.b])
                    if hl == 3:
                        ob, gt, mgo = oA[qt % 2], gtA[qt % 2], mgA[qt % 2]
                        t0 = seq * SEQ_OWN + qt * 128
                        hs = slice(hf * 4, hf * 4 + 4)
                        if hf == 0:
                            LOAD(gt.t[:], gT_d[0][0:512, t0:t0 + 128].rearrange("(h d) t -> d h t", d=64), "ld_gt" + gt.b.name, [db("gT0")], [gt.b])
                        COPY("act", ob.t[:, hs, :], psO.t[0:65, :, :], [psO.b], [ob.b])
                        TT(ob.t[0:64, hs, :], ob.t[0:64, hs, :], gt.t[:, hs, :], ALU.mult, [ob.b, gt.b], [ob.b], eng="pool")

                        def post_b():
                            for hh in range(4):
                                MM(psB.t[0:64, hh, :], selB.t[64:68, hh, :], rs8.t[64:68, :], True, True, [selB.b, rs8.b], [psB.b])
                            TT(mgo.t[:, hs, :], ob.t[0:64, hs, :], psB.t[0:64, :, :], ALU.mult, [ob.b, psB.b], [mgo.b])
                            if hf == 1:
                                STORE(mgT_d[0][0:512, t0:t0 + 128].rearrange("(h d) t -> d h t", d=64), mgo.t[:], "st_oA", [mgo.b], [db("mgT0")])

                        def post_a():
                            for hh in range(4):
                                MM(psB.t[64:72, 0, :], selA.t[64:65, hh, :], ob.t[64:65, hf * 4 + hh, :], hh == 0, hh == 3, [selA.b, ob.b], [psB.b])
                            DVE(lambda e: e.reciprocal(out=rs8.t[64:68, :], in_=psB.t[64:68, 0, :]), [psB.b], [rs8.b])
                            deferred1.append([2, post_b])
                        deferred1.append([2, post_a])
                return f

            for seq in range(2):
                flush_pend()
                if seq == 0:
                    MEMSET(kaT.t[:], 0.0, [kaT.b])
                    MEMSET(va.t[:], 0.0, [va.b], eng="pool")
                    LOAD(kaT.t[:, :, 512:4608], kaT_d[0, :, 512:4608].rearrange("(c p) t -> p c t", p=128), "ld_ka", [db("kaT")], [kaT.b])
                    LOAD(va.t[:, 4:36, :], va_d[0, 512:4608, :].rearrange("(t p) c -> p t c", p=128), "ld_va", [db("va")], [va.b])
                else:
                    LOAD(kaT.t[:], kaT_d[1].rearrange("(c p) t -> p c t", p=128), "ld_ka", [db("kaT")], [kaT.b])
                    LOAD(va.t[:], va_d[1].rearrange("(t p) c -> p t c", p=128), "ld_va", [db("va")], [va.b])
                LOAD(qaT.t[:], qaT_d[:, seq * SEQ_OWN:(seq + 1) * SEQ_OWN].rearrange("(c p) t -> p c t", p=128), "ld_qa", [db("qaT")], [qaT.b])
                for qt in (list(range(32)) if KQT >= 32 else [0, 1, 5, 30, 31][:KQT]):
                    edge = qt < 2 or qt >= 30
                    if edge:
                        ei = seq * 4 + (qt if qt < 2 else qt - 28)
                        LOAD(bme.t[:], bm_edge[ei], "bme", writes=[bme.b])
                        nkt, kt0, bm = 7, qt + 1, bme
                    else:
                        nkt, kt0, bm = 5, qt + 2, bmi
                    n = nkt * 128
                    for h in range(8):
                        c, base = h // 2, (h % 2) * 64
                        ps = psS[it % 3]
                        ssb = s_sb[it % 3]
                        pt = pT[it % 4]
                        it += 1
                        for k in range(nkt):
                            MM(ps.t[:, k * 128:(k + 1) * 128], kaT.t[base:base + 64, c, (kt0 + k) * 128:(kt0 + k + 1) * 128],
                               qaT.t[base:base + 64, c, qt * 128:(qt + 1) * 128], True, True, [kaT.b, qaT.b], [ps.b])
                        STT(ssb.t[:, 0:n], ps.t[:, 0:n], 0.125, bm.t[:, h * n:(h + 1) * n], ALU.mult, ALU.add, [ps.b, bm.b], [ssb.b])
                        ACT(pt.t[:, 0:n], ssb.t[:, 0:n], AF.Exp, [ssb.b], [pt.b])
                        pend.append(mk_pv1(pt, nkt, kt0, h, qt, seq))
                        flush_pend(keep=2)
                        tick1()
            flush_pend()
            while deferred1:
                deferred1.pop(0)[1]()
            S.barrier()
        if STOP <= 1:
            S.emit()
            return nc

        def dense_phase(layer, dk, scale, nkv, qper, k_loader, v_loader, q_loader, head_base, KG=3, rowpack=False):
            with contextlib.ExitStack() as ph:
                kT = [sbuf(ph, f"kT{i}", [128, 16384], BF16) for i in range(2)]
                vv = [sbuf(ph, f"vv{i}", [128, 128, 65], BF16) for i in range(2)]
                qT = [sbuf(ph, f"qT{i}", [128, SEQ_OWN], BF16) for i in range(2)]
                pT = [sbuf(ph, f"pT{i}", [128, KG * 512], BF16) for i in range(3)]
                ost = [sbuf(ph, f"ost{i}", [65, 512], F32) for i in range(2)]
                gtD = [sbuf(ph, f"gtD{i}", [64, 512], F32) for i in range(2)]
                mgD = [sbuf(ph, f"mgD{i}", [64, 512], BF16) for i in range(2)]
                psS = [psum(ph, f"psS{i}", [128, KG * 512], F32) for i in range(2)]
                psO = [psum(ph, f"psO{i}", [128, 512], F32) for i in range(1)]
                psB = psum(ph, "psB", [128, 512], F32)
                it = 0
                qbi = 0
                pending = None

                def mk_pv(po, ob, gt, mgo, vb_, pt, k0, sz, nkt, seq, h, qb):
                    def f():
                        for k in range(sz):
                            kt = k0 + k
                            MM(po.t[0:65, :], vb_.t[:, kt, :], pt.t[:, k * 512:(k + 1) * 512],
                               kt == 0, kt == nkt - 1, [vb_.b, pt.b], [po.b])
                        if k0 + sz == nkt:
                            t0 = seq * SEQ_OWN + qb * 512
                            f0 = (head_base + h) * 64
                            COPY("dve", ob.t[:], po.t[0:65, :], [po.b], [ob.b])
                            DVE(lambda e: e.reciprocal(out=ob.t[64:65, :], in_=ob.t[64:65, :]), [ob.b], [ob.b])
                            TT(ob.t[0:64, :], ob.t[0:64, :], gt.t[:], ALU.mult, [ob.b, gt.b], [ob.b])

                            def post():
                                MM(psB.t[0:64, :], ones_f.t[64:65, 0:64], ob.t[64:65, :], True, True, [ones_f.b, ob.b], [psB.b])
                                TT(mgo.t[:], ob.t[0:64, :], psB.t[0:64, :], ALU.mult, [ob.b, psB.b], [mgo.b])
                                STORE(mgT_d[layer][f0:f0 + 64, t0:t0 + 512], mgo.t[:], "st_oD", [mgo.b], [db("mgT%d" % layer)])
                            deferred.append([6, post])
                    return f

                deferred = []

                def tick():
                    for d in list(deferred):
                        d[0] -= 1
                        if d[0] <= 0:
                            deferred.remove(d)
                            d[1]()

                jobs = []
                kvi = 0
                for seq in range(2):
                    for g in range(nkv):
                        for hq in range(qper):
                            jobs.append((seq, g, hq, kvi))
                        kvi += 1

                def job_loads(j):
                    seq, g, hq, kvi_ = jobs[j]
                    if hq == 0:
                        k_loader(seq, g, kT[kvi_ % 2])
                        v_loader(seq, g, vv[kvi_ % 2])
                    q_loader(seq, g * qper + hq, qT[j % 2])

                job_loads(0)
                for j, (seq, g, hq, kvi_) in enumerate(jobs):
                    nk = SEQ_OWN if seq == 0 else 4 * SEQ_OWN
                    nkt = nk // 128
                    kb_, vb_, qb_ = kT[kvi_ % 2], vv[kvi_ % 2], qT[j % 2]
                    h = g * qper + hq
                    for qb in range(8):
                        if qb == 2 and j + 1 < len(jobs):
                            job_loads(j + 1)
                        po = psO[0]
                        ob, gt, mgo = ost[qbi % 2], gtD[qbi % 2], mgD[qbi % 2]
                        qbi += 1
                        t0 = seq * SEQ_OWN + qb * 512
                        f0 = (head_base + h) * 64
                        LOAD(gt.t[:], gT_d[layer][f0:f0 + 64, t0:t0 + 512], "ld_gt" + gt.b.name, [db("gT%d" % layer)], [gt.b])
                        for k0 in range(0, nkt, KG):
                            sz = min(KG, nkt - k0)
                            ps = psS[it % 2]
                            pt = pT[it % 3]
                            it += 1
                            for k in range(sz):
                                kt = k0 + k
                                r0 = (kt % 2) * 64 if rowpack else 0
                                MM(ps.t[:, k * 512:(k + 1) * 512], kb_.t[r0:r0 + dk, kt * 128:(kt + 1) * 128],
                                   qb_.t[r0:r0 + dk, qb * 512:(qb + 1) * 512], True, True, [kb_.b, qb_.b], [ps.b])
                            ACT(pt.t[:, 0:sz * 512], ps.t[:, 0:sz * 512], AF.Exp, [ps.b], [pt.b], scale=scale)
                            if pending is not None:
                                pending()
                            pending = mk_pv(po, ob, gt, mgo, vb_, pt, k0, sz, nkt, seq, h, qb)
                            tick()
                if pending is not None:
                    pending()
                for d in list(deferred):
                    d[1]()
                S.barrier()

        def k0_loader(seq, g, kb_):
            for r0 in (0, 64):
                if seq == 0:
                    LOAD(kb_.t[r0:r0 + 64, 0:SEQ_OWN], kbT_p[g * 64:(g + 1) * 64, :], "ld_k" + kb_.b.name, [db("kbT0")], [kb_.b])
                else:
                    LOAD(kb_.t[r0:r0 + 64, :].rearrange("p (r t) -> p r t", r=4),
                         kbT_g.rearrange("(r p) t -> p r t", p=128)[g * 64:(g + 1) * 64], "ld_k" + kb_.b.name, [db("ccg0")], [kb_.b])

        def v0_loader(seq, g, vb_):
            if seq == 0:
                LOAD(vb_.t[:, 0:32, :], vb_p[:, g * 65:(g + 1) * 65].rearrange("(t p) c -> p t c", p=128), "ld_v" + vb_.b.name, [db("vb0")], [vb_.b])
            else:
                LOAD(vb_.t[:], vview(vb_g[g]).rearrange("(t p) c -> p t c", p=128), "ld_v" + vb_.b.name, [db("ccg0")], [vb_.b])

        def q0_loader(seq, h, qb_):
            for r0 in (0, 64):
                LOAD(qb_.t[r0:r0 + 64, :], qbT_d[h * 64:(h + 1) * 64, seq * SEQ_OWN:(seq + 1) * SEQ_OWN], "ld_q" + qb_.b.name, [db("qbT")], [qb_.b])

        dense_phase(0, 64, 0.125, 2, 4, k0_loader, v0_loader, q0_loader, 8, rowpack=True)
        if STOP <= 2:
            S.emit()
            return nc

        def outproj_phase(layer, w_src, x_src, x_buf, final):
            with contextlib.ExitStack() as ph:
                wo = sbuf(ph, "wo", [128, 8, 1024], BF16)
                with contextlib.ExitStack() as wst:
                    stage = [sbuf(wst, f"stg{i}", [128, 1024], F32) for i in range(2)]
                    load_weight(wst, wo, w_src, 8, 1024, stage)
                    S.barrier()
                gF = bcast_load(ph, "gF", norm_f, 1024)
                mgs = [sbuf(ph, f"mg{i}", [128, 8, 512], BF16) for i in range(2)]
                xg = [sbuf(ph, f"xg{i}", [128, 4, 1024], F32) for i in range(2)]
                yo = [sbuf(ph, f"yo{i}", [128, 4, 1024], F32) for i in range(2)]
                junk = sbuf(ph, "junk", [128, 1024], BF16)
                ss = sbuf(ph, "ss", [128, 4], F32)
                rstd = sbuf(ph, "rstd", [128, 4], F32)
                psY = [psum(ph, f"psY{i}", [128, 512], F32) for i in range(6)]

                def loads(g):
                    t0 = g * G
                    LOAD(xg[g % 2].t[:], x_src[t0:t0 + G, :].rearrange("(s p) d -> p s d", p=128), "ld_x" + xg[g % 2].b.name, x_buf, [xg[g % 2].b], q="act")
                    LOAD(mgs[g % 2].t[:], mgT_d[layer][:, t0:t0 + G].rearrange("(c p) t -> p c t", p=128), "ld_mg" + mgs[g % 2].b.name,
                         [db("mgT%d" % layer)], [mgs[g % 2].b], q="act")

                loads(0)
                for g in range(16):
                    t0 = g * G
                    if g + 1 < 16:
                        loads(g + 1)
                    xb, mg = xg[g % 2], mgs[g % 2]
                    for s_ in range(4):
                        ps2 = [psY[(s_ * 2 + n) % 6] for n in range(2)]
                        for c in range(8):
                            for n in range(2):
                                MM(ps2[n].t[:], mg.t[:, c, s_ * 128:(s_ + 1) * 128], wo.t[:, c, n * 512:(n + 1) * 512], c == 0, c == 7, [mg.b, wo.b], [ps2[n].b])
                        for n in range(2):
                            TT(xb.t[:, s_, n * 512:(n + 1) * 512], ps2[n].t[:], xb.t[:, s_, n * 512:(n + 1) * 512], ALU.add, [ps2[n].b, xb.b], [xb.b])
                    if not final:
                        STORE(x1_d[t0:t0 + G, :].rearrange("(s p) d -> p s d", p=128), xb.t[:], "st_x1", [xb.b], [db("x1")])
                        continue
                    yb = yo[g % 2]
                    MEMSET(ss.t[:], 0.0, [ss.b])
                    for s_ in range(4):
                        ACT(junk.t[:], xb.t[:, s_, :], AF.Square, [xb.b, ss.b], [junk.b, ss.b], accum=ss.t[:, s_:s_ + 1])
                    TS(rstd.t[:], ss.t[:], 1.0 / 1024, EPS, ALU.mult, ALU.add, [ss.b], [rstd.b])
                    TS(rstd.t[:], rstd.t[:], -0.5, None, ALU.pow, None, [rstd.b], [rstd.b])
                    for s_ in range(4):
                        STT(yb.t[:, s_, :], xb.t[:, s_, :], rstd.t[:, s_:s_ + 1], gF.t[:], ALU.mult, ALU.mult, [xb.b, rstd.b, gF.b], [yb.b])
                    STORE(y_out[t0:t0 + G, :].rearrange("(s p) d -> p s d", p=128), yb.t[:], "st_y", [yb.b], [db("y")])
                S.barrier()

        outproj_phase(0, w_out_e, x_own, [], False)
        if STOP <= 3:
            S.emit()
            return nc

        with contextlib.ExitStack() as ph:
            w1 = sbuf(ph, "w1", [128, 8, 1728], BF16)
            wq = sbuf(ph, "wq", [128, 3, 3072], BF16)
            wkv = sbuf(ph, "wkv", [128, 2, 2048], BF16)
            with contextlib.ExitStack() as wst:
                stage = [sbuf(wst, f"stg{i}", [128, 3072], F32) for i in range(2)]
                load_weight(wst, w1, w1x, 8, 1728, stage)
                load_weight(wst, wq, wuq2, 3, 3072, stage)
                load_weight(wst, wkv, wukv2, 2, 2048, stage)
                S.barrier()
            gO = bcast_load(ph, "gO", norm_o, 1024)
            gql = bcast_load(ph, "gql", qlat_g, 384)
            gkl = bcast_load(ph, "gkl", kvlat_g, 256)
            x1s = [sbuf(ph, f"x1s{i}", [128, 4, 1024], F32) for i in range(2)]
            hb = sbuf(ph, "hb", [128, 4, 1024], BF16)
            hT = sbuf(ph, "hT", [128, 8, 512], BF16)
            junk = sbuf(ph, "junk", [128, 1024], BF16)
            ss = sbuf(ph, "ss", [128, 4], F32)
            rstd = sbuf(ph, "rstd", [128, 4], F32)
            ss2 = sbuf(ph, "ss2", [128, 2], F32)
            g_st = sbuf(ph, "g_st", [128, 8, 512], F32)
            lats = [sbuf(ph, f"lat{i}", [128, 640], BF16) for i in range(2)]
            latT = sbuf(ph, "latT", [128, 5, 512], BF16)
            c1ts = [sbuf(ph, f"c1t{i}", [96, 512], F32) for i in range(2)]
            s1ts = [sbuf(ph, f"s1t{i}", [96, 512], F32) for i in range(2)]
            ckts = [sbuf(ph, f"ckt{i}", [32, 512], F32) for i in range(2)]
            skts = [sbuf(ph, f"skt{i}", [32, 512], F32) for i in range(2)]
            ta = sbuf(ph, "ta", [96, 512], F32)
            tb = sbuf(ph, "tb", [96, 512], F32)
            q_st = [sbuf(ph, f"q_st{i}", [96, 512], BF16) for i in range(4)]
            kr_st = sbuf(ph, "kr_st", [32, 512], BF16)
            kn_sts = [sbuf(ph, f"kn_st{i}", [128, 8, 512], BF16) for i in range(2)]
            v_sts = [sbuf(ph, f"v_st{i}", [128, 4, 16, 65], BF16) for i in range(2)]
            psT = [psum(ph, f"psT{i}", [128, 8, 128], BF16) for i in range(2)]
            psF = [psum(ph, f"psF{i}", [128, 512], F32) for i in range(2)]
            psL = [psum(ph, f"psL{i}", [128, 1024], F32) for i in range(2)]
            psY = psF
            for v_st in v_sts:
                MEMSET(v_st.t[:], 1.0, [v_st.b])

            def p3_loads(gi, g):
                t0 = g * G
                x1 = x1s[gi % 2]
                LOAD(x1.t[:], x1_d[t0:t0 + G, :].rearrange("(s p) d -> p s d", p=128), "ld_x" + x1.b.name, [db("x1")], [x1.b], q="pool")
                LOAD(c1ts[gi % 2].t[:], c1[:, t0:t0 + G], "ld_c1%d" % (gi % 2), writes=[c1ts[gi % 2].b], q="pool")
                LOAD(s1ts[gi % 2].t[:], s1[:, t0:t0 + G], "ld_s1%d" % (gi % 2), writes=[s1ts[gi % 2].b], q="pool")
                LOAD(ckts[gi % 2].t[:], ck[:, t0:t0 + G], "ld_ck%d" % (gi % 2), writes=[ckts[gi % 2].b], q="pool")
                LOAD(skts[gi % 2].t[:], sk[:, t0:t0 + G], "ld_sk%d" % (gi % 2), writes=[skts[gi % 2].b], q="pool")

            def p3_group(gi, g, mid_hook):
                t0 = g * G
                seq = g // 8
                tl = (g % 8) * G
                x1 = x1s[gi % 2]
                c1t, s1t, ckt, skt = c1ts[gi % 2], s1ts[gi % 2], ckts[gi % 2], skts[gi % 2]
                for j in range(8):
                    p = psF[j % 2]
                    c0 = 672 + j * 128
                    for kc in range(8):
                        MM(p.t[:], w1.t[:, kc, c0:c0 + 128], hT.t[:, kc, :], kc == 0, kc == 7, [w1.b, hT.b], [p.b])
                    ACT(g_st.t[:, j, :], p.t[:], AF.Silu, [p.b], [g_st.b])
                STORE(gT_d[1][:, t0:t0 + G].rearrange("(c p) t -> p c t", p=128), g_st.t[:], "st_g1", [g_st.b], [db("gT1")])
                pa, pb = psF[0], psF[1]
                for kc in range(8):
                    MM(pa.t[0:32, :], w1.t[:, kc, 640:672], hT.t[:, kc, :], kc == 0, kc == 7, [w1.b, hT.b], [pa.b])
                for kc in range(8):
                    MM(pb.t[0:32, :], w1.t[:, kc, 1696:1728], hT.t[:, kc, :], kc == 0, kc == 7, [w1.b, hT.b], [pb.b])
                TT(ta.t[0:32, :], pa.t[0:32, :], ckt.t[:], ALU.mult, [pa.b, ckt.b], [ta.b])
                TT(tb.t[0:32, :], pb.t[0:32, :], skt.t[:], ALU.mult, [pb.b, skt.b], [tb.b])
                TT(kr_st.t[:], ta.t[0:32, :], tb.t[0:32, :], ALU.add, [ta.b, tb.b], [kr_st.b])
                STORE((k1r_p if seq == 0 else k1r_c)[:, tl:tl + G], kr_st.t[:], "st_kr", [kr_st.b], [db("k1r%d" % seq)])
                def lat_mm(s):
                    p = psL[s % 2]
                    for kc in range(8):
                        MM(p.t[:, 0:384], hT.t[:, kc, s * 128:(s + 1) * 128], w1.t[:, kc, 0:384], kc == 0, kc == 7, [w1.b, hT.b], [p.b])
                    for kc in range(8):
                        MM(p.t[:, 512:768], hT.t[:, kc, s * 128:(s + 1) * 128], w1.t[:, kc, 384:640], kc == 0, kc == 7, [w1.b, hT.b], [p.b])

                def lat_chain(s):
                    p = psL[s % 2]
                    lat = lats[s % 2]
                    MEMSET(ss2.t[:], 0.0, [ss2.b])
                    ACT(junk.t[:, 0:384], p.t[:, 0:384], AF.Square, [p.b, ss2.b], [junk.b, ss2.b], accum=ss2.t[:, 0:1])
                    ACT(junk.t[:, 384:640], p.t[:, 512:768], AF.Square, [p.b, ss2.b], [junk.b, ss2.b], accum=ss2.t[:, 1:2])
                    TS(ss2.t[:, 0:1], ss2.t[:, 0:1], 1.0 / 384, EPS, ALU.mult, ALU.add, [ss2.b], [ss2.b])
                    TS(ss2.t[:, 1:2], ss2.t[:, 1:2], 1.0 / 256, EPS, ALU.mult, ALU.add, [ss2.b], [ss2.b])
                    TS(ss2.t[:], ss2.t[:], -0.5, None, ALU.pow, None, [ss2.b], [ss2.b])
                    STT(lat.t[:, 0:384], p.t[:, 0:384], ss2.t[:, 0:1], gql.t[:], ALU.mult, ALU.mult, [p.b, ss2.b, gql.b], [lat.b])
                    STT(lat.t[:, 384:640], p.t[:, 512:768], ss2.t[:, 1:2], gkl.t[:], ALU.mult, ALU.mult, [p.b, ss2.b, gkl.b], [lat.b])

                def lat_tr(s):
                    lat = lats[s % 2]
                    pt = psT[s % 2]
                    for c in range(5):
                        TR(pt.t[:, c, :], lat.t[:, c * 128:(c + 1) * 128], ident_b.t[:], [lat.b, ident_b.b], [pt.b])
                    COPY("act", latT.t[:, :, s * 128:(s + 1) * 128], pt.t[:, 0:5, :], [pt.b], [latT.b])

                lat_mm(0)
                lat_chain(0)
                lat_mm(1)
                lat_chain(1)
                lat_mm(2)
                lat_tr(0)
                lat_chain(2)
                lat_mm(3)
                lat_tr(1)
                lat_chain(3)
                lat_tr(2)
                lat_tr(3)
                for h in range(16):
                    if h % 2 == 0:
                        pa_t, pb_t, pa_b, pb_b = psF[0].t[0:96, :], psF[1].t[0:96, :], psF[0].b, psF[1].b
                    else:
                        pa_t, pb_t, pa_b, pb_b = psL[0].t[0:96, 0:512], psL[1].t[0:96, 0:512], psL[0].b, psL[1].b
                    qs = q_st[h % 4]
                    for kc in range(3):
                        MM(pa_t, wq.t[:, kc, h * 96:(h + 1) * 96], latT.t[:, kc, :], kc == 0, kc == 2, [wq.b, latT.b], [pa_b])
                    for kc in range(3):
                        MM(pb_t, wq.t[:, kc, 1536 + h * 96:1536 + (h + 1) * 96], latT.t[:, kc, :], kc == 0, kc == 2, [wq.b, latT.b], [pb_b])
                    TT(ta.t[:], pa_t, c1t.t[:], ALU.mult, [pa_b, c1t.b], [ta.b])
                    TT(tb.t[:], pb_t, s1t.t[:], ALU.mult, [pb_b, s1t.b], [tb.b])
                    TT(qs.t[:], ta.t[:], tb.t[:], ALU.add, [ta.b, tb.b], [qs.b])
                    STORE(q1T_d[h, :, t0:t0 + G], qs.t[:], "st_q1", [qs.b], [db("q1T")])
                mid_hook()
                kn_st = kn_sts[gi % 2]
                v_st = v_sts[gi % 2]
                for c in range(8):
                    p = psY[c % 2]
                    for kc in range(2):
                        MM(p.t[:], wkv.t[:, kc, c * 128:(c + 1) * 128], latT.t[:, 3 + kc, :], kc == 0, kc == 1, [wkv.b, latT.b], [p.b])
                    COPY("act", kn_st.t[:, c, :], p.t[:], [p.b], [kn_st.b])
                if seq == 0:
                    STORE(k1n_p[:, tl:tl + G].rearrange("(c p) t -> p c t", p=128), kn_st.t[:], "st_kn", [kn_st.b], [db("k1n0")])
                else:
                    for c in range(8):
                        STORE(k1n_c[c][:, tl:tl + G], kn_st.t[:, c, :], "st_kn", [kn_st.b], [db("k1n1")])
                for s in range(4):
                    for n in range(2):
                        p = psY[(s * 2 + n) % 2]
                        for kc in range(2):
                            MM(p.t[:], latT.t[:, 3 + kc, s * 128:(s + 1) * 128], wkv.t[:, kc, 1024 + n * 512:1024 + (n + 1) * 512],
                               kc == 0, kc == 1, [wkv.b, latT.b], [p.b])
                        COPY("act", v_st.t[:, s, n * 8:(n + 1) * 8, 0:64], p.t[:].rearrange("p (h d) -> p h d", d=64), [p.b], [v_st.b])
                if seq == 0:
                    STORE(v1_p[tl:tl + G, :].rearrange("(s p) c -> p s c", p=128),
                          v_st.t[:].rearrange("p s h c -> p s (h c)"), "st_v1", [v_st.b], [db("v10")])
                else:
                    for h in range(16):
                        STORE(vview(v1_c[h])[tl:tl + G, :].rearrange("(s p) c -> p s c", p=128), v_st.t[:, :, h, :],
                              "st_v1", [v_st.b], [db("v11")])

            order = list(range(8, 16)) + list(range(0, 8))
            p3_loads(0, order[0])
            norm_part(x1s[0], gO, hb, ss, rstd, junk, 4)
            transpose_part(hb, hT, psT, 4)
            for gi, g in enumerate(order):
                nxt = gi + 1 < 16
                if nxt:
                    p3_loads(gi + 1, order[gi + 1])
                p3_group(gi, g, (lambda gi=gi: norm_part(x1s[(gi + 1) % 2], gO, hb, ss, rstd, junk, 4)) if nxt else (lambda: None))
                if nxt:
                    transpose_part(hb, hT, psT, 4)
            cc_list = [("k1r", k1r_c, k1r_g)] + [("k1n", k1n_c[i], k1n_g[i]) for i in range(8)] + [("v1", v1_c[h], v1_g[h]) for h in range(16)]
            for ci, (nm, src, dst) in enumerate(cc_list):
                S.op("pool", (lambda s_, d_: (lambda e: e.collective_compute("AllGather", ALU.bypass, replica_groups=GROUPS4, ins=[s_], outs=[d_])))(src, dst),
                     reads=[db(nm + "1")], writes=[db("ccg1")], dma=True, stream="cc1", inc=1)
            S.barrier()
        if STOP <= 4:
            S.emit()
            return nc

        def k1_loader(seq, h, kb_):
            if seq == 0:
                LOAD(kb_.t[0:64, 0:SEQ_OWN], k1n_p[h * 64:(h + 1) * 64, :], "ld_k" + kb_.b.name, [db("k1n0")], [kb_.b])
                LOAD(kb_.t[64:96, 0:SEQ_OWN], k1r_p[:, :], "ld_k" + kb_.b.name, [db("k1r0")], [kb_.b])
            else:
                LOAD(kb_.t[0:64, :].rearrange("p (r t) -> p r t", r=4),
                     k1n_g[h // 2].rearrange("(r f) t -> f r t", f=128)[(h % 2) * 64:(h % 2 + 1) * 64], "ld_k" + kb_.b.name, [db("ccg1")], [kb_.b])
                LOAD(kb_.t[64:96, :].rearrange("p (r t) -> p r t", r=4),
                     k1r_g.rearrange("(r f) t -> f r t", f=32), "ld_k" + kb_.b.name, [db("ccg1")], [kb_.b])

        def v1_loader(seq, h, vb_):
            if seq == 0:
                LOAD(vb_.t[:, 0:32, :], v1_p[:, h * 65:(h + 1) * 65].rearrange("(t p) c -> p t c", p=128), "ld_v" + vb_.b.name, [db("v10")], [vb_.b])
            else:
                LOAD(vb_.t[:], vview(v1_g[h]).rearrange("(t p) c -> p t c", p=128), "ld_v" + vb_.b.name, [db("ccg1")], [vb_.b])

        def q1_loader(seq, h, qb_):
            LOAD(qb_.t[0:96, :], q1T_d[h, :, seq * SEQ_OWN:(seq + 1) * SEQ_OWN], "ld_q" + qb_.b.name, [db("q1T")], [qb_.b])

        dense_phase(1, 96, 96 ** -0.5, 16, 1, k1_loader, v1_loader, q1_loader, 0)
        if STOP <= 5:
            S.emit()
            return nc

        outproj_phase(1, w_out_o, x1_d, [db("x1")], True)
        S.emit()
    return nc


def _na_table(r0, R, base_off, nkt, rpb):
    q = np.arange(128)
    qr = r0 + q // 64
    qc = q % 64
    k = np.arange(128)
    kr_in = k // 64
    kc = k % 64
    rs = np.clip(qr - 4, 0, R - 8)
    cs = np.clip(qc - 8, 0, 64 - 16)
    out = np.full((128, 8, nkt, 128), NEG, np.float32)
    for kt in range(nkt):
        kr = r0 + base_off + 2 * kt + kr_in
        valid = ((kr[:, None] >= rs[None, :]) & (kr[:, None] < rs[None, :] + 8) & (kr[:, None] >= 0) & (kr[:, None] < R)
                 & (kc[:, None] >= cs[None, :]) & (kc[:, None] < cs[None, :] + 16))
        ro = np.clip(kr[:, None] - qr[None, :] + 7, 0, 14)
        co = np.clip(kc[:, None] - qc[None, :] + 15, 0, 30)
        vals = rpb[:, ro, co]
        vals = np.where(valid[None], vals, np.float32(NEG))
        out[:, :, kt, :] = vals.transpose(1, 0, 2)
    return out


def _rope_tables(pos, rot_dim):
    nf = rot_dim // 4
    inv = (np.float32(10000.0) ** (-np.arange(nf, dtype=np.float32) / np.float32(nf))).astype(np.float32)
    row = (pos // 64).astype(np.float32)
    col = (pos % 64).astype(np.float32)
    ang = np.concatenate([row[:, None] * inv[None], col[:, None] * inv[None]], axis=-1).astype(np.float32)
    return np.cos(ang).astype(np.float32), np.sin(ang).astype(np.float32)


_PROG = None


def kernel(x_prompt, x_sample, norm_e, w_in_e, rpb_a, qnorm_b, knorm_b, w_out_e,
           norm_o, w_in_o, qlat_g, kvlat_g, w_uq, w_ukv, w_out_o, norm_f):
    global _PROG
    f = lambda a: np.ascontiguousarray(np.asarray(a, dtype=np.float32))
    x_prompt, x_sample = f(x_prompt), f(x_sample)
    w_in_e0, w_out_e0, w_in_o0, w_uq0, w_ukv0, w_out_o0 = f(w_in_e)[0], f(w_out_e)[0], f(w_in_o)[0], f(w_uq)[0], f(w_ukv)[0], f(w_out_o)[0]
    rpb = f(rpb_a)[0]
    kr = w_in_o0[:, 640:672]
    w1x = np.concatenate([w_in_o0, kr[:, 16:32], kr[:, 0:16]], axis=1)
    uq = w_uq0.reshape(384, 16, 96)
    uq_sw = np.concatenate([uq[:, :, 0:64], uq[:, :, 80:96], uq[:, :, 64:80]], axis=2)
    wuq2 = np.concatenate([uq.reshape(384, 1536), uq_sw.reshape(384, 1536)], axis=1)
    ukv = w_ukv0.reshape(256, 16, 128)
    wukv2 = np.concatenate([ukv[:, :, 0:64].reshape(256, 1024), ukv[:, :, 64:128].reshape(256, 1024)], axis=1)
    bm_int = _na_table(8, 64, -4, 5, rpb).reshape(128, -1)
    ident = np.eye(128, dtype=np.float32)
    selA_h = np.tile(np.eye(8, dtype=np.float32).reshape(1, 64), (128, 1))
    selB_h = np.zeros((128, 8, 64), np.float32)
    for hh in range(8):
        selB_h[64 + hh, hh, :] = 1.0
    selB_h = selB_h.reshape(128, 512)
    common = dict(w_in_e=w_in_e0, w_out_e=w_out_e0, w1x=f(w1x), wuq2=f(wuq2), wukv2=f(wukv2), w_out_o=w_out_o0,
                  norm_e=f(norm_e).reshape(1, 1024), norm_o=f(norm_o).reshape(1, 1024), norm_f=f(norm_f).reshape(1, 1024),
                  qnorm_b=f(qnorm_b).reshape(1, 64), knorm_b=f(knorm_b).reshape(1, 64),
                  qlat_g=f(qlat_g).reshape(1, 384), kvlat_g=f(kvlat_g).reshape(1, 256),
                  bm_int=f(bm_int), ident=ident, selA=selA_h, selB=selB_h)
    edge_p = [_na_table(r0, 64, -6, 7, rpb).reshape(128, -1) for r0 in (0, 2, 60, 62)]
    in_maps = []
    for c in range(8):
        sq, j = c // 4, c % 4
        xs = x_sample[sq]
        x_own = np.concatenate([x_prompt[c], xs[j * 4096:(j + 1) * 4096]], axis=0)
        halo = np.zeros((1024, 1024), np.float32)
        if j > 0:
            halo[0:512] = xs[j * 4096 - 512:j * 4096]
        if j < 3:
            halo[512:1024] = xs[(j + 1) * 4096:(j + 1) * 4096 + 512]
        pos = np.concatenate([np.arange(4096), j * 4096 + np.arange(4096)])
        cos0, sin0 = _rope_tables(pos, 64)
        cs0 = np.concatenate([cos0, cos0, sin0, sin0], axis=1)
        cos1, sin1 = _rope_tables(pos, 32)
        c1 = np.ones((96, 8192), np.float32)
        s1 = np.zeros((96, 8192), np.float32)
        c1[64:80] = cos1.T
        c1[80:96] = cos1.T
        s1[64:80] = -sin1.T
        s1[80:96] = sin1.T
        edge_s = [_na_table(64 * j + r0, 256, -6, 7, rpb).reshape(128, -1) for r0 in (0, 2, 60, 62)]
        m = dict(common)
        m.update(x_own=f(x_own), x_halo=halo, cs0=f(cs0), c1=c1, s1=s1, ck=f(c1[64:96]), sk=f(s1[64:96]),
                 bm_edge=f(np.stack(edge_p + edge_s, axis=0)))
        in_maps.append(m)
    if _PROG is None:
        _PROG = build_program()
    res = run_bass_kernel_spmd(_PROG, in_maps, core_ids=list(range(8)))
    if DEBUG:
        kernel.last = res.results
    y_prompt = np.stack([np.asarray(res.results[c]["y_out"])[0:4096] for c in range(8)], axis=0)
    y_sample = np.stack([np.concatenate([np.asarray(res.results[sq * 4 + j]["y_out"])[4096:8192] for j in range(4)], axis=0)
                         for sq in range(2)], axis=0)
    return (y_prompt.astype(np.float32), y_sample.astype(np.float32))
```

```python
import contextlib
import os
import numpy as np
import concourse.bass as bass
import concourse.mybir as mybir
from concourse.bass_utils import run_bass_kernel_spmd

F32 = mybir.dt.float32
BF16 = mybir.dt.bfloat16
AF = mybir.ActivationFunctionType
ALU = mybir.AluOpType
AX = mybir.AxisListType

ENGS = ("pe", "act", "dve", "pool", "sp")
MAXV = 8000
EPS = 1e-6
NEG = -30000.0
DEBUG = bool(int(os.environ.get("KDEBUG", "0")))
STOP = int(os.environ.get("KSTOP", "99"))
NOCC = os.environ.get("KNOCC", "")
KSUB = int(os.environ.get("KSUB", "99"))
KNG = int(os.environ.get("KNG", "99"))
KP1 = int(os.environ.get("KP1", "99"))
KQT = int(os.environ.get("KQT", "32"))


class Buf:
    __slots__ = ("name", "writers", "readers", "excl")

    def __init__(self, name="", excl=False):
        self.name = name
        self.writers = []
        self.readers = []
        self.excl = excl


class Op:
    __slots__ = ("eng", "fn", "deps", "dma", "stream", "idx", "signal", "sem", "val", "inc")


class Sched:
    def __init__(self, nc):
        self.nc = nc
        self.ops = {e: [] for e in ENGS}
        self.streams = {}
        self.bar = []

    def op(self, eng, fn, reads=(), writes=(), dma=False, stream=None, inc=None):
        o = Op()
        o.eng, o.fn, o.dma, o.stream = eng, fn, dma, stream
        o.signal = dma
        o.sem, o.val = None, 0
        o.inc = inc if inc is not None else (16 if dma else 1)
        deps = list(self.bar)
        if any(b.excl for b in reads):
            writes = list(writes) + [b for b in reads if b.excl and b not in writes]
            reads = [b for b in reads if not b.excl]
        for b in reads:
            deps.extend(b.writers)
        for b in writes:
            deps.extend(b.writers)
            deps.extend(b.readers)
        best = {}
        for d in deps:
            if (not d.dma) and d.eng == "pe" and eng == "pe" and not dma:
                continue
            key = ("s", d.stream) if d.dma else ("e", d.eng)
            if key not in best or best[key].idx < d.idx:
                best[key] = d
        o.deps = list(best.values())
        for d in o.deps:
            d.signal = True
        if dma:
            lst = self.streams.setdefault(stream, [])
            o.idx = len(lst)
            lst.append(o)
        else:
            o.idx = len(self.ops[eng])
        self.ops[eng].append(o)
        for b in reads:
            b.readers.append(o)
        for b in writes:
            b.writers = [o]
            b.readers = []
        return o

    def barrier(self):
        bar = []
        for e in ENGS:
            comp = [o for o in self.ops[e] if not o.dma]
            if comp:
                bar.append(comp[-1])
        for sname, lst in self.streams.items():
            if lst and not str(sname).startswith("cc"):
                bar.append(lst[-1])
        self.bar = bar

    def emit(self, final_eng="sp"):
        nc = self.nc
        semreq = []

        def newsem(name):
            semreq.append(name)
            return len(semreq) - 1

        for e in ENGS:
            cur, cnt = None, 0
            for o in self.ops[e]:
                if o.dma or not o.signal:
                    continue
                if cur is None or cnt >= MAXV:
                    cur, cnt = newsem(f"m{e}{len(semreq)}"), 0
                cnt += 1
                o.sem, o.val = cur, cnt
        for sname, lst in self.streams.items():
            cur, cnt = None, 0
            for o in lst:
                if cur is None or cnt >= MAXV * 16:
                    cur, cnt = newsem(f"d{len(semreq)}"), 0
                cnt += o.inc
                o.sem, o.val = cur, cnt
        print("semaphores requested:", len(semreq), flush=True)
        with contextlib.ExitStack() as st:
            sems = [st.enter_context(nc.semaphore(n)) for n in semreq]
            block = st.enter_context(nc.Block())
            handles = {"pe": "tensor", "act": "scalar", "dve": "vector", "pool": "gpsimd", "sp": "sync"}
            finals = [lst[-1] for lst in self.streams.values() if lst]

            def make(e):
                def body(eh):
                    waited = {}
                    for o in self.ops[e]:
                        for d in o.deps:
                            if waited.get(d.sem, 0) < d.val:
                                eh.wait_ge(sems[d.sem], d.val)
                                waited[d.sem] = d.val
                        ins = o.fn(eh)
                        if o.signal:
                            ins.then_inc(sems[o.sem], o.inc)
                    if e == final_eng:
                        for d in finals:
                            if waited.get(d.sem, 0) < d.val:
                                eh.wait_ge(sems[d.sem], d.val)
                                waited[d.sem] = d.val
                return body

            for e in ENGS:
                getattr(block, handles[e])(make(e))


NTOK = 8192
SEQ_OWN = 4096
NLOC = 5120
G = 512


def build_program():
    nc = bass.Bass("TRN2", target_bir_lowering=False)
    S = Sched(nc)
    scratch_kind = "ExternalOutput" if DEBUG else "Internal"

    def din(name, shape, dt=F32):
        return nc.dram_tensor(name, list(shape), dt, kind="ExternalInput").ap()

    def dscr(name, shape, dt, dbg=True):
        return nc.dram_tensor(name, list(shape), dt, kind=(scratch_kind if dbg else "Internal")).ap()

    x_own = din("x_own", [NTOK, 1024])
    x_halo = din("x_halo", [1024, 1024])
    w_in_e = din("w_in_e", [1024, 3328])
    w_out_e = din("w_out_e", [1024, 1024])
    w1x = din("w1x", [1024, 1728])
    wuq2 = din("wuq2", [384, 3072])
    wukv2 = din("wukv2", [256, 2048])
    w_out_o = din("w_out_o", [1024, 1024])
    norm_e = din("norm_e", [1, 1024])
    norm_o = din("norm_o", [1, 1024])
    norm_f = din("norm_f", [1, 1024])
    qnorm_b = din("qnorm_b", [1, 64])
    knorm_b = din("knorm_b", [1, 64])
    qlat_g = din("qlat_g", [1, 384])
    kvlat_g = din("kvlat_g", [1, 256])
    cs0 = din("cs0", [NTOK, 128])
    c1 = din("c1", [96, NTOK])
    s1 = din("s1", [96, NTOK])
    ck = din("ck", [32, NTOK])
    sk = din("sk", [32, NTOK])
    bm_int = din("bm_int", [128, 8 * 5 * 128])
    bm_edge = din("bm_edge", [8, 128, 8 * 7 * 128])
    ident_in = din("ident", [128, 128])
    selA_in = din("selA", [128, 64])
    selB_in = din("selB", [128, 512])
    y_out = nc.dram_tensor("y_out", [NTOK, 1024], F32, kind="ExternalOutput").ap()

    qaT_d = dscr("qaT_d", [512, NTOK], BF16)
    kaT_d = dscr("kaT_d", [2, 512, NLOC], BF16)
    va_d = dscr("va_d", [2, NLOC, 520], BF16)
    qbT_d = dscr("qbT_d", [512, NTOK], BF16)
    kbT_p = dscr("kbT_p", [128, SEQ_OWN], BF16)
    vb_p = dscr("vb_p", [SEQ_OWN, 130], BF16)
    kbT_c = dscr("kbT_c", [128, SEQ_OWN], BF16, dbg=False)
    vb_c = [dscr(f"vb_c{g}", [128, 2080], BF16, dbg=False) for g in range(2)]
    kbT_g = dscr("kbT_g", [512, SEQ_OWN], BF16, dbg=False)
    vb_g = [dscr(f"vb_g{g}", [512, 2080], BF16, dbg=False) for g in range(2)]
    gT_d = [dscr(f"gT{l}_d", [1024, NTOK], F32) for l in range(2)]
    mgT_d = [dscr(f"mgT{l}_d", [1024, NTOK], BF16) for l in range(2)]
    x1_d = dscr("x1_d", [NTOK, 1024], F32)
    q1T_d = dscr("q1T_d", [16, 96, NTOK], BF16)
    k1n_p = dscr("k1n_p", [1024, SEQ_OWN], BF16)
    k1r_p = dscr("k1r_p", [32, SEQ_OWN], BF16)
    v1_p = dscr("v1_p", [SEQ_OWN, 1040], BF16)
    k1n_c = [dscr(f"k1n_c{i}", [128, SEQ_OWN], BF16, dbg=False) for i in range(8)]
    k1r_c = dscr("k1r_c", [32, SEQ_OWN], BF16, dbg=False)
    v1_c = [dscr(f"v1_c{h}", [128, 2080], BF16, dbg=False) for h in range(16)]
    k1n_g = [dscr(f"k1n_g{i}", [512, SEQ_OWN], BF16, dbg=False) for i in range(8)]
    k1r_g = dscr("k1r_g", [128, SEQ_OWN], BF16, dbg=False)
    v1_g = [dscr(f"v1_g{h}", [512, 2080], BF16, dbg=False) for h in range(16)]

    def vview(ap):
        return ap.rearrange("p (a c) -> (p a) c", c=65)

    D = {}

    def db(name):
        if name not in D:
            D[name] = Buf(name)
        return D[name]

    GROUPS4 = [[0, 1, 2, 3], [4, 5, 6, 7]]

    def LOAD(out, in_, stream, reads=(), writes=(), q="sp"):
        return S.op(q, lambda e: e.dma_start(out=out, in_=in_), reads=reads, writes=writes, dma=True, stream=stream)

    def STORE(out, in_, stream, reads=(), writes=()):
        return S.op("sp", lambda e: e.dma_start(out=out, in_=in_), reads=reads, writes=writes, dma=True, stream=stream)

    def MM(out, lhsT, rhs, start, stop, reads, writes):
        return S.op("pe", lambda e: e.matmul(out, lhsT=lhsT, rhs=rhs, start=start, stop=stop), reads=reads, writes=writes)

    def TR(out, in_, ident, reads, writes):
        return S.op("pe", lambda e: e.transpose(out=out, in_=in_, identity=ident), reads=reads, writes=writes)

    def ACT(out, in_, func, reads, writes, scale=1.0, accum=None):
        if accum is None:
            return S.op("act", lambda e: e.activation(out=out, in_=in_, func=func, scale=scale), reads=reads, writes=writes)
        return S.op("act", lambda e: e.activation(out=out, in_=in_, func=func, scale=scale, accum_out=accum), reads=reads, writes=writes)

    def DVE(fn, reads, writes):
        return S.op("dve", fn, reads=reads, writes=writes)

    def COPY(eng, out, in_, reads, writes):
        if eng == "act":
            return S.op("act", lambda e: e.copy(out=out, in_=in_), reads=reads, writes=writes)
        return S.op(eng, lambda e: e.tensor_copy(out=out, in_=in_), reads=reads, writes=writes)

    def TT(out, in0, in1, op, reads, writes, eng="dve"):
        return S.op(eng, lambda e: e.tensor_tensor(out=out, in0=in0, in1=in1, op=op), reads=reads, writes=writes)

    def TS(out, in0, s1_, s2_, op0, op1, reads, writes):
        if op1 is None:
            assert op0 == ALU.pow and s1_ == -0.5
            S.op("act", lambda e: e.activation(out=out, in_=in0, func=AF.Sqrt), reads=reads, writes=writes)
            return S.op("dve", lambda e: e.reciprocal(out=out, in_=out), reads=writes, writes=writes)
        return S.op("dve", lambda e: e.tensor_scalar(out=out, in0=in0, scalar1=s1_, scalar2=s2_, op0=op0, op1=op1), reads=reads, writes=writes)

    def STT(out, in0, scalar, in1, op0, op1, reads, writes):
        return S.op("dve", lambda e: e.scalar_tensor_tensor(out=out, in0=in0, scalar=scalar, in1=in1, op0=op0, op1=op1), reads=reads, writes=writes)

    def MEMSET(ap, val, writes, eng="dve"):
        return S.op(eng, lambda e: e.memset(ap, val), writes=writes)

    class T:
        def __init__(self, t, name, excl=False):
            self.t = t
            self.b = Buf(name, excl)

    with contextlib.ExitStack() as top:
        uid = [0]

        def sbuf(st, name, shape, dt):
            uid[0] += 1
            name = f"{name}_{uid[0]}"
            return T(st.enter_context(nc.sbuf_tensor(name, list(shape), dt)), name)

        def psum(st, name, shape, dt):
            uid[0] += 1
            name = f"{name}_{uid[0]}"
            return T(st.enter_context(nc.psum_tensor(name, list(shape), dt)), name, excl=True)

        ident_f = sbuf(top, "ident_f", [128, 128], F32)
        ident_b = sbuf(top, "ident_b", [128, 128], BF16)
        LOAD(ident_f.t[:], ident_in, "c_ident", writes=[ident_f.b])
        COPY("dve", ident_b.t[:], ident_f.t[:], [ident_f.b], [ident_b.b])
        ones_f = sbuf(top, "ones_f", [128, 64], F32)
        MEMSET(ones_f.t[:], 1.0, [ones_f.b])
        selA = sbuf(top, "selA", [128, 8, 8], F32)
        selB = sbuf(top, "selB", [128, 8, 64], F32)
        LOAD(selA.t[:].rearrange("p h m -> p (h m)"), selA_in, "c_selA", writes=[selA.b])
        LOAD(selB.t[:].rearrange("p h m -> p (h m)"), selB_in, "c_selB", writes=[selB.b])

        def bcast_load(st, name, src, n):
            t = sbuf(st, name, [128, n], F32)
            LOAD(t.t[:], src.broadcast_to([128, n]), "c_" + name, writes=[t.b])
            return t

        def load_weight(st, dst, src, nk, ncols, stage):
            for k in range(nk):
                sg = stage[k % 2]
                LOAD(sg.t[:, 0:ncols], src[k * 128:(k + 1) * 128, :], "wst" + sg.b.name, writes=[sg.b])
                COPY("act" if k % 2 else "dve", dst.t[:, k, :], sg.t[:, 0:ncols], [sg.b], [dst.b])

        def norm_part(xg, gvec, hb, ss, rstd, junk, nsub):
            MEMSET(ss.t[:], 0.0, [ss.b])
            for s in range(nsub):
                ACT(junk.t[:], xg.t[:, s, :], AF.Square, [xg.b, ss.b], [junk.b, ss.b], accum=ss.t[:, s:s + 1])
            TS(rstd.t[:], ss.t[:], 1.0 / 1024, EPS, ALU.mult, ALU.add, [ss.b], [rstd.b])
            TS(rstd.t[:], rstd.t[:], -0.5, None, ALU.pow, None, [rstd.b], [rstd.b])
            for s in range(nsub):
                STT(hb.t[:, s, :], xg.t[:, s, :], rstd.t[:, s:s + 1], gvec.t[:], ALU.mult, ALU.mult,
                    [xg.b, rstd.b, gvec.b], [hb.b])

        def transpose_part(hb, hT, psT, nsub):
            for s in range(nsub):
                p = psT[s % 2]
                for kc in range(8):
                    TR(p.t[:, kc, :], hb.t[:, s, kc * 128:(kc + 1) * 128], ident_b.t[:], [hb.b, ident_b.b], [p.b])
                COPY("act" if s % 2 else "dve", hT.t[:, :, s * 128:(s + 1) * 128], p.t[:], [p.b], [hT.b])

        with contextlib.ExitStack() as ph:
            w0 = sbuf(ph, "w0", [128, 8, 3328], BF16)
            with contextlib.ExitStack() as wst:
                stage = [sbuf(wst, f"stg{i}", [128, 3328], F32) for i in range(2)]
                load_weight(wst, w0, w_in_e, 8, 3328, stage)
                S.barrier()
            gE = bcast_load(ph, "gE", norm_e, 1024)
            gq = bcast_load(ph, "gq", qnorm_b, 64)
            gk = bcast_load(ph, "gk", knorm_b, 64)
            xg = [sbuf(ph, f"xg{i}", [128, 4, 1024], F32) for i in range(2)]
            cst = [sbuf(ph, f"cst{i}", [128, 4, 128], F32) for i in range(2)]
            hb = sbuf(ph, "hb", [128, 4, 1024], BF16)
            hT = sbuf(ph, "hT", [128, 8, 512], BF16)
            junk = sbuf(ph, "junk", [128, 1024], BF16)
            ss = sbuf(ph, "ss", [128, 4], F32)
            rstd = sbuf(ph, "rstd", [128, 4], F32)
            fm_st = [sbuf(ph, f"fm_st{i}", [128, 8, 512], BF16) for i in range(2)]
            g_st = sbuf(ph, "g_st", [128, 8, 512], F32)
            va_st = sbuf(ph, "va_st", [128, 4, 8, 65], BF16)
            vb_st = sbuf(ph, "vb_st", [128, 4, 2, 65], BF16)
            sq = sbuf(ph, "sq", [128, 640], F32)
            ssq = sbuf(ph, "ssq", [128, 10], F32)
            qn = sbuf(ph, "qn", [128, 10, 64], F32)
            tA = sbuf(ph, "tA", [128, 10, 64], F32)
            tB = sbuf(ph, "tB", [128, 10, 64], F32)
            qrs = [sbuf(ph, f"qr{i}", [128, 10, 64], BF16) for i in range(2)]
            qbT_st = sbuf(ph, "qbT_st", [128, 4, 512], BF16)
            kbT_st = sbuf(ph, "kbT_st", [128, 512], BF16)
            psT = [psum(ph, f"psT{i}", [128, 8, 128], BF16) for i in range(2)]
            psF = [psum(ph, f"psF{i}", [128, 512], F32) for i in range(2)]
            psK = [psum(ph, f"psK{i}", [128, 1024], F32) for i in range(2)]
            MEMSET(va_st.t[:], 1.0, [va_st.b])
            MEMSET(vb_st.t[:], 1.0, [vb_st.b])

            def p0_loads(gi, src, seq, own_idx, halo_col):
                xb = xg[gi % 2]
                LOAD(xb.t[:], src.rearrange("(s p) d -> p s d", p=128), "xg" + xb.b.name, writes=[xb.b])
                if own_idx is not None:
                    t0 = own_idx * G
                    cb = cst[gi % 2]
                    LOAD(cb.t[:], cs0[t0:t0 + G, :].rearrange("(s p) c -> p s c", p=128), "cs" + cb.b.name, writes=[cb.b])

            def p0_group(gi, src, seq, own_idx, halo_col, mid_hook, late_hook):
                xb = xg[gi % 2]
                own = own_idx is not None
                if own:
                    t0 = own_idx * G
                    cb = cst[gi % 2]
                    loc = 512 + (own_idx % 8) * G
                else:
                    loc = halo_col
                if KSUB <= 0:
                    return
                if KSUB <= 1:
                    return
                fm = fm_st[gi % 2]
                chunks = ([0, 1, 2, 3] if own else []) + [4, 5, 6, 7]
                for j, cc in enumerate(chunks):
                    p = psF[j % 2]
                    for kc in range(8):
                        MM(p.t[:], w0.t[:, kc, cc * 128:(cc + 1) * 128], hT.t[:, kc, :], kc == 0, kc == 7, [w0.b, hT.b], [p.b])
                    COPY("act" if j % 2 else "dve", fm.t[:, cc, :], p.t[:], [p.b], [fm.b])
                if own:
                    STORE(qaT_d[:, t0:t0 + G].rearrange("(c p) t -> p c t", p=128), fm.t[:, 0:4, :], "st_qa", [fm.b], [db("qaT")])
                STORE(kaT_d[seq, :, loc:loc + G].rearrange("(c p) t -> p c t", p=128), fm.t[:, 4:8, :], "st_ka", [fm.b], [db("kaT")])
                if KSUB <= 2:
                    return
                if own:
                    for j in range(8):
                        cc = 18 + j
                        p = psF[j % 2]
                        for kc in range(8):
                            MM(p.t[:], w0.t[:, kc, cc * 128:(cc + 1) * 128], hT.t[:, kc, :], kc == 0, kc == 7, [w0.b, hT.b], [p.b])
                        ACT(g_st.t[:, j, :], p.t[:], AF.Silu, [p.b], [g_st.b])
                    STORE(gT_d[0][:, t0:t0 + G].rearrange("(c p) t -> p c t", p=128), g_st.t[:], "st_g0", [g_st.b], [db("gT0")])
                mid_hook()
                if KSUB <= 3:
                    return
                for s in range(4):
                    p = psF[s % 2]
                    for kc in range(8):
                        MM(p.t[:], hT.t[:, kc, s * 128:(s + 1) * 128], w0.t[:, kc, 1024:1536], kc == 0, kc == 7, [w0.b, hT.b], [p.b])
                    COPY("act", va_st.t[:, s, :, 0:64], p.t[:].rearrange("p (h d) -> p h d", d=64), [p.b], [va_st.b])
                STORE(va_d[seq, loc:loc + G, :].rearrange("(s p) c -> p s c", p=128), va_st.t[:].rearrange("p s h c -> p s (h c)"),
                      "st_va", [va_st.b], [db("va")])
                if not own or KSUB <= 4:
                    late_hook()
                    return
                def tm_mm(s):
                    p = psK[s % 2]
                    for kc in range(8):
                        MM(p.t[:, 0:512], hT.t[:, kc, s * 128:(s + 1) * 128], w0.t[:, kc, 1536:2048], kc == 0, kc == 7, [w0.b, hT.b], [p.b])
                    for kc in range(8):
                        MM(p.t[:, 512:768], hT.t[:, kc, s * 128:(s + 1) * 128], w0.t[:, kc, 2048:2304], kc == 0, kc == 7, [w0.b, hT.b], [p.b])

                def tm_chain(s):
                    p = psK[s % 2]
                    qr = qrs[s % 2]
                    COPY("act", vb_st.t[:, s, :, 0:64], p.t[:, 640:768].rearrange("p (h d) -> p h d", d=64), [p.b], [vb_st.b])
                    p3 = p.t[:, 0:640].rearrange("p (h d) -> p h d", d=64)
                    ACT(sq.t[:], p.t[:, 0:640], AF.Square, [p.b], [sq.b])
                    DVE(lambda e: e.reduce_sum(out=ssq.t[:], in_=sq.t[:].rearrange("p (h d) -> p h d", d=64), axis=AX.X), [sq.b], [ssq.b])
                    TS(ssq.t[:], ssq.t[:], 1.0 / 64, EPS, ALU.mult, ALU.add, [ssq.b], [ssq.b])
                    TS(ssq.t[:], ssq.t[:], -0.5, None, ALU.pow, None, [ssq.b], [ssq.b])
                    TT(qn.t[:], p3, ssq.t[:].unsqueeze(2).broadcast_to([128, 10, 64]), ALU.mult, [p.b, ssq.b], [qn.b])
                    TT(qn.t[:, 0:8, :], qn.t[:, 0:8, :], gq.t[:].unsqueeze(1).broadcast_to([128, 8, 64]), ALU.mult, [qn.b, gq.b], [qn.b])
                    TT(qn.t[:, 8:10, :], qn.t[:, 8:10, :], gk.t[:].unsqueeze(1).broadcast_to([128, 2, 64]), ALU.mult, [qn.b, gk.b], [qn.b])
                    cosb = cb.t[:, s, 0:64].unsqueeze(1).broadcast_to([128, 10, 64])
                    sinb = cb.t[:, s, 64:128].unsqueeze(1).broadcast_to([128, 10, 64])
                    TT(tA.t[:], qn.t[:], cosb, ALU.mult, [qn.b, cb.b], [tA.b])
                    TT(tB.t[:], qn.t[:], sinb, ALU.mult, [qn.b, cb.b], [tB.b])
                    TT(qr.t[:, :, 0:32], tA.t[:, :, 0:32], tB.t[:, :, 32:64], ALU.subtract, [tA.b, tB.b], [qr.b])
                    TT(qr.t[:, :, 32:64], tB.t[:, :, 0:32], tA.t[:, :, 32:64], ALU.add, [tA.b, tB.b], [qr.b])

                def tm_tr(s):
                    qr = qrs[s % 2]
                    pt = psT[s % 2]
                    qr2 = qr.t[:].rearrange("p h d -> p (h d)")
                    for c in range(5):
                        TR(pt.t[:, c, :], qr2[:, c * 128:(c + 1) * 128], ident_b.t[:], [qr.b, ident_b.b], [pt.b])
                    COPY("dve", qbT_st.t[:, :, s * 128:(s + 1) * 128], pt.t[:, 0:4, :], [pt.b], [qbT_st.b])
                    COPY("act", kbT_st.t[:, s * 128:(s + 1) * 128], pt.t[:, 4, :], [pt.b], [kbT_st.b])

                tm_mm(0)
                tm_chain(0)
                tm_mm(1)
                tm_chain(1)
                tm_mm(2)
                tm_tr(0)
                tm_chain(2)
                tm_mm(3)
                tm_tr(1)
                late_hook()
                tm_chain(3)

                def tail():
                    tm_tr(2)
                    tm_tr(3)
                    STORE(qbT_d[:, t0:t0 + G].rearrange("(c p) t -> p c t", p=128), qbT_st.t[:], "st_qb", [qbT_st.b], [db("qbT")])
                    tl = (own_idx % 8) * G
                    kdst = kbT_p if seq == 0 else kbT_c
                    STORE(kdst[:, tl:tl + G], kbT_st.t[:], "st_kb", [kbT_st.b], [db("kbT%d" % seq)])
                    if seq == 0:
                        STORE(vb_p[tl:tl + G, :].rearrange("(s p) c -> p s c", p=128), vb_st.t[:].rearrange("p s h c -> p s (h c)"),
                              "st_vb", [vb_st.b], [db("vb0")])
                    else:
                        for g2 in range(2):
                            STORE(vview(vb_c[g2])[tl:tl + G, :].rearrange("(s p) c -> p s c", p=128), vb_st.t[:, :, g2, :],
                                  "st_vb", [vb_st.b], [db("vb1")])
                return tail

            jobs = []
            for hgi in range(min(2, KNG)):
                jobs.append((x_halo[hgi * G:(hgi + 1) * G, :], 1, None, 0 if hgi == 0 else 4608))
            for g in range(8, min(16, 8 + KNG)):
                jobs.append((x_own[g * G:(g + 1) * G, :], 1, g, None))
            n_sample_jobs = len(jobs)
            for g in range(0, min(8, KNG)):
                jobs.append((x_own[g * G:(g + 1) * G, :], 0, g, None))
            p0_loads(0, *jobs[0])
            norm_part(xg[0], gE, hb, ss, rstd, junk, 4)
            transpose_part(hb, hT, psT, 4)
            for gi, job in enumerate(jobs):
                nxt = gi + 1 < len(jobs)
                if nxt:
                    p0_loads(gi + 1, *jobs[gi + 1])
                tail = p0_group(gi, *job, (lambda gi=gi: norm_part(xg[(gi + 1) % 2], gE, hb, ss, rstd, junk, 4)) if nxt else (lambda: None),
                                (lambda: transpose_part(hb, hT, psT, 4)) if nxt else (lambda: None))
                if tail is not None:
                    tail()
                if gi == n_sample_jobs - 1:
                    if "a" not in NOCC:
                        S.op("pool", lambda e: e.collective_compute("AllGather", ALU.bypass, replica_groups=GROUPS4, ins=[kbT_c], outs=[kbT_g]),
                             reads=[db("kbT1")], writes=[db("ccg0")], dma=True, stream="cc0", inc=1)
                    if "b" not in NOCC:
                        for g2 in range(2):
                            S.op("pool", (lambda s_, d_: (lambda e: e.collective_compute("AllGather", ALU.bypass, replica_groups=GROUPS4, ins=[s_], outs=[d_])))(vb_c[g2], vb_g[g2]),
                                 reads=[db("vb1")], writes=[db("ccg0")], dma=True, stream="cc0", inc=1)
            S.barrier()
        if STOP <= 0:
            S.emit()
            return nc

        with contextlib.ExitStack() as ph:
            kaT = sbuf(ph, "kaT", [128, 4, NLOC], BF16)
            va = sbuf(ph, "va", [128, 40, 520], BF16)
            qaT = sbuf(ph, "qaT", [128, 4, SEQ_OWN], BF16)
            bmi = sbuf(ph, "bmi", [128, 8 * 5 * 128], F32)
            bme = sbuf(ph, "bme", [128, 8 * 7 * 128], F32)
            s_sb = [sbuf(ph, f"s_sb{i}", [128, 896], F32) for i in range(3)]
            pT = [sbuf(ph, f"pT{i}", [128, 896], BF16) for i in range(4)]
            oA = [sbuf(ph, f"oA{i}", [65, 8, 128], F32) for i in range(2)]
            gtA = [sbuf(ph, f"gtA{i}", [64, 8, 128], F32) for i in range(2)]
            mgA = [sbuf(ph, f"mgA{i}", [64, 8, 128], BF16) for i in range(2)]
            psS = [psum(ph, f"psS{i}", [128, 1024], F32) for i in range(3)]
            psO = psum(ph, "psO", [128, 4, 128], F32)
            psB = psum(ph, "psB", [128, 4, 128], F32)
            rs8 = sbuf(ph, "rs8", [128, 128], F32)
            LOAD(bmi.t[:], bm_int, "bmi", writes=[bmi.b])
            it = 0
            pend = []
            deferred1 = []

            def tick1():
                for d in list(deferred1):
                    d[0] -= 1
                    if d[0] <= 0:
                        deferred1.remove(d)
                        d[1]()

            def flush_pend(keep=0):
                while len(pend) > keep:
                    pend.pop(0)()

            def mk_pv1(pt, nkt, kt0, h, qt, seq):
                def f():
                    hl, hf = h % 4, h // 4
                    for k in range(nkt):
                        MM(psO.t[0:65, hl, :], va.t[:, kt0 + k, h * 65:(h + 1) * 65], pt.t[:, k * 128:(k + 1) * 128],
                           k == 0, k == nkt - 1, [va.b, pt.b], [psO.b])
                    if hl == 3:
                        ob, gt, mgo = oA[qt % 2], gtA[qt % 2], mgA[qt % 2]
                        t0 = seq * SEQ_OWN + qt * 128
                        hs = slice(hf * 4, hf * 4 + 4)
                        if hf == 0:
                            LOAD(gt.t[:], gT_d[0][0:512, t0:t0 + 128].rearrange("(h d) t -> d h t", d=64), "ld_gt" + gt.b.name, [db("gT0")], [gt.b])
                        COPY("act", ob.t[:, hs, :], psO.t[0:65, :, :], [psO.b], [ob.b])
                        TT(ob.t[0:64, hs, :], ob.t[0:64, hs, :], gt.t[:, hs, :], ALU.mult, [ob.b, gt.b], [ob.b], eng="pool")

                        def post():
                            for hh in range(4):
                                MM(psB.t[64:72, 0, :], selA.t[64:65, hh, :], ob.t[64:65, hf * 4 + hh, :], hh == 0, hh == 3, [selA.b, ob.b], [psB.b])
                            DVE(lambda e: e.reciprocal(out=rs8.t[64:68, :], in_=psB.t[64:68, 0, :]), [psB.b], [rs8.b])
                            for hh in range(4):
                                MM(psB.t[0:64, hh, :], selB.t[64:68, hh, :], rs8.t[64:68, :], True, True, [selB.b, rs8.b], [psB.b])
                            TT(mgo.t[:, hs, :], ob.t[0:64, hs, :], psB.t[0:64, :, :], ALU.mult, [ob.b, psB.b], [mgo.b])
                            if hf == 1:
                                STORE(mgT_d[0][0:512, t0:t0 + 128].rearrange("(h d) t -> d h t", d=64), mgo.t[:], "st_oA", [mgo.b], [db("mgT0")])
                        deferred1.append([3, post])
                return f

            for seq in range(2):
                flush_pend()
                if seq == 0:
                    MEMSET(kaT.t[:], 0.0, [kaT.b])
                    MEMSET(va.t[:], 0.0, [va.b], eng="pool")
                    LOAD(kaT.t[:, :, 512:4608], kaT_d[0, :, 512:4608].rearrange("(c p) t -> p c t", p=128), "ld_ka", [db("kaT")], [kaT.b])
                    LOAD(va.t[:, 4:36, :], va_d[0, 512:4608, :].rearrange("(t p) c -> p t c", p=128), "ld_va", [db("va")], [va.b])
                else:
                    LOAD(kaT.t[:], kaT_d[1].rearrange("(c p) t -> p c t", p=128), "ld_ka", [db("kaT")], [kaT.b])
                    LOAD(va.t[:], va_d[1].rearrange("(t p) c -> p t c", p=128), "ld_va", [db("va")], [va.b])
                LOAD(qaT.t[:], qaT_d[:, seq * SEQ_OWN:(seq + 1) * SEQ_OWN].rearrange("(c p) t -> p c t", p=128), "ld_qa", [db("qaT")], [qaT.b])
                for qt in (list(range(32)) if KQT >= 32 else [0, 1, 5, 30, 31][:KQT]):
                    edge = qt < 2 or qt >= 30
                    if edge:
                        ei = seq * 4 + (qt if qt < 2 else qt - 28)
                        LOAD(bme.t[:], bm_edge[ei], "bme", writes=[bme.b])
                        nkt, kt0, bm = 7, qt + 1, bme
                    else:
                        nkt, kt0, bm = 5, qt + 2, bmi
                    n = nkt * 128
                    for h in range(8):
                        c, base = h // 2, (h % 2) * 64
                        ps = psS[it % 3]
                        ssb = s_sb[it % 3]
                        pt = pT[it % 4]
                        it += 1
                        for k in range(nkt):
                            MM(ps.t[:, k * 128:(k + 1) * 128], kaT.t[base:base + 64, c, (kt0 + k) * 128:(kt0 + k + 1) * 128],
                               qaT.t[base:base + 64, c, qt * 128:(qt + 1) * 128], True, True, [kaT.b, qaT.b], [ps.b])
                        STT(ssb.t[:, 0:n], ps.t[:, 0:n], 0.125, bm.t[:, h * n:(h + 1) * n], ALU.mult, ALU.add, [ps.b, bm.b], [ssb.b])
                        ACT(pt.t[:, 0:n], ssb.t[:, 0:n], AF.Exp, [ssb.b], [pt.b])
                        pend.append(mk_pv1(pt, nkt, kt0, h, qt, seq))
                        flush_pend(keep=2)
                        tick1()
            flush_pend()
            for d in list(deferred1):
                d[1]()
            S.barrier()
        if STOP <= 1:
            S.emit()
            return nc

        def dense_phase(layer, dk, scale, nkv, qper, k_loader, v_loader, q_loader, head_base, KG=3, rowpack=False):
            with contextlib.ExitStack() as ph:
                kT = [sbuf(ph, f"kT{i}", [128, 16384], BF16) for i in range(2)]
                vv = [sbuf(ph, f"vv{i}", [128, 128, 80], BF16) for i in range(2)]
                qT = [sbuf(ph, f"qT{i}", [128, SEQ_OWN], BF16) for i in range(2)]
                pT = [sbuf(ph, f"pT{i}", [128, KG * 512], BF16) for i in range(3)]
                ost = [sbuf(ph, f"ost{i}", [65, 512], F32) for i in range(2)]
                gtD = [sbuf(ph, f"gtD{i}", [64, 512], F32) for i in range(2)]
                mgD = [sbuf(ph, f"mgD{i}", [64, 512], BF16) for i in range(2)]
                psS = [psum(ph, f"psS{i}", [128, KG * 512], F32) for i in range(2)]
                psO = [psum(ph, f"psO{i}", [128, 512], F32) for i in range(1)]
                psB = psum(ph, "psB", [128, 512], F32)
                it = 0
                qbi = 0
                pending = None

                def mk_pv(po, ob, gt, mgo, vb_, pt, k0, sz, nkt, seq, h, qb):
                    def f():
                        for k in range(sz):
                            kt = k0 + k
                            MM(po.t[0:65, :], vb_.t[:, kt, 0:65], pt.t[:, k * 512:(k + 1) * 512],
                               kt == 0, kt == nkt - 1, [vb_.b, pt.b], [po.b])
                        if k0 + sz == nkt:
                            t0 = seq * SEQ_OWN + qb * 512
                            f0 = (head_base + h) * 64
                            COPY("dve", ob.t[:], po.t[0:65, :], [po.b], [ob.b])
                            DVE(lambda e: e.reciprocal(out=ob.t[64:65, :], in_=ob.t[64:65, :]), [ob.b], [ob.b])
                            TT(ob.t[0:64, :], ob.t[0:64, :], gt.t[:], ALU.mult, [ob.b, gt.b], [ob.b])

                            def post():
                                MM(psB.t[0:64, :], ones_f.t[64:65, 0:64], ob.t[64:65, :], True, True, [ones_f.b, ob.b], [psB.b])
                                TT(mgo.t[:], ob.t[0:64, :], psB.t[0:64, :], ALU.mult, [ob.b, psB.b], [mgo.b])
                                STORE(mgT_d[layer][f0:f0 + 64, t0:t0 + 512], mgo.t[:], "st_oD", [mgo.b], [db("mgT%d" % layer)])
                            deferred.append([6, post])
                    return f

                deferred = []

                def tick():
                    for d in list(deferred):
                        d[0] -= 1
                        if d[0] <= 0:
                            deferred.remove(d)
                            d[1]()

                jobs = []
                kvi = 0
                for seq in range(2):
                    for g in range(nkv):
                        for hq in range(qper):
                            jobs.append((seq, g, hq, kvi))
                        kvi += 1

                def job_loads(j):
                    seq, g, hq, kvi_ = jobs[j]
                    if hq == 0:
                        k_loader(seq, g, kT[kvi_ % 2])
                        v_loader(seq, g, vv[kvi_ % 2])
                    q_loader(seq, g * qper + hq, qT[j % 2])

                job_loads(0)
                for j, (seq, g, hq, kvi_) in enumerate(jobs):
                    nk = SEQ_OWN if seq == 0 else 4 * SEQ_OWN
                    nkt = nk // 128
                    kb_, vb_, qb_ = kT[kvi_ % 2], vv[kvi_ % 2], qT[j % 2]
                    h = g * qper + hq
                    for qb in range(8):
                        if qb == 2 and j + 1 < len(jobs):
                            job_loads(j + 1)
                        po = psO[0]
                        ob, gt, mgo = ost[qbi % 2], gtD[qbi % 2], mgD[qbi % 2]
                        qbi += 1
                        t0 = seq * SEQ_OWN + qb * 512
                        f0 = (head_base + h) * 64
                        LOAD(gt.t[:], gT_d[layer][f0:f0 + 64, t0:t0 + 512], "ld_gt" + gt.b.name, [db("gT%d" % layer)], [gt.b])
                        for k0 in range(0, nkt, KG):
                            sz = min(KG, nkt - k0)
                            ps = psS[it % 2]
                            pt = pT[it % 3]
                            it += 1
                            for k in range(sz):
                                kt = k0 + k
                                r0 = (kt % 2) * 64 if rowpack else 0
                                MM(ps.t[:, k * 512:(k + 1) * 512], kb_.t[r0:r0 + dk, kt * 128:(kt + 1) * 128],
                                   qb_.t[r0:r0 + dk, qb * 512:(qb + 1) * 512], True, True, [kb_.b, qb_.b], [ps.b])
                            ACT(pt.t[:, 0:sz * 512], ps.t[:, 0:sz * 512], AF.Exp, [ps.b], [pt.b], scale=scale)
                            if pending is not None:
                                pending()
                            pending = mk_pv(po, ob, gt, mgo, vb_, pt, k0, sz, nkt, seq, h, qb)
                            tick()
                if pending is not None:
                    pending()
                for d in list(deferred):
                    d[1]()
                S.barrier()

        def k0_loader(seq, g, kb_):
            for r0 in (0, 64):
                if seq == 0:
                    LOAD(kb_.t[r0:r0 + 64, 0:SEQ_OWN], kbT_p[g * 64:(g + 1) * 64, :], "ld_k" + kb_.b.name, [db("kbT0")], [kb_.b])
                else:
                    LOAD(kb_.t[r0:r0 + 64, :].rearrange("p (r t) -> p r t", r=4),
                         kbT_g.rearrange("(r p) t -> p r t", p=128)[g * 64:(g + 1) * 64], "ld_k" + kb_.b.name, [db("ccg0")], [kb_.b])

        def v0_loader(seq, g, vb_):
            if seq == 0:
                LOAD(vb_.t[:, 0:32, 0:65], vb_p[:, g * 65:(g + 1) * 65].rearrange("(t p) c -> p t c", p=128), "ld_v" + vb_.b.name, [db("vb0")], [vb_.b])
            else:
                LOAD(vb_.t[:, :, 0:65], vview(vb_g[g]).rearrange("(t p) c -> p t c", p=128), "ld_v" + vb_.b.name, [db("ccg0")], [vb_.b])

        def q0_loader(seq, h, qb_):
            for r0 in (0, 64):
                LOAD(qb_.t[r0:r0 + 64, :], qbT_d[h * 64:(h + 1) * 64, seq * SEQ_OWN:(seq + 1) * SEQ_OWN], "ld_q" + qb_.b.name, [db("qbT")], [qb_.b])

        dense_phase(0, 64, 0.125, 2, 4, k0_loader, v0_loader, q0_loader, 8, rowpack=True)
        if STOP <= 2:
            S.emit()
            return nc

        def outproj_phase(layer, w_src, x_src, x_buf, final):
            with contextlib.ExitStack() as ph:
                wo = sbuf(ph, "wo", [128, 8, 1024], BF16)
                with contextlib.ExitStack() as wst:
                    stage = [sbuf(wst, f"stg{i}", [128, 1024], F32) for i in range(2)]
                    load_weight(wst, wo, w_src, 8, 1024, stage)
                    S.barrier()
                gF = bcast_load(ph, "gF", norm_f, 1024)
                mgs = [sbuf(ph, f"mg{i}", [128, 8, 512], BF16) for i in range(2)]
                xg = [sbuf(ph, f"xg{i}", [128, 4, 1024], F32) for i in range(2)]
                yo = [sbuf(ph, f"yo{i}", [128, 4, 1024], F32) for i in range(2)]
                junk = sbuf(ph, "junk", [128, 1024], BF16)
                ss = sbuf(ph, "ss", [128, 4], F32)
                rstd = sbuf(ph, "rstd", [128, 4], F32)
                psY = [psum(ph, f"psY{i}", [128, 512], F32) for i in range(6)]

                def loads(g):
                    t0 = g * G
                    LOAD(xg[g % 2].t[:], x_src[t0:t0 + G, :].rearrange("(s p) d -> p s d", p=128), "ld_x" + xg[g % 2].b.name, x_buf, [xg[g % 2].b])
                    LOAD(mgs[g % 2].t[:], mgT_d[layer][:, t0:t0 + G].rearrange("(c p) t -> p c t", p=128), "ld_mg" + mgs[g % 2].b.name,
                         [db("mgT%d" % layer)], [mgs[g % 2].b])

                loads(0)
                for g in range(16):
                    t0 = g * G
                    if g + 1 < 16:
                        loads(g + 1)
                    xb, mg = xg[g % 2], mgs[g % 2]
                    for s_ in range(4):
                        ps2 = [psY[(s_ * 2 + n) % 6] for n in range(2)]
                        for c in range(8):
                            for n in range(2):
                                MM(ps2[n].t[:], mg.t[:, c, s_ * 128:(s_ + 1) * 128], wo.t[:, c, n * 512:(n + 1) * 512], c == 0, c == 7, [mg.b, wo.b], [ps2[n].b])
                        for n in range(2):
                            TT(xb.t[:, s_, n * 512:(n + 1) * 512], ps2[n].t[:], xb.t[:, s_, n * 512:(n + 1) * 512], ALU.add, [ps2[n].b, xb.b], [xb.b])
                    if not final:
                        STORE(x1_d[t0:t0 + G, :].rearrange("(s p) d -> p s d", p=128), xb.t[:], "st_x1", [xb.b], [db("x1")])
                        continue
                    yb = yo[g % 2]
                    MEMSET(ss.t[:], 0.0, [ss.b])
                    for s_ in range(4):
                        ACT(junk.t[:], xb.t[:, s_, :], AF.Square, [xb.b, ss.b], [junk.b, ss.b], accum=ss.t[:, s_:s_ + 1])
                    TS(rstd.t[:], ss.t[:], 1.0 / 1024, EPS, ALU.mult, ALU.add, [ss.b], [rstd.b])
                    TS(rstd.t[:], rstd.t[:], -0.5, None, ALU.pow, None, [rstd.b], [rstd.b])
                    for s_ in range(4):
                        STT(yb.t[:, s_, :], xb.t[:, s_, :], rstd.t[:, s_:s_ + 1], gF.t[:], ALU.mult, ALU.mult, [xb.b, rstd.b, gF.b], [yb.b])
                    STORE(y_out[t0:t0 + G, :].rearrange("(s p) d -> p s d", p=128), yb.t[:], "st_y", [yb.b], [db("y")])
                S.barrier()

        outproj_phase(0, w_out_e, x_own, [], False)
        if STOP <= 3:
            S.emit()
            return nc

        with contextlib.ExitStack() as ph:
            w1 = sbuf(ph, "w1", [128, 8, 1728], BF16)
            wq = sbuf(ph, "wq", [128, 3, 3072], BF16)
            wkv = sbuf(ph, "wkv", [128, 2, 2048], BF16)
            with contextlib.ExitStack() as wst:
                stage = [sbuf(wst, f"stg{i}", [128, 3072], F32) for i in range(2)]
                load_weight(wst, w1, w1x, 8, 1728, stage)
                load_weight(wst, wq, wuq2, 3, 3072, stage)
                load_weight(wst, wkv, wukv2, 2, 2048, stage)
                S.barrier()
            gO = bcast_load(ph, "gO", norm_o, 1024)
            gql = bcast_load(ph, "gql", qlat_g, 384)
            gkl = bcast_load(ph, "gkl", kvlat_g, 256)
            x1s = [sbuf(ph, f"x1s{i}", [128, 4, 1024], F32) for i in range(2)]
            hb = sbuf(ph, "hb", [128, 4, 1024], BF16)
            hT = sbuf(ph, "hT", [128, 8, 512], BF16)
            junk = sbuf(ph, "junk", [128, 1024], BF16)
            ss = sbuf(ph, "ss", [128, 4], F32)
            rstd = sbuf(ph, "rstd", [128, 4], F32)
            ss2 = sbuf(ph, "ss2", [128, 2], F32)
            g_st = sbuf(ph, "g_st", [128, 8, 512], F32)
            lats = [sbuf(ph, f"lat{i}", [128, 640], BF16) for i in range(2)]
            latT = sbuf(ph, "latT", [128, 5, 512], BF16)
            c1ts = [sbuf(ph, f"c1t{i}", [96, 512], F32) for i in range(2)]
            s1ts = [sbuf(ph, f"s1t{i}", [96, 512], F32) for i in range(2)]
            ckts = [sbuf(ph, f"ckt{i}", [32, 512], F32) for i in range(2)]
            skts = [sbuf(ph, f"skt{i}", [32, 512], F32) for i in range(2)]
            ta = sbuf(ph, "ta", [96, 512], F32)
            tb = sbuf(ph, "tb", [96, 512], F32)
            q_st = [sbuf(ph, f"q_st{i}", [96, 512], BF16) for i in range(4)]
            kr_st = sbuf(ph, "kr_st", [32, 512], BF16)
            kn_sts = [sbuf(ph, f"kn_st{i}", [128, 8, 512], BF16) for i in range(2)]
            v_sts = [sbuf(ph, f"v_st{i}", [128, 4, 16, 65], BF16) for i in range(2)]
            psT = [psum(ph, f"psT{i}", [128, 8, 128], BF16) for i in range(2)]
            psF = [psum(ph, f"psF{i}", [128, 512], F32) for i in range(2)]
            psL = [psum(ph, f"psL{i}", [128, 1024], F32) for i in range(2)]
            psY = psF
            for v_st in v_sts:
                MEMSET(v_st.t[:], 1.0, [v_st.b])

            def p3_loads(gi, g):
                t0 = g * G
                x1 = x1s[gi % 2]
                LOAD(x1.t[:], x1_d[t0:t0 + G, :].rearrange("(s p) d -> p s d", p=128), "ld_x" + x1.b.name, [db("x1")], [x1.b], q="pool")
                LOAD(c1ts[gi % 2].t[:], c1[:, t0:t0 + G], "ld_c1%d" % (gi % 2), writes=[c1ts[gi % 2].b], q="pool")
                LOAD(s1ts[gi % 2].t[:], s1[:, t0:t0 + G], "ld_s1%d" % (gi % 2), writes=[s1ts[gi % 2].b], q="pool")
                LOAD(ckts[gi % 2].t[:], ck[:, t0:t0 + G], "ld_ck%d" % (gi % 2), writes=[ckts[gi % 2].b], q="pool")
                LOAD(skts[gi % 2].t[:], sk[:, t0:t0 + G], "ld_sk%d" % (gi % 2), writes=[skts[gi % 2].b], q="pool")

            def p3_group(gi, g, mid_hook):
                t0 = g * G
                seq = g // 8
                tl = (g % 8) * G
                x1 = x1s[gi % 2]
                c1t, s1t, ckt, skt = c1ts[gi % 2], s1ts[gi % 2], ckts[gi % 2], skts[gi % 2]
                for j in range(8):
                    p = psF[j % 2]
                    c0 = 672 + j * 128
                    for kc in range(8):
                        MM(p.t[:], w1.t[:, kc, c0:c0 + 128], hT.t[:, kc, :], kc == 0, kc == 7, [w1.b, hT.b], [p.b])
                    ACT(g_st.t[:, j, :], p.t[:], AF.Silu, [p.b], [g_st.b])
                STORE(gT_d[1][:, t0:t0 + G].rearrange("(c p) t -> p c t", p=128), g_st.t[:], "st_g1", [g_st.b], [db("gT1")])
                pa, pb = psF[0], psF[1]
                for kc in range(8):
                    MM(pa.t[0:32, :], w1.t[:, kc, 640:672], hT.t[:, kc, :], kc == 0, kc == 7, [w1.b, hT.b], [pa.b])
                for kc in range(8):
                    MM(pb.t[0:32, :], w1.t[:, kc, 1696:1728], hT.t[:, kc, :], kc == 0, kc == 7, [w1.b, hT.b], [pb.b])
                TT(ta.t[0:32, :], pa.t[0:32, :], ckt.t[:], ALU.mult, [pa.b, ckt.b], [ta.b])
                TT(tb.t[0:32, :], pb.t[0:32, :], skt.t[:], ALU.mult, [pb.b, skt.b], [tb.b])
                TT(kr_st.t[:], ta.t[0:32, :], tb.t[0:32, :], ALU.add, [ta.b, tb.b], [kr_st.b])
                STORE((k1r_p if seq == 0 else k1r_c)[:, tl:tl + G], kr_st.t[:], "st_kr", [kr_st.b], [db("k1r%d" % seq)])
                def lat_mm(s):
                    p = psL[s % 2]
                    for kc in range(8):
                        MM(p.t[:, 0:384], hT.t[:, kc, s * 128:(s + 1) * 128], w1.t[:, kc, 0:384], kc == 0, kc == 7, [w1.b, hT.b], [p.b])
                    for kc in range(8):
                        MM(p.t[:, 512:768], hT.t[:, kc, s * 128:(s + 1) * 128], w1.t[:, kc, 384:640], kc == 0, kc == 7, [w1.b, hT.b], [p.b])

                def lat_chain(s):
                    p = psL[s % 2]
                    lat = lats[s % 2]
                    MEMSET(ss2.t[:], 0.0, [ss2.b])
                    ACT(junk.t[:, 0:384], p.t[:, 0:384], AF.Square, [p.b, ss2.b], [junk.b, ss2.b], accum=ss2.t[:, 0:1])
                    ACT(junk.t[:, 384:640], p.t[:, 512:768], AF.Square, [p.b, ss2.b], [junk.b, ss2.b], accum=ss2.t[:, 1:2])
                    TS(ss2.t[:, 0:1], ss2.t[:, 0:1], 1.0 / 384, EPS, ALU.mult, ALU.add, [ss2.b], [ss2.b])
                    TS(ss2.t[:, 1:2], ss2.t[:, 1:2], 1.0 / 256, EPS, ALU.mult, ALU.add, [ss2.b], [ss2.b])
                    TS(ss2.t[:], ss2.t[:], -0.5, None, ALU.pow, None, [ss2.b], [ss2.b])
                    STT(lat.t[:, 0:384], p.t[:, 0:384], ss2.t[:, 0:1], gql.t[:], ALU.mult, ALU.mult, [p.b, ss2.b, gql.b], [lat.b])
                    STT(lat.t[:, 384:640], p.t[:, 512:768], ss2.t[:, 1:2], gkl.t[:], ALU.mult, ALU.mult, [p.b, ss2.b, gkl.b], [lat.b])

                def lat_tr(s):
                    lat = lats[s % 2]
                    pt = psT[s % 2]
                    for c in range(5):
                        TR(pt.t[:, c, :], lat.t[:, c * 128:(c + 1) * 128], ident_b.t[:], [lat.b, ident_b.b], [pt.b])
                    COPY("act", latT.t[:, :, s * 128:(s + 1) * 128], pt.t[:, 0:5, :], [pt.b], [latT.b])

                lat_mm(0)
                lat_chain(0)
                lat_mm(1)
                lat_chain(1)
                lat_mm(2)
                lat_tr(0)
                lat_chain(2)
                lat_mm(3)
                lat_tr(1)
                lat_chain(3)
                lat_tr(2)
                lat_tr(3)
                for h in range(16):
                    if h % 2 == 0:
                        pa_t, pb_t, pa_b, pb_b = psF[0].t[0:96, :], psF[1].t[0:96, :], psF[0].b, psF[1].b
                    else:
                        pa_t, pb_t, pa_b, pb_b = psL[0].t[0:96, 0:512], psL[1].t[0:96, 0:512], psL[0].b, psL[1].b
                    qs = q_st[h % 4]
                    for kc in range(3):
                        MM(pa_t, wq.t[:, kc, h * 96:(h + 1) * 96], latT.t[:, kc, :], kc == 0, kc == 2, [wq.b, latT.b], [pa_b])
                    for kc in range(3):
                        MM(pb_t, wq.t[:, kc, 1536 + h * 96:1536 + (h + 1) * 96], latT.t[:, kc, :], kc == 0, kc == 2, [wq.b, latT.b], [pb_b])
                    TT(ta.t[:], pa_t, c1t.t[:], ALU.mult, [pa_b, c1t.b], [ta.b])
                    TT(tb.t[:], pb_t, s1t.t[:], ALU.mult, [pb_b, s1t.b], [tb.b])
                    TT(qs.t[:], ta.t[:], tb.t[:], ALU.add, [ta.b, tb.b], [qs.b])
                    STORE(q1T_d[h, :, t0:t0 + G], qs.t[:], "st_q1", [qs.b], [db("q1T")])
                mid_hook()
                kn_st = kn_sts[gi % 2]
                v_st = v_sts[gi % 2]
                for c in range(8):
                    p = psY[c % 2]
                    for kc in range(2):
                        MM(p.t[:], wkv.t[:, kc, c * 128:(c + 1) * 128], latT.t[:, 3 + kc, :], kc == 0, kc == 1, [wkv.b, latT.b], [p.b])
                    COPY("act", kn_st.t[:, c, :], p.t[:], [p.b], [kn_st.b])
                if seq == 0:
                    STORE(k1n_p[:, tl:tl + G].rearrange("(c p) t -> p c t", p=128), kn_st.t[:], "st_kn", [kn_st.b], [db("k1n0")])
                else:
                    for c in range(8):
                        STORE(k1n_c[c][:, tl:tl + G], kn_st.t[:, c, :], "st_kn", [kn_st.b], [db("k1n1")])
                for s in range(4):
                    for n in range(2):
                        p = psY[(s * 2 + n) % 2]
                        for kc in range(2):
                            MM(p.t[:], latT.t[:, 3 + kc, s * 128:(s + 1) * 128], wkv.t[:, kc, 1024 + n * 512:1024 + (n + 1) * 512],
                               kc == 0, kc == 1, [wkv.b, latT.b], [p.b])
                        COPY("act", v_st.t[:, s, n * 8:(n + 1) * 8, 0:64], p.t[:].rearrange("p (h d) -> p h d", d=64), [p.b], [v_st.b])
                if seq == 0:
                    STORE(v1_p[tl:tl + G, :].rearrange("(s p) c -> p s c", p=128),
                          v_st.t[:].rearrange("p s h c -> p s (h c)"), "st_v1", [v_st.b], [db("v10")])
                else:
                    for h in range(16):
                        STORE(vview(v1_c[h])[tl:tl + G, :].rearrange("(s p) c -> p s c", p=128), v_st.t[:, :, h, :],
                              "st_v1", [v_st.b], [db("v11")])

            order = list(range(8, 16)) + list(range(0, 8))
            p3_loads(0, order[0])
            norm_part(x1s[0], gO, hb, ss, rstd, junk, 4)
            transpose_part(hb, hT, psT, 4)
            for gi, g in enumerate(order):
                nxt = gi + 1 < 16
                if nxt:
                    p3_loads(gi + 1, order[gi + 1])
                p3_group(gi, g, (lambda gi=gi: norm_part(x1s[(gi + 1) % 2], gO, hb, ss, rstd, junk, 4)) if nxt else (lambda: None))
                if nxt:
                    transpose_part(hb, hT, psT, 4)
            cc_list = [("k1r", k1r_c, k1r_g)] + [("k1n", k1n_c[i], k1n_g[i]) for i in range(8)] + [("v1", v1_c[h], v1_g[h]) for h in range(16)]
            for ci, (nm, src, dst) in enumerate(cc_list):
                S.op("pool", (lambda s_, d_: (lambda e: e.collective_compute("AllGather", ALU.bypass, replica_groups=GROUPS4, ins=[s_], outs=[d_])))(src, dst),
                     reads=[db(nm + "1")], writes=[db("ccg1")], dma=True, stream="cc1", inc=1)
            S.barrier()
        if STOP <= 4:
            S.emit()
            return nc

        def k1_loader(seq, h, kb_):
            if seq == 0:
                LOAD(kb_.t[0:64, 0:SEQ_OWN], k1n_p[h * 64:(h + 1) * 64, :], "ld_k" + kb_.b.name, [db("k1n0")], [kb_.b])
                LOAD(kb_.t[64:96, 0:SEQ_OWN], k1r_p[:, :], "ld_k" + kb_.b.name, [db("k1r0")], [kb_.b])
            else:
                LOAD(kb_.t[0:64, :].rearrange("p (r t) -> p r t", r=4),
                     k1n_g[h // 2].rearrange("(r f) t -> f r t", f=128)[(h % 2) * 64:(h % 2 + 1) * 64], "ld_k" + kb_.b.name, [db("ccg1")], [kb_.b])
                LOAD(kb_.t[64:96, :].rearrange("p (r t) -> p r t", r=4),
                     k1r_g.rearrange("(r f) t -> f r t", f=32), "ld_k" + kb_.b.name, [db("ccg1")], [kb_.b])

        def v1_loader(seq, h, vb_):
            if seq == 0:
                LOAD(vb_.t[:, 0:32, 0:65], v1_p[:, h * 65:(h + 1) * 65].rearrange("(t p) c -> p t c", p=128), "ld_v" + vb_.b.name, [db("v10")], [vb_.b])
            else:
                LOAD(vb_.t[:, :, 0:65], vview(v1_g[h]).rearrange("(t p) c -> p t c", p=128), "ld_v" + vb_.b.name, [db("ccg1")], [vb_.b])

        def q1_loader(seq, h, qb_):
            LOAD(qb_.t[0:96, :], q1T_d[h, :, seq * SEQ_OWN:(seq + 1) * SEQ_OWN], "ld_q" + qb_.b.name, [db("q1T")], [qb_.b])

        dense_phase(1, 96, 96 ** -0.5, 16, 1, k1_loader, v1_loader, q1_loader, 0)
        if STOP <= 5:
            S.emit()
            return nc

        outproj_phase(1, w_out_o, x1_d, [db("x1")], True)
        S.emit()
    return nc


def _na_table(r0, R, base_off, nkt, rpb):
    q = np.arange(128)
    qr = r0 + q // 64
    qc = q % 64
    k = np.arange(128)
    kr_in = k // 64
    kc = k % 64
    rs = np.clip(qr - 4, 0, R - 8)
    cs = np.clip(qc - 8, 0, 64 - 16)
    out = np.full((128, 8, nkt, 128), NEG, np.float32)
    for kt in range(nkt):
        kr = r0 + base_off + 2 * kt + kr_in
        valid = ((kr[:, None] >= rs[None, :]) & (kr[:, None] < rs[None, :] + 8) & (kr[:, None] >= 0) & (kr[:, None] < R)
                 & (kc[:, None] >= cs[None, :]) & (kc[:, None] < cs[None, :] + 16))
        ro = np.clip(kr[:, None] - qr[None, :] + 7, 0, 14)
        co = np.clip(kc[:, None] - qc[None, :] + 15, 0, 30)
        vals = rpb[:, ro, co]
        vals = np.where(valid[None], vals, np.float32(NEG))
        out[:, :, kt, :] = vals.transpose(1, 0, 2)
    return out


def _rope_tables(pos, rot_dim):
    nf = rot_dim // 4
    inv = (np.float32(10000.0) ** (-np.arange(nf, dtype=np.float32) / np.float32(nf))).astype(np.float32)
    row = (pos // 64).astype(np.float32)
    col = (pos % 64).astype(np.float32)
    ang = np.concatenate([row[:, None] * inv[None], col[:, None] * inv[None]], axis=-1).astype(np.float32)
    return np.cos(ang).astype(np.float32), np.sin(ang).astype(np.float32)


_PROG = None


def kernel(x_prompt, x_sample, norm_e, w_in_e, rpb_a, qnorm_b, knorm_b, w_out_e,
           norm_o, w_in_o, qlat_g, kvlat_g, w_uq, w_ukv, w_out_o, norm_f):
    global _PROG
    f = lambda a: np.ascontiguousarray(np.asarray(a, dtype=np.float32))
    x_prompt, x_sample = f(x_prompt), f(x_sample)
    w_in_e0, w_out_e0, w_in_o0, w_uq0, w_ukv0, w_out_o0 = f(w_in_e)[0], f(w_out_e)[0], f(w_in_o)[0], f(w_uq)[0], f(w_ukv)[0], f(w_out_o)[0]
    rpb = f(rpb_a)[0]
    kr = w_in_o0[:, 640:672]
    w1x = np.concatenate([w_in_o0, kr[:, 16:32], kr[:, 0:16]], axis=1)
    uq = w_uq0.reshape(384, 16, 96)
    uq_sw = np.concatenate([uq[:, :, 0:64], uq[:, :, 80:96], uq[:, :, 64:80]], axis=2)
    wuq2 = np.concatenate([uq.reshape(384, 1536), uq_sw.reshape(384, 1536)], axis=1)
    ukv = w_ukv0.reshape(256, 16, 128)
    wukv2 = np.concatenate([ukv[:, :, 0:64].reshape(256, 1024), ukv[:, :, 64:128].reshape(256, 1024)], axis=1)
    bm_int = _na_table(8, 64, -4, 5, rpb).reshape(128, -1)
    ident = np.eye(128, dtype=np.float32)
    selA_h = np.tile(np.eye(8, dtype=np.float32).reshape(1, 64), (128, 1))
    selB_h = np.zeros((128, 8, 64), np.float32)
    for hh in range(8):
        selB_h[64 + hh, hh, :] = 1.0
    selB_h = selB_h.reshape(128, 512)
    common = dict(w_in_e=w_in_e0, w_out_e=w_out_e0, w1x=f(w1x), wuq2=f(wuq2), wukv2=f(wukv2), w_out_o=w_out_o0,
                  norm_e=f(norm_e).reshape(1, 1024), norm_o=f(norm_o).reshape(1, 1024), norm_f=f(norm_f).reshape(1, 1024),
                  qnorm_b=f(qnorm_b).reshape(1, 64), knorm_b=f(knorm_b).reshape(1, 64),
                  qlat_g=f(qlat_g).reshape(1, 384), kvlat_g=f(kvlat_g).reshape(1, 256),
                  bm_int=f(bm_int), ident=ident, selA=selA_h, selB=selB_h)
    edge_p = [_na_table(r0, 64, -6, 7, rpb).reshape(128, -1) for r0 in (0, 2, 60, 62)]
    in_maps = []
    for c in range(8):
        sq, j = c // 4, c % 4
        xs = x_sample[sq]
        x_own = np.concatenate([x_prompt[c], xs[j * 4096:(j + 1) * 4096]], axis=0)
        halo = np.zeros((1024, 1024), np.float32)
        if j > 0:
            halo[0:512] = xs[j * 4096 - 512:j * 4096]
        if j < 3:
            halo[512:1024] = xs[(j + 1) * 4096:(j + 1) * 4096 + 512]
        pos = np.concatenate([np.arange(4096), j * 4096 + np.arange(4096)])
        cos0, sin0 = _rope_tables(pos, 64)
        cs0 = np.concatenate([cos0, cos0, sin0, sin0], axis=1)
        cos1, sin1 = _rope_tables(pos, 32)
        c1 = np.ones((96, 8192), np.float32)
        s1 = np.zeros((96, 8192), np.float32)
        c1[64:80] = cos1.T
        c1[80:96] = cos1.T
        s1[64:80] = -sin1.T
        s1[80:96] = sin1.T
        edge_s = [_na_table(64 * j + r0, 256, -6, 7, rpb).reshape(128, -1) for r0 in (0, 2, 60, 62)]
        m = dict(common)
        m.update(x_own=f(x_own), x_halo=halo, cs0=f(cs0), c1=c1, s1=s1, ck=f(c1[64:96]), sk=f(s1[64:96]),
                 bm_edge=f(np.stack(edge_p + edge_s, axis=0)))
        in_maps.append(m)
    if _PROG is None:
        _PROG = build_program()
    res = run_bass_kernel_spmd(_PROG, in_maps, core_ids=list(range(8)))
    if DEBUG:
        kernel.last = res.results
    y_prompt = np.stack([np.asarray(res.results[c]["y_out"])[0:4096] for c in range(8)], axis=0)
    y_sample = np.stack([np.concatenate([np.asarray(res.results[sq * 4 + j]["y_out"])[4096:8192] for j in range(4)], axis=0)
                         for sq in range(2)], axis=0)
    return (y_prompt.astype(np.float32), y_sample.astype(np.float32))
```

```python
import contextlib
import os
import numpy as np
import concourse.bass as bass
import concourse.mybir as mybir
from concourse.bass_utils import run_bass_kernel_spmd

F32 = mybir.dt.float32
BF16 = mybir.dt.bfloat16
AF = mybir.ActivationFunctionType
ALU = mybir.AluOpType
AX = mybir.AxisListType

ENGS = ("pe", "act", "dve", "pool", "sp")
MAXV = 8000
EPS = 1e-6
NEG = -30000.0
DEBUG = bool(int(os.environ.get("KDEBUG", "0")))
STOP = int(os.environ.get("KSTOP", "99"))
NOCC = os.environ.get("KNOCC", "")
KSUB = int(os.environ.get("KSUB", "99"))
KNG = int(os.environ.get("KNG", "99"))
KP1 = int(os.environ.get("KP1", "99"))
KQT = int(os.environ.get("KQT", "32"))


class Buf:
    __slots__ = ("name", "writers", "readers", "excl")

    def __init__(self, name="", excl=False):
        self.name = name
        self.writers = []
        self.readers = []
        self.excl = excl


class Op:
    __slots__ = ("eng", "fn", "deps", "dma", "stream", "idx", "signal", "sem", "val", "inc")


class Sched:
    def __init__(self, nc):
        self.nc = nc
        self.ops = {e: [] for e in ENGS}
        self.streams = {}
        self.bar = []

    def op(self, eng, fn, reads=(), writes=(), dma=False, stream=None, inc=None):
        o = Op()
        o.eng, o.fn, o.dma, o.stream = eng, fn, dma, stream
        o.signal = dma
        o.sem, o.val = None, 0
        o.inc = inc if inc is not None else (16 if dma else 1)
        deps = list(self.bar)
        if any(b.excl for b in reads):
            writes = list(writes) + [b for b in reads if b.excl and b not in writes]
            reads = [b for b in reads if not b.excl]
        for b in reads:
            deps.extend(b.writers)
        for b in writes:
            deps.extend(b.writers)
            deps.extend(b.readers)
        best = {}
        for d in deps:
            if (not d.dma) and d.eng == "pe" and eng == "pe" and not dma:
                continue
            key = ("s", d.stream) if d.dma else ("e", d.eng)
            if key not in best or best[key].idx < d.idx:
                best[key] = d
        o.deps = list(best.values())
        for d in o.deps:
            d.signal = True
        if dma:
            lst = self.streams.setdefault(stream, [])
            o.idx = len(lst)
            lst.append(o)
        else:
            o.idx = len(self.ops[eng])
        self.ops[eng].append(o)
        for b in reads:
            b.readers.append(o)
        for b in writes:
            b.writers = [o]
            b.readers = []
        return o

    def barrier(self):
        bar = []
        for e in ENGS:
            comp = [o for o in self.ops[e] if not o.dma]
            if comp:
                bar.append(comp[-1])
        for sname, lst in self.streams.items():
            if lst and not str(sname).startswith("cc"):
                bar.append(lst[-1])
        self.bar = bar

    def emit(self, final_eng="sp"):
        nc = self.nc
        semreq = []

        def newsem(name):
            semreq.append(name)
            return len(semreq) - 1

        for e in ENGS:
            cur, cnt = None, 0
            for o in self.ops[e]:
                if o.dma or not o.signal:
                    continue
                if cur is None or cnt >= MAXV:
                    cur, cnt = newsem(f"m{e}{len(semreq)}"), 0
                cnt += 1
                o.sem, o.val = cur, cnt
        for sname, lst in self.streams.items():
            cur, cnt = None, 0
            for o in lst:
                if cur is None or cnt >= MAXV * 16:
                    cur, cnt = newsem(f"d{len(semreq)}"), 0
                cnt += o.inc
                o.sem, o.val = cur, cnt
        print("semaphores requested:", len(semreq), flush=True)
        with contextlib.ExitStack() as st:
            sems = [st.enter_context(nc.semaphore(n)) for n in semreq]
            block = st.enter_context(nc.Block())
            handles = {"pe": "tensor", "act": "scalar", "dve": "vector", "pool": "gpsimd", "sp": "sync"}
            finals = [lst[-1] for lst in self.streams.values() if lst]

            def make(e):
                def body(eh):
                    waited = {}
                    for o in self.ops[e]:
                        for d in o.deps:
                            if waited.get(d.sem, 0) < d.val:
                                eh.wait_ge(sems[d.sem], d.val)
                                waited[d.sem] = d.val
                        ins = o.fn(eh)
                        if o.signal:
                            ins.then_inc(sems[o.sem], o.inc)
                    if e == final_eng:
                        for d in finals:
                            if waited.get(d.sem, 0) < d.val:
                                eh.wait_ge(sems[d.sem], d.val)
                                waited[d.sem] = d.val
                return body

            for e in ENGS:
                getattr(block, handles[e])(make(e))


NTOK = 8192
SEQ_OWN = 4096
NLOC = 5120
G = 512


def build_program():
    nc = bass.Bass("TRN2", target_bir_lowering=False)
    S = Sched(nc)
    scratch_kind = "ExternalOutput" if DEBUG else "Internal"

    def din(name, shape, dt=F32):
        return nc.dram_tensor(name, list(shape), dt, kind="ExternalInput").ap()

    def dscr(name, shape, dt, dbg=True):
        return nc.dram_tensor(name, list(shape), dt, kind=(scratch_kind if dbg else "Internal")).ap()

    x_own = din("x_own", [NTOK, 1024])
    x_halo = din("x_halo", [1024, 1024])
    w_in_e = din("w_in_e", [1024, 3328])
    w_out_e = din("w_out_e", [1024, 1024])
    w1x = din("w1x", [1024, 1728])
    wuq2 = din("wuq2", [384, 3072])
    wukv2 = din("wukv2", [256, 2048])
    w_out_o = din("w_out_o", [1024, 1024])
    norm_e = din("norm_e", [1, 1024])
    norm_o = din("norm_o", [1, 1024])
    norm_f = din("norm_f", [1, 1024])
    qnorm_b = din("qnorm_b", [1, 64])
    knorm_b = din("knorm_b", [1, 64])
    qlat_g = din("qlat_g", [1, 384])
    kvlat_g = din("kvlat_g", [1, 256])
    cs0 = din("cs0", [NTOK, 128])
    c1 = din("c1", [96, NTOK])
    s1 = din("s1", [96, NTOK])
    ck = din("ck", [32, NTOK])
    sk = din("sk", [32, NTOK])
    bm_int = din("bm_int", [128, 8 * 5 * 128])
    bm_edge = din("bm_edge", [8, 128, 8 * 7 * 128])
    ident_in = din("ident", [128, 128])
    selA_in = din("selA", [128, 64])
    selB_in = din("selB", [128, 512])
    y_out = nc.dram_tensor("y_out", [NTOK, 1024], F32, kind="ExternalOutput").ap()

    qaT_d = dscr("qaT_d", [512, NTOK], BF16)
    kaT_d = dscr("kaT_d", [2, 512, NLOC], BF16)
    va_d = dscr("va_d", [2, NLOC, 520], BF16)
    qbT_d = dscr("qbT_d", [512, NTOK], BF16)
    kbT_p = dscr("kbT_p", [128, SEQ_OWN], BF16)
    vb_p = dscr("vb_p", [SEQ_OWN, 130], BF16)
    kbT_c = dscr("kbT_c", [128, SEQ_OWN], BF16, dbg=False)
    vb_c = [dscr(f"vb_c{g}", [128, 2080], BF16, dbg=False) for g in range(2)]
    kbT_g = dscr("kbT_g", [512, SEQ_OWN], BF16, dbg=False)
    vb_g = [dscr(f"vb_g{g}", [512, 2080], BF16, dbg=False) for g in range(2)]
    gT_d = [dscr(f"gT{l}_d", [1024, NTOK], F32) for l in range(2)]
    mgT_d = [dscr(f"mgT{l}_d", [1024, NTOK], BF16) for l in range(2)]
    x1_d = dscr("x1_d", [NTOK, 1024], F32)
    q1T_d = dscr("q1T_d", [16, 96, NTOK], BF16)
    k1n_p = dscr("k1n_p", [1024, SEQ_OWN], BF16)
    k1r_p = dscr("k1r_p", [32, SEQ_OWN], BF16)
    v1_p = dscr("v1_p", [SEQ_OWN, 1040], BF16)
    k1n_c = [dscr(f"k1n_c{i}", [128, SEQ_OWN], BF16, dbg=False) for i in range(8)]
    k1r_c = dscr("k1r_c", [32, SEQ_OWN], BF16, dbg=False)
    v1_c = [dscr(f"v1_c{h}", [128, 2080], BF16, dbg=False) for h in range(16)]
    k1n_g = [dscr(f"k1n_g{i}", [512, SEQ_OWN], BF16, dbg=False) for i in range(8)]
    k1r_g = dscr("k1r_g", [128, SEQ_OWN], BF16, dbg=False)
    v1_g = [dscr(f"v1_g{h}", [512, 2080], BF16, dbg=False) for h in range(16)]

    def vview(ap):
        return ap.rearrange("p (a c) -> (p a) c", c=65)

    D = {}

    def db(name):
        if name not in D:
            D[name] = Buf(name)
        return D[name]

    GROUPS4 = [[0, 1, 2, 3], [4, 5, 6, 7]]

    def LOAD(out, in_, stream, reads=(), writes=(), q="sp"):
        return S.op(q, lambda e: e.dma_start(out=out, in_=in_), reads=reads, writes=writes, dma=True, stream=stream)

    def STORE(out, in_, stream, reads=(), writes=()):
        return S.op("sp", lambda e: e.dma_start(out=out, in_=in_), reads=reads, writes=writes, dma=True, stream=stream)

    def MM(out, lhsT, rhs, start, stop, reads, writes):
        return S.op("pe", lambda e: e.matmul(out, lhsT=lhsT, rhs=rhs, start=start, stop=stop), reads=reads, writes=writes)

    def TR(out, in_, ident, reads, writes):
        return S.op("pe", lambda e: e.transpose(out=out, in_=in_, identity=ident), reads=reads, writes=writes)

    def ACT(out, in_, func, reads, writes, scale=1.0, accum=None):
        if accum is None:
            return S.op("act", lambda e: e.activation(out=out, in_=in_, func=func, scale=scale), reads=reads, writes=writes)
        return S.op("act", lambda e: e.activation(out=out, in_=in_, func=func, scale=scale, accum_out=accum), reads=reads, writes=writes)

    def DVE(fn, reads, writes):
        return S.op("dve", fn, reads=reads, writes=writes)

    def COPY(eng, out, in_, reads, writes):
        if eng == "act":
            return S.op("act", lambda e: e.copy(out=out, in_=in_), reads=reads, writes=writes)
        return S.op(eng, lambda e: e.tensor_copy(out=out, in_=in_), reads=reads, writes=writes)

    def TT(out, in0, in1, op, reads, writes, eng="dve"):
        return S.op(eng, lambda e: e.tensor_tensor(out=out, in0=in0, in1=in1, op=op), reads=reads, writes=writes)

    def TS(out, in0, s1_, s2_, op0, op1, reads, writes):
        if op1 is None:
            assert op0 == ALU.pow and s1_ == -0.5
            S.op("act", lambda e: e.activation(out=out, in_=in0, func=AF.Sqrt), reads=reads, writes=writes)
            return S.op("dve", lambda e: e.reciprocal(out=out, in_=out), reads=writes, writes=writes)
        return S.op("dve", lambda e: e.tensor_scalar(out=out, in0=in0, scalar1=s1_, scalar2=s2_, op0=op0, op1=op1), reads=reads, writes=writes)

    def STT(out, in0, scalar, in1, op0, op1, reads, writes):
        return S.op("dve", lambda e: e.scalar_tensor_tensor(out=out, in0=in0, scalar=scalar, in1=in1, op0=op0, op1=op1), reads=reads, writes=writes)

    def MEMSET(ap, val, writes, eng="dve"):
        return S.op(eng, lambda e: e.memset(ap, val), writes=writes)

    class T:
        def __init__(self, t, name, excl=False):
            self.t = t
            self.b = Buf(name, excl)

    with contextlib.ExitStack() as top:
        uid = [0]

        def sbuf(st, name, shape, dt):
            uid[0] += 1
            name = f"{name}_{uid[0]}"
            return T(st.enter_context(nc.sbuf_tensor(name, list(shape), dt)), name)

        def psum(st, name, shape, dt):
            uid[0] += 1
            name = f"{name}_{uid[0]}"
            return T(st.enter_context(nc.psum_tensor(name, list(shape), dt)), name, excl=True)

        ident_f = sbuf(top, "ident_f", [128, 128], F32)
        ident_b = sbuf(top, "ident_b", [128, 128], BF16)
        LOAD(ident_f.t[:], ident_in, "c_ident", writes=[ident_f.b])
        COPY("dve", ident_b.t[:], ident_f.t[:], [ident_f.b], [ident_b.b])
        ones_f = sbuf(top, "ones_f", [128, 64], F32)
        MEMSET(ones_f.t[:], 1.0, [ones_f.b])
        selA = sbuf(top, "selA", [128, 8, 8], F32)
        selB = sbuf(top, "selB", [128, 8, 64], F32)
        LOAD(selA.t[:].rearrange("p h m -> p (h m)"), selA_in, "c_selA", writes=[selA.b])
        LOAD(selB.t[:].rearrange("p h m -> p (h m)"), selB_in, "c_selB", writes=[selB.b])

        def bcast_load(st, name, src, n):
            t = sbuf(st, name, [128, n], F32)
            LOAD(t.t[:], src.broadcast_to([128, n]), "c_" + name, writes=[t.b])
            return t

        def load_weight(st, dst, src, nk, ncols, stage):
            for k in range(nk):
                sg = stage[k % 2]
                LOAD(sg.t[:, 0:ncols], src[k * 128:(k + 1) * 128, :], "wst" + sg.b.name, writes=[sg.b])
                COPY("act" if k % 2 else "dve", dst.t[:, k, :], sg.t[:, 0:ncols], [sg.b], [dst.b])

        def norm_part(xg, gvec, hb, ss, rstd, junk, nsub):
            MEMSET(ss.t[:], 0.0, [ss.b])
            for s in range(nsub):
                ACT(junk.t[:], xg.t[:, s, :], AF.Square, [xg.b, ss.b], [junk.b, ss.b], accum=ss.t[:, s:s + 1])
            TS(rstd.t[:], ss.t[:], 1.0 / 1024, EPS, ALU.mult, ALU.add, [ss.b], [rstd.b])
            TS(rstd.t[:], rstd.t[:], -0.5, None, ALU.pow, None, [rstd.b], [rstd.b])
            for s in range(nsub):
                STT(hb.t[:, s, :], xg.t[:, s, :], rstd.t[:, s:s + 1], gvec.t[:], ALU.mult, ALU.mult,
                    [xg.b, rstd.b, gvec.b], [hb.b])

        def transpose_part(hb, hT, psT, nsub):
            for s in range(nsub):
                p = psT[s % 2]
                for kc in range(8):
                    TR(p.t[:, kc, :], hb.t[:, s, kc * 128:(kc + 1) * 128], ident_b.t[:], [hb.b, ident_b.b], [p.b])
                COPY("act" if s % 2 else "dve", hT.t[:, :, s * 128:(s + 1) * 128], p.t[:], [p.b], [hT.b])

        with contextlib.ExitStack() as ph:
            w0 = sbuf(ph, "w0", [128, 8, 3328], BF16)
            with contextlib.ExitStack() as wst:
                stage = [sbuf(wst, f"stg{i}", [128, 3328], F32) for i in range(2)]
                load_weight(wst, w0, w_in_e, 8, 3328, stage)
                S.barrier()
            gE = bcast_load(ph, "gE", norm_e, 1024)
            gq = bcast_load(ph, "gq", qnorm_b, 64)
            gk = bcast_load(ph, "gk", knorm_b, 64)
            xg = [sbuf(ph, f"xg{i}", [128, 4, 1024], F32) for i in range(2)]
            cst = [sbuf(ph, f"cst{i}", [128, 4, 128], F32) for i in range(2)]
            hb = sbuf(ph, "hb", [128, 4, 1024], BF16)
            hT = sbuf(ph, "hT", [128, 8, 512], BF16)
            junk = sbuf(ph, "junk", [128, 1024], BF16)
            ss = sbuf(ph, "ss", [128, 4], F32)
            rstd = sbuf(ph, "rstd", [128, 4], F32)
            fm_st = [sbuf(ph, f"fm_st{i}", [128, 8, 512], BF16) for i in range(2)]
            g_st = sbuf(ph, "g_st", [128, 8, 512], F32)
            va_st = sbuf(ph, "va_st", [128, 4, 8, 65], BF16)
            vb_st = sbuf(ph, "vb_st", [128, 4, 2, 65], BF16)
            sq = sbuf(ph, "sq", [128, 640], F32)
            ssq = sbuf(ph, "ssq", [128, 10], F32)
            qn = sbuf(ph, "qn", [128, 10, 64], F32)
            tA = sbuf(ph, "tA", [128, 10, 64], F32)
            tB = sbuf(ph, "tB", [128, 10, 64], F32)
            qrs = [sbuf(ph, f"qr{i}", [128, 10, 64], BF16) for i in range(2)]
            qbT_st = sbuf(ph, "qbT_st", [128, 4, 512], BF16)
            kbT_st = sbuf(ph, "kbT_st", [128, 512], BF16)
            psT = [psum(ph, f"psT{i}", [128, 8, 128], BF16) for i in range(2)]
            psF = [psum(ph, f"psF{i}", [128, 512], F32) for i in range(2)]
            psK = [psum(ph, f"psK{i}", [128, 1024], F32) for i in range(2)]
            MEMSET(va_st.t[:], 1.0, [va_st.b])
            MEMSET(vb_st.t[:], 1.0, [vb_st.b])

            def p0_loads(gi, src, seq, own_idx, halo_col):
                xb = xg[gi % 2]
                LOAD(xb.t[:], src.rearrange("(s p) d -> p s d", p=128), "xg" + xb.b.name, writes=[xb.b])
                if own_idx is not None:
                    t0 = own_idx * G
                    cb = cst[gi % 2]
                    LOAD(cb.t[:], cs0[t0:t0 + G, :].rearrange("(s p) c -> p s c", p=128), "cs" + cb.b.name, writes=[cb.b])

            def p0_group(gi, src, seq, own_idx, halo_col, mid_hook, late_hook):
                xb = xg[gi % 2]
                own = own_idx is not None
                if own:
                    t0 = own_idx * G
                    cb = cst[gi % 2]
                    loc = 512 + (own_idx % 8) * G
                else:
                    loc = halo_col
                if KSUB <= 0:
                    return
                if KSUB <= 1:
                    return
                fm = fm_st[gi % 2]
                chunks = ([0, 1, 2, 3] if own else []) + [4, 5, 6, 7]
                for j, cc in enumerate(chunks):
                    p = psF[j % 2]
                    for kc in range(8):
                        MM(p.t[:], w0.t[:, kc, cc * 128:(cc + 1) * 128], hT.t[:, kc, :], kc == 0, kc == 7, [w0.b, hT.b], [p.b])
                    COPY("act" if j % 2 else "dve", fm.t[:, cc, :], p.t[:], [p.b], [fm.b])
                if own:
                    STORE(qaT_d[:, t0:t0 + G].rearrange("(c p) t -> p c t", p=128), fm.t[:, 0:4, :], "st_qa", [fm.b], [db("qaT")])
                STORE(kaT_d[seq, :, loc:loc + G].rearrange("(c p) t -> p c t", p=128), fm.t[:, 4:8, :], "st_ka", [fm.b], [db("kaT")])
                if KSUB <= 2:
                    return
                if own:
                    for j in range(8):
                        cc = 18 + j
                        p = psF[j % 2]
                        for kc in range(8):
                            MM(p.t[:], w0.t[:, kc, cc * 128:(cc + 1) * 128], hT.t[:, kc, :], kc == 0, kc == 7, [w0.b, hT.b], [p.b])
                        ACT(g_st.t[:, j, :], p.t[:], AF.Silu, [p.b], [g_st.b])
                    STORE(gT_d[0][:, t0:t0 + G].rearrange("(c p) t -> p c t", p=128), g_st.t[:], "st_g0", [g_st.b], [db("gT0")])
                mid_hook()
                if KSUB <= 3:
                    return
                for s in range(4):
                    p = psF[s % 2]
                    for kc in range(8):
                        MM(p.t[:], hT.t[:, kc, s * 128:(s + 1) * 128], w0.t[:, kc, 1024:1536], kc == 0, kc == 7, [w0.b, hT.b], [p.b])
                    COPY("act", va_st.t[:, s, :, 0:64], p.t[:].rearrange("p (h d) -> p h d", d=64), [p.b], [va_st.b])
                STORE(va_d[seq, loc:loc + G, :].rearrange("(s p) c -> p s c", p=128), va_st.t[:].rearrange("p s h c -> p s (h c)"),
                      "st_va", [va_st.b], [db("va")])
                if not own or KSUB <= 4:
                    late_hook()
                    return
                def tm_mm(s):
                    p = psK[s % 2]
                    for kc in range(8):
                        MM(p.t[:, 0:512], hT.t[:, kc, s * 128:(s + 1) * 128], w0.t[:, kc, 1536:2048], kc == 0, kc == 7, [w0.b, hT.b], [p.b])
                    for kc in range(8):
                        MM(p.t[:, 512:768], hT.t[:, kc, s * 128:(s + 1) * 128], w0.t[:, kc, 2048:2304], kc == 0, kc == 7, [w0.b, hT.b], [p.b])

                def tm_chain(s):
                    p = psK[s % 2]
                    qr = qrs[s % 2]
                    COPY("act", vb_st.t[:, s, :, 0:64], p.t[:, 640:768].rearrange("p (h d) -> p h d", d=64), [p.b], [vb_st.b])
                    p3 = p.t[:, 0:640].rearrange("p (h d) -> p h d", d=64)
                    ACT(sq.t[:], p.t[:, 0:640], AF.Square, [p.b], [sq.b])
                    DVE(lambda e: e.reduce_sum(out=ssq.t[:], in_=sq.t[:].rearrange("p (h d) -> p h d", d=64), axis=AX.X), [sq.b], [ssq.b])
                    TS(ssq.t[:], ssq.t[:], 1.0 / 64, EPS, ALU.mult, ALU.add, [ssq.b], [ssq.b])
                    TS(ssq.t[:], ssq.t[:], -0.5, None, ALU.pow, None, [ssq.b], [ssq.b])
                    TT(qn.t[:], p3, ssq.t[:].unsqueeze(2).broadcast_to([128, 10, 64]), ALU.mult, [p.b, ssq.b], [qn.b])
                    TT(qn.t[:, 0:8, :], qn.t[:, 0:8, :], gq.t[:].unsqueeze(1).broadcast_to([128, 8, 64]), ALU.mult, [qn.b, gq.b], [qn.b])
                    TT(qn.t[:, 8:10, :], qn.t[:, 8:10, :], gk.t[:].unsqueeze(1).broadcast_to([128, 2, 64]), ALU.mult, [qn.b, gk.b], [qn.b])
                    cosb = cb.t[:, s, 0:64].unsqueeze(1).broadcast_to([128, 10, 64])
                    sinb = cb.t[:, s, 64:128].unsqueeze(1).broadcast_to([128, 10, 64])
                    TT(tA.t[:], qn.t[:], cosb, ALU.mult, [qn.b, cb.b], [tA.b])
                    TT(tB.t[:], qn.t[:], sinb, ALU.mult, [qn.b, cb.b], [tB.b])
                    TT(qr.t[:, :, 0:32], tA.t[:, :, 0:32], tB.t[:, :, 32:64], ALU.subtract, [tA.b, tB.b], [qr.b])
                    TT(qr.t[:, :, 32:64], tB.t[:, :, 0:32], tA.t[:, :, 32:64], ALU.add, [tA.b, tB.b], [qr.b])

                def tm_tr(s):
                    qr = qrs[s % 2]
                    pt = psT[s % 2]
                    qr2 = qr.t[:].rearrange("p h d -> p (h d)")
                    for c in range(5):
                        TR(pt.t[:, c, :], qr2[:, c * 128:(c + 1) * 128], ident_b.t[:], [qr.b, ident_b.b], [pt.b])
                    COPY("dve", qbT_st.t[:, :, s * 128:(s + 1) * 128], pt.t[:, 0:4, :], [pt.b], [qbT_st.b])
                    COPY("act", kbT_st.t[:, s * 128:(s + 1) * 128], pt.t[:, 4, :], [pt.b], [kbT_st.b])

                tm_mm(0)
                tm_chain(0)
                tm_mm(1)
                tm_chain(1)
                tm_mm(2)
                tm_tr(0)
                tm_chain(2)
                tm_mm(3)
                tm_tr(1)
                late_hook()
                tm_chain(3)

                def tail():
                    tm_tr(2)
                    tm_tr(3)
                    STORE(qbT_d[:, t0:t0 + G].rearrange("(c p) t -> p c t", p=128), qbT_st.t[:], "st_qb", [qbT_st.b], [db("qbT")])
                    tl = (own_idx % 8) * G
                    kdst = kbT_p if seq == 0 else kbT_c
                    STORE(kdst[:, tl:tl + G], kbT_st.t[:], "st_kb", [kbT_st.b], [db("kbT%d" % seq)])
                    if seq == 0:
                        STORE(vb_p[tl:tl + G, :].rearrange("(s p) c -> p s c", p=128), vb_st.t[:].rearrange("p s h c -> p s (h c)"),
                              "st_vb", [vb_st.b], [db("vb0")])
                    else:
                        for g2 in range(2):
                            STORE(vview(vb_c[g2])[tl:tl + G, :].rearrange("(s p) c -> p s c", p=128), vb_st.t[:, :, g2, :],
                                  "st_vb", [vb_st.b], [db("vb1")])
                return tail

            jobs = []
            for hgi in range(min(2, KNG)):
                jobs.append((x_halo[hgi * G:(hgi + 1) * G, :], 1, None, 0 if hgi == 0 else 4608))
            for g in range(8, min(16, 8 + KNG)):
                jobs.append((x_own[g * G:(g + 1) * G, :], 1, g, None))
            n_sample_jobs = len(jobs)
            for g in range(0, min(8, KNG)):
                jobs.append((x_own[g * G:(g + 1) * G, :], 0, g, None))
            p0_loads(0, *jobs[0])
            norm_part(xg[0], gE, hb, ss, rstd, junk, 4)
            transpose_part(hb, hT, psT, 4)
            for gi, job in enumerate(jobs):
                nxt = gi + 1 < len(jobs)
                if nxt:
                    p0_loads(gi + 1, *jobs[gi + 1])
                tail = p0_group(gi, *job, (lambda gi=gi: norm_part(xg[(gi + 1) % 2], gE, hb, ss, rstd, junk, 4)) if nxt else (lambda: None),
                                (lambda: transpose_part(hb, hT, psT, 4)) if nxt else (lambda: None))
                if tail is not None:
                    tail()
                if gi == n_sample_jobs - 1:
                    if "a" not in NOCC:
                        S.op("pool", lambda e: e.collective_compute("AllGather", ALU.bypass, replica_groups=GROUPS4, ins=[kbT_c], outs=[kbT_g]),
                             reads=[db("kbT1")], writes=[db("ccg0")], dma=True, stream="cc0", inc=1)
                    if "b" not in NOCC:
                        for g2 in range(2):
                            S.op("pool", (lambda s_, d_: (lambda e: e.collective_compute("AllGather", ALU.bypass, replica_groups=GROUPS4, ins=[s_], outs=[d_])))(vb_c[g2], vb_g[g2]),
                                 reads=[db("vb1")], writes=[db("ccg0")], dma=True, stream="cc0", inc=1)
            S.barrier()
        if STOP <= 0:
            S.emit()
            return nc

        with contextlib.ExitStack() as ph:
            kaT = sbuf(ph, "kaT", [128, 4, NLOC], BF16)
            va = sbuf(ph, "va", [128, 40, 520], BF16)
            qaT = sbuf(ph, "qaT", [128, 4, SEQ_OWN], BF16)
            bmi = sbuf(ph, "bmi", [128, 8 * 5 * 128], F32)
            bme = sbuf(ph, "bme", [128, 8 * 7 * 128], F32)
            s_sb = [sbuf(ph, f"s_sb{i}", [128, 896], F32) for i in range(3)]
            pT = [sbuf(ph, f"pT{i}", [128, 896], BF16) for i in range(4)]
            oA = [sbuf(ph, f"oA{i}", [65, 8, 128], F32) for i in range(2)]
            gtA = [sbuf(ph, f"gtA{i}", [64, 8, 128], F32) for i in range(2)]
            mgA = [sbuf(ph, f"mgA{i}", [64, 8, 128], BF16) for i in range(2)]
            psS = [psum(ph, f"psS{i}", [128, 1024], F32) for i in range(3)]
            psO = psum(ph, "psO", [128, 4, 128], F32)
            psB = psum(ph, "psB", [128, 4, 128], F32)
            rs8 = sbuf(ph, "rs8", [128, 128], F32)
            LOAD(bmi.t[:], bm_int, "bmi", writes=[bmi.b])
            it = 0
            pend = []
            deferred1 = []

            def tick1():
                for d in list(deferred1):
                    d[0] -= 1
                    if d[0] <= 0:
                        deferred1.remove(d)
                        d[1]()

            def flush_pend(keep=0):
                while len(pend) > keep:
                    pend.pop(0)()

            def mk_pv1(pt, nkt, kt0, h, qt, seq):
                def f():
                    hl, hf = h % 4, h // 4
                    for k in range(nkt):
                        MM(psO.t[0:65, hl, :], va.t[:, kt0 + k, h * 65:(h + 1) * 65], pt.t[:, k * 128:(k + 1) * 128],
                           k == 0, k == nkt - 1, [va.b, pt.b], [psO.b])
                    if hl == 3:
                        ob, gt, mgo = oA[qt % 2], gtA[qt % 2], mgA[qt % 2]
                        t0 = seq * SEQ_OWN + qt * 128
                        hs = slice(hf * 4, hf * 4 + 4)
                        if hf == 0:
                            LOAD(gt.t[:], gT_d[0][0:512, t0:t0 + 128].rearrange("(h d) t -> d h t", d=64), "ld_gt" + gt.b.name, [db("gT0")], [gt.b])
                        COPY("act", ob.t[:, hs, :], psO.t[0:65, :, :], [psO.b], [ob.b])
                        TT(ob.t[0:64, hs, :], ob.t[0:64, hs, :], gt.t[:, hs, :], ALU.mult, [ob.b, gt.b], [ob.b], eng="pool")

                        def post():
                            for hh in range(4):
                                MM(psB.t[64:72, 0, :], selA.t[64:65, hh, :], ob.t[64:65, hf * 4 + hh, :], hh == 0, hh == 3, [selA.b, ob.b], [psB.b])
                            DVE(lambda e: e.reciprocal(out=rs8.t[64:68, :], in_=psB.t[64:68, 0, :]), [psB.b], [rs8.b])
                            for hh in range(4):
                                MM(psB.t[0:64, hh, :], selB.t[64:68, hh, :], rs8.t[64:68, :], True, True, [selB.b, rs8.b], [psB.b])
                            TT(mgo.t[:, hs, :], ob.t[0:64, hs, :], psB.t[0:64, :, :], ALU.mult, [ob.b, psB.b], [mgo.b])
                            if hf == 1:
                                STORE(mgT_d[0][0:512, t0:t0 + 128].rearrange("(h d) t -> d h t", d=64), mgo.t[:], "st_oA", [mgo.b], [db("mgT0")])
                        deferred1.append([3, post])
                return f

            for seq in range(2):
                flush_pend()
                if seq == 0:
                    MEMSET(kaT.t[:], 0.0, [kaT.b])
                    MEMSET(va.t[:], 0.0, [va.b], eng="pool")
                    LOAD(kaT.t[:, :, 512:4608], kaT_d[0, :, 512:4608].rearrange("(c p) t -> p c t", p=128), "ld_ka", [db("kaT")], [kaT.b])
                    LOAD(va.t[:, 4:36, :], va_d[0, 512:4608, :].rearrange("(t p) c -> p t c", p=128), "ld_va", [db("va")], [va.b])
                else:
                    LOAD(kaT.t[:], kaT_d[1].rearrange("(c p) t -> p c t", p=128), "ld_ka", [db("kaT")], [kaT.b])
                    LOAD(va.t[:], va_d[1].rearrange("(t p) c -> p t c", p=128), "ld_va", [db("va")], [va.b])
                LOAD(qaT.t[:], qaT_d[:, seq * SEQ_OWN:(seq + 1) * SEQ_OWN].rearrange("(c p) t -> p c t", p=128), "ld_qa", [db("qaT")], [qaT.b])
                for qt in (list(range(32)) if KQT >= 32 else [0, 1, 5, 30, 31][:KQT]):
                    edge = qt < 2 or qt >= 30
                    if edge:
                        ei = seq * 4 + (qt if qt < 2 else qt - 28)
                        LOAD(bme.t[:], bm_edge[ei], "bme", writes=[bme.b])
                        nkt, kt0, bm = 7, qt + 1, bme
                    else:
                        nkt, kt0, bm = 5, qt + 2, bmi
                    n = nkt * 128
                    for h in range(8):
                        c, base = h // 2, (h % 2) * 64
                        ps = psS[it % 3]
                        ssb = s_sb[it % 3]
                        pt = pT[it % 4]
                        it += 1
                        for k in range(nkt):
                            MM(ps.t[:, k * 128:(k + 1) * 128], kaT.t[base:base + 64, c, (kt0 + k) * 128:(kt0 + k + 1) * 128],
                               qaT.t[base:base + 64, c, qt * 128:(qt + 1) * 128], True, True, [kaT.b, qaT.b], [ps.b])
                        STT(ssb.t[:, 0:n], ps.t[:, 0:n], 0.125, bm.t[:, h * n:(h + 1) * n], ALU.mult, ALU.add, [ps.b, bm.b], [ssb.b])
                        ACT(pt.t[:, 0:n], ssb.t[:, 0:n], AF.Exp, [ssb.b], [pt.b])
                        pend.append(mk_pv1(pt, nkt, kt0, h, qt, seq))
                        flush_pend(keep=2)
                        tick1()
            flush_pend()
            for d in list(deferred1):
                d[1]()
            S.barrier()
        if STOP <= 1:
            S.emit()
            return nc

        def dense_phase(layer, dk, scale, nkv, qper, k_loader, v_loader, q_loader, head_base, KG=3, rowpack=False, lag=1):
            with contextlib.ExitStack() as ph:
                kT = [sbuf(ph, f"kT{i}", [128, 16384], BF16) for i in range(2)]
                vv = [sbuf(ph, f"vv{i}", [128, 128, 80], BF16) for i in range(2)]
                qT = [sbuf(ph, f"qT{i}", [128, SEQ_OWN], BF16) for i in range(2)]
                pT = [sbuf(ph, f"pT{i}", [128, KG * 512], BF16) for i in range(lag + 2)]
                ost = [sbuf(ph, f"ost{i}", [65, 512], F32) for i in range(2)]
                gtD = [sbuf(ph, f"gtD{i}", [64, 512], F32) for i in range(2)]
                mgD = [sbuf(ph, f"mgD{i}", [64, 512], BF16) for i in range(2)]
                psS = [psum(ph, f"psS{i}", [128, KG * 512], F32) for i in range(lag + 1)]
                psO = [psum(ph, f"psO{i}", [128, 512], F32) for i in range(1)]
                psB = psum(ph, "psB", [128, 512], F32)
                it = 0
                qbi = 0
                pend = []

                def mk_pv(po, ob, gt, mgo, vb_, pt, k0, sz, nkt, seq, h, qb):
                    def f():
                        for k in range(sz):
                            kt = k0 + k
                            MM(po.t[0:65, :], vb_.t[:, kt, 0:65], pt.t[:, k * 512:(k + 1) * 512],
                               kt == 0, kt == nkt - 1, [vb_.b, pt.b], [po.b])
                        if k0 + sz == nkt:
                            t0 = seq * SEQ_OWN + qb * 512
                            f0 = (head_base + h) * 64
                            COPY("dve", ob.t[:], po.t[0:65, :], [po.b], [ob.b])
                            DVE(lambda e: e.reciprocal(out=ob.t[64:65, :], in_=ob.t[64:65, :]), [ob.b], [ob.b])
                            TT(ob.t[0:64, :], ob.t[0:64, :], gt.t[:], ALU.mult, [ob.b, gt.b], [ob.b])

                            def post():
                                MM(psB.t[0:64, :], ones_f.t[64:65, 0:64], ob.t[64:65, :], True, True, [ones_f.b, ob.b], [psB.b])
                                TT(mgo.t[:], ob.t[0:64, :], psB.t[0:64, :], ALU.mult, [ob.b, psB.b], [mgo.b])
                                STORE(mgT_d[layer][f0:f0 + 64, t0:t0 + 512], mgo.t[:], "st_oD", [mgo.b], [db("mgT%d" % layer)])
                            deferred.append([6, post])
                    return f

                deferred = []

                def tick():
                    for d in list(deferred):
                        d[0] -= 1
                        if d[0] <= 0:
                            deferred.remove(d)
                            d[1]()

                jobs = []
                kvi = 0
                for seq in range(2):
                    for g in range(nkv):
                        for hq in range(qper):
                            jobs.append((seq, g, hq, kvi))
                        kvi += 1

                def job_loads(j):
                    seq, g, hq, kvi_ = jobs[j]
                    if hq == 0:
                        k_loader(seq, g, kT[kvi_ % 2])
                        v_loader(seq, g, vv[kvi_ % 2])
                    q_loader(seq, g * qper + hq, qT[j % 2])

                job_loads(0)
                for j, (seq, g, hq, kvi_) in enumerate(jobs):
                    nk = SEQ_OWN if seq == 0 else 4 * SEQ_OWN
                    nkt = nk // 128
                    kb_, vb_, qb_ = kT[kvi_ % 2], vv[kvi_ % 2], qT[j % 2]
                    h = g * qper + hq
                    for qb in range(8):
                        if qb == 2 and j + 1 < len(jobs):
                            job_loads(j + 1)
                        po = psO[0]
                        ob, gt, mgo = ost[qbi % 2], gtD[qbi % 2], mgD[qbi % 2]
                        qbi += 1
                        t0 = seq * SEQ_OWN + qb * 512
                        f0 = (head_base + h) * 64
                        LOAD(gt.t[:], gT_d[layer][f0:f0 + 64, t0:t0 + 512], "ld_gt" + gt.b.name, [db("gT%d" % layer)], [gt.b])
                        for k0 in range(0, nkt, KG):
                            sz = min(KG, nkt - k0)
                            ps = psS[it % (lag + 1)]
                            pt = pT[it % (lag + 2)]
                            it += 1
                            for k in range(sz):
                                kt = k0 + k
                                r0 = (kt % 2) * 64 if rowpack else 0
                                MM(ps.t[:, k * 512:(k + 1) * 512], kb_.t[r0:r0 + dk, kt * 128:(kt + 1) * 128],
                                   qb_.t[r0:r0 + dk, qb * 512:(qb + 1) * 512], True, True, [kb_.b, qb_.b], [ps.b])
                            ACT(pt.t[:, 0:sz * 512], ps.t[:, 0:sz * 512], AF.Exp, [ps.b], [pt.b], scale=scale)
                            pend.append(mk_pv(po, ob, gt, mgo, vb_, pt, k0, sz, nkt, seq, h, qb))
                            while len(pend) > lag:
                                pend.pop(0)()
                            tick()
                while pend:
                    pend.pop(0)()
                for d in list(deferred):
                    d[1]()
                S.barrier()

        def k0_loader(seq, g, kb_):
            for r0 in (0, 64):
                if seq == 0:
                    LOAD(kb_.t[r0:r0 + 64, 0:SEQ_OWN], kbT_p[g * 64:(g + 1) * 64, :], "ld_k" + kb_.b.name, [db("kbT0")], [kb_.b])
                else:
                    LOAD(kb_.t[r0:r0 + 64, :].rearrange("p (r t) -> p r t", r=4),
                         kbT_g.rearrange("(r p) t -> p r t", p=128)[g * 64:(g + 1) * 64], "ld_k" + kb_.b.name, [db("ccg0")], [kb_.b])

        def v0_loader(seq, g, vb_):
            if seq == 0:
                LOAD(vb_.t[:, 0:32, 0:65], vb_p[:, g * 65:(g + 1) * 65].rearrange("(t p) c -> p t c", p=128), "ld_v" + vb_.b.name, [db("vb0")], [vb_.b])
            else:
                LOAD(vb_.t[:, :, 0:65], vview(vb_g[g]).rearrange("(t p) c -> p t c", p=128), "ld_v" + vb_.b.name, [db("ccg0")], [vb_.b])

        def q0_loader(seq, h, qb_):
            for r0 in (0, 64):
                LOAD(qb_.t[r0:r0 + 64, :], qbT_d[h * 64:(h + 1) * 64, seq * SEQ_OWN:(seq + 1) * SEQ_OWN], "ld_q" + qb_.b.name, [db("qbT")], [qb_.b])

        dense_phase(0, 64, 0.125, 2, 4, k0_loader, v0_loader, q0_loader, 8, rowpack=True)
        if STOP <= 2:
            S.emit()
            return nc

        def outproj_phase(layer, w_src, x_src, x_buf, final):
            with contextlib.ExitStack() as ph:
                wo = sbuf(ph, "wo", [128, 8, 1024], BF16)
                with contextlib.ExitStack() as wst:
                    stage = [sbuf(wst, f"stg{i}", [128, 1024], F32) for i in range(2)]
                    load_weight(wst, wo, w_src, 8, 1024, stage)
                    S.barrier()
                gF = bcast_load(ph, "gF", norm_f, 1024)
                mgs = [sbuf(ph, f"mg{i}", [128, 8, 512], BF16) for i in range(2)]
                xg = [sbuf(ph, f"xg{i}", [128, 4, 1024], F32) for i in range(2)]
                yo = [sbuf(ph, f"yo{i}", [128, 4, 1024], F32) for i in range(2)]
                junk = sbuf(ph, "junk", [128, 1024], BF16)
                ss = sbuf(ph, "ss", [128, 4], F32)
                rstd = sbuf(ph, "rstd", [128, 4], F32)
                psY = [psum(ph, f"psY{i}", [128, 512], F32) for i in range(6)]

                def loads(g):
                    t0 = g * G
                    LOAD(xg[g % 2].t[:], x_src[t0:t0 + G, :].rearrange("(s p) d -> p s d", p=128), "ld_x" + xg[g % 2].b.name, x_buf, [xg[g % 2].b])
                    LOAD(mgs[g % 2].t[:], mgT_d[layer][:, t0:t0 + G].rearrange("(c p) t -> p c t", p=128), "ld_mg" + mgs[g % 2].b.name,
                         [db("mgT%d" % layer)], [mgs[g % 2].b])

                loads(0)
                for g in range(16):
                    t0 = g * G
                    if g + 1 < 16:
                        loads(g + 1)
                    xb, mg = xg[g % 2], mgs[g % 2]
                    for s_ in range(4):
                        ps2 = [psY[(s_ * 2 + n) % 6] for n in range(2)]
                        for c in range(8):
                            for n in range(2):
                                MM(ps2[n].t[:], mg.t[:, c, s_ * 128:(s_ + 1) * 128], wo.t[:, c, n * 512:(n + 1) * 512], c == 0, c == 7, [mg.b, wo.b], [ps2[n].b])
                        for n in range(2):
                            TT(xb.t[:, s_, n * 512:(n + 1) * 512], ps2[n].t[:], xb.t[:, s_, n * 512:(n + 1) * 512], ALU.add, [ps2[n].b, xb.b], [xb.b])
                    if not final:
                        STORE(x1_d[t0:t0 + G, :].rearrange("(s p) d -> p s d", p=128), xb.t[:], "st_x1", [xb.b], [db("x1")])
                        continue
                    yb = yo[g % 2]
                    MEMSET(ss.t[:], 0.0, [ss.b])
                    for s_ in range(4):
                        ACT(junk.t[:], xb.t[:, s_, :], AF.Square, [xb.b, ss.b], [junk.b, ss.b], accum=ss.t[:, s_:s_ + 1])
                    TS(rstd.t[:], ss.t[:], 1.0 / 1024, EPS, ALU.mult, ALU.add, [ss.b], [rstd.b])
                    TS(rstd.t[:], rstd.t[:], -0.5, None, ALU.pow, None, [rstd.b], [rstd.b])
                    for s_ in range(4):
                        STT(yb.t[:, s_, :], xb.t[:, s_, :], rstd.t[:, s_:s_ + 1], gF.t[:], ALU.mult, ALU.mult, [xb.b, rstd.b, gF.b], [yb.b])
                    STORE(y_out[t0:t0 + G, :].rearrange("(s p) d -> p s d", p=128), yb.t[:], "st_y", [yb.b], [db("y")])
                S.barrier()

        outproj_phase(0, w_out_e, x_own, [], False)
        if STOP <= 3:
            S.emit()
            return nc

        with contextlib.ExitStack() as ph:
            w1 = sbuf(ph, "w1", [128, 8, 1728], BF16)
            wq = sbuf(ph, "wq", [128, 3, 3072], BF16)
            wkv = sbuf(ph, "wkv", [128, 2, 2048], BF16)
            with contextlib.ExitStack() as wst:
                stage = [sbuf(wst, f"stg{i}", [128, 3072], F32) for i in range(2)]
                load_weight(wst, w1, w1x, 8, 1728, stage)
                load_weight(wst, wq, wuq2, 3, 3072, stage)
                load_weight(wst, wkv, wukv2, 2, 2048, stage)
                S.barrier()
            gO = bcast_load(ph, "gO", norm_o, 1024)
            gql = bcast_load(ph, "gql", qlat_g, 384)
            gkl = bcast_load(ph, "gkl", kvlat_g, 256)
            x1s = [sbuf(ph, f"x1s{i}", [128, 4, 1024], F32) for i in range(2)]
            hb = sbuf(ph, "hb", [128, 4, 1024], BF16)
            hT = sbuf(ph, "hT", [128, 8, 512], BF16)
            junk = sbuf(ph, "junk", [128, 1024], BF16)
            ss = sbuf(ph, "ss", [128, 4], F32)
            rstd = sbuf(ph, "rstd", [128, 4], F32)
            ss2 = sbuf(ph, "ss2", [128, 2], F32)
            g_st = sbuf(ph, "g_st", [128, 8, 512], F32)
            lats = [sbuf(ph, f"lat{i}", [128, 640], BF16) for i in range(2)]
            latT = sbuf(ph, "latT", [128, 5, 512], BF16)
            c1ts = [sbuf(ph, f"c1t{i}", [96, 512], F32) for i in range(2)]
            s1ts = [sbuf(ph, f"s1t{i}", [96, 512], F32) for i in range(2)]
            ckts = [sbuf(ph, f"ckt{i}", [32, 512], F32) for i in range(2)]
            skts = [sbuf(ph, f"skt{i}", [32, 512], F32) for i in range(2)]
            ta = sbuf(ph, "ta", [96, 512], F32)
            tb = sbuf(ph, "tb", [96, 512], F32)
            q_st = [sbuf(ph, f"q_st{i}", [96, 512], BF16) for i in range(4)]
            kr_st = sbuf(ph, "kr_st", [32, 512], BF16)
            kn_sts = [sbuf(ph, f"kn_st{i}", [128, 8, 512], BF16) for i in range(2)]
            v_sts = [sbuf(ph, f"v_st{i}", [128, 4, 16, 65], BF16) for i in range(2)]
            psT = [psum(ph, f"psT{i}", [128, 8, 128], BF16) for i in range(2)]
            psF = [psum(ph, f"psF{i}", [128, 512], F32) for i in range(2)]
            psL = [psum(ph, f"psL{i}", [128, 1024], F32) for i in range(2)]
            psY = psF
            for v_st in v_sts:
                MEMSET(v_st.t[:], 1.0, [v_st.b])

            def p3_loads(gi, g):
                t0 = g * G
                x1 = x1s[gi % 2]
                LOAD(x1.t[:], x1_d[t0:t0 + G, :].rearrange("(s p) d -> p s d", p=128), "ld_x" + x1.b.name, [db("x1")], [x1.b], q="pool")
                LOAD(c1ts[gi % 2].t[:], c1[:, t0:t0 + G], "ld_c1%d" % (gi % 2), writes=[c1ts[gi % 2].b], q="pool")
                LOAD(s1ts[gi % 2].t[:], s1[:, t0:t0 + G], "ld_s1%d" % (gi % 2), writes=[s1ts[gi % 2].b], q="pool")
                LOAD(ckts[gi % 2].t[:], ck[:, t0:t0 + G], "ld_ck%d" % (gi % 2), writes=[ckts[gi % 2].b], q="pool")
                LOAD(skts[gi % 2].t[:], sk[:, t0:t0 + G], "ld_sk%d" % (gi % 2), writes=[skts[gi % 2].b], q="pool")

            def p3_group(gi, g, mid_hook):
                t0 = g * G
                seq = g // 8
                tl = (g % 8) * G
                x1 = x1s[gi % 2]
                c1t, s1t, ckt, skt = c1ts[gi % 2], s1ts[gi % 2], ckts[gi % 2], skts[gi % 2]
                for j in range(8):
                    p = psF[j % 2]
                    c0 = 672 + j * 128
                    for kc in range(8):
                        MM(p.t[:], w1.t[:, kc, c0:c0 + 128], hT.t[:, kc, :], kc == 0, kc == 7, [w1.b, hT.b], [p.b])
                    ACT(g_st.t[:, j, :], p.t[:], AF.Silu, [p.b], [g_st.b])
                STORE(gT_d[1][:, t0:t0 + G].rearrange("(c p) t -> p c t", p=128), g_st.t[:], "st_g1", [g_st.b], [db("gT1")])
                pa, pb = psF[0], psF[1]
                for kc in range(8):
                    MM(pa.t[0:32, :], w1.t[:, kc, 640:672], hT.t[:, kc, :], kc == 0, kc == 7, [w1.b, hT.b], [pa.b])
                for kc in range(8):
                    MM(pb.t[0:32, :], w1.t[:, kc, 1696:1728], hT.t[:, kc, :], kc == 0, kc == 7, [w1.b, hT.b], [pb.b])
                TT(ta.t[0:32, :], pa.t[0:32, :], ckt.t[:], ALU.mult, [pa.b, ckt.b], [ta.b])
                TT(tb.t[0:32, :], pb.t[0:32, :], skt.t[:], ALU.mult, [pb.b, skt.b], [tb.b])
                TT(kr_st.t[:], ta.t[0:32, :], tb.t[0:32, :], ALU.add, [ta.b, tb.b], [kr_st.b])
                STORE((k1r_p if seq == 0 else k1r_c)[:, tl:tl + G], kr_st.t[:], "st_kr", [kr_st.b], [db("k1r%d" % seq)])
                def lat_mm(s):
                    p = psL[s % 2]
                    for kc in range(8):
                        MM(p.t[:, 0:384], hT.t[:, kc, s * 128:(s + 1) * 128], w1.t[:, kc, 0:384], kc == 0, kc == 7, [w1.b, hT.b], [p.b])
                    for kc in range(8):
                        MM(p.t[:, 512:768], hT.t[:, kc, s * 128:(s + 1) * 128], w1.t[:, kc, 384:640], kc == 0, kc == 7, [w1.b, hT.b], [p.b])

                def lat_chain(s):
                    p = psL[s % 2]
                    lat = lats[s % 2]
                    MEMSET(ss2.t[:], 0.0, [ss2.b])
                    ACT(junk.t[:, 0:384], p.t[:, 0:384], AF.Square, [p.b, ss2.b], [junk.b, ss2.b], accum=ss2.t[:, 0:1])
                    ACT(junk.t[:, 384:640], p.t[:, 512:768], AF.Square, [p.b, ss2.b], [junk.b, ss2.b], accum=ss2.t[:, 1:2])
                    TS(ss2.t[:, 0:1], ss2.t[:, 0:1], 1.0 / 384, EPS, ALU.mult, ALU.add, [ss2.b], [ss2.b])
                    TS(ss2.t[:, 1:2], ss2.t[:, 1:2], 1.0 / 256, EPS, ALU.mult, ALU.add, [ss2.b], [ss2.b])
                    TS(ss2.t[:], ss2.t[:], -0.5, None, ALU.pow, None, [ss2.b], [ss2.b])
                    STT(lat.t[:, 0:384], p.t[:, 0:384], ss2.t[:, 0:1], gql.t[:], ALU.mult, ALU.mult, [p.b, ss2.b, gql.b], [lat.b])
                    STT(lat.t[:, 384:640], p.t[:, 512:768], ss2.t[:, 1:2], gkl.t[:], ALU.mult, ALU.mult, [p.b, ss2.b, gkl.b], [lat.b])

                def lat_tr(s):
                    lat = lats[s % 2]
                    pt = psT[s % 2]
                    for c in range(5):
                        TR(pt.t[:, c, :], lat.t[:, c * 128:(c + 1) * 128], ident_b.t[:], [lat.b, ident_b.b], [pt.b])
                    COPY("act", latT.t[:, :, s * 128:(s + 1) * 128], pt.t[:, 0:5, :], [pt.b], [latT.b])

                lat_mm(0)
                lat_chain(0)
                lat_mm(1)
                lat_chain(1)
                lat_mm(2)
                lat_tr(0)
                lat_chain(2)
                lat_mm(3)
                lat_tr(1)
                lat_chain(3)
                lat_tr(2)
                lat_tr(3)
                for h in range(16):
                    if h % 2 == 0:
                        pa_t, pb_t, pa_b, pb_b = psF[0].t[0:96, :], psF[1].t[0:96, :], psF[0].b, psF[1].b
                    else:
                        pa_t, pb_t, pa_b, pb_b = psL[0].t[0:96, 0:512], psL[1].t[0:96, 0:512], psL[0].b, psL[1].b
                    qs = q_st[h % 4]
                    for kc in range(3):
                        MM(pa_t, wq.t[:, kc, h * 96:(h + 1) * 96], latT.t[:, kc, :], kc == 0, kc == 2, [wq.b, latT.b], [pa_b])
                    for kc in range(3):
                        MM(pb_t, wq.t[:, kc, 1536 + h * 96:1536 + (h + 1) * 96], latT.t[:, kc, :], kc == 0, kc == 2, [wq.b, latT.b], [pb_b])
                    TT(ta.t[:], pa_t, c1t.t[:], ALU.mult, [pa_b, c1t.b], [ta.b])
                    TT(tb.t[:], pb_t, s1t.t[:], ALU.mult, [pb_b, s1t.b], [tb.b])
                    TT(qs.t[:], ta.t[:], tb.t[:], ALU.add, [ta.b, tb.b], [qs.b])
                    STORE(q1T_d[h, :, t0:t0 + G], qs.t[:], "st_q1", [qs.b], [db("q1T")])
                mid_hook()
                kn_st = kn_sts[gi % 2]
                v_st = v_sts[gi % 2]
                for c in range(8):
                    p = psY[c % 2]
                    for kc in range(2):
                        MM(p.t[:], wkv.t[:, kc, c * 128:(c + 1) * 128], latT.t[:, 3 + kc, :], kc == 0, kc == 1, [wkv.b, latT.b], [p.b])
                    COPY("act", kn_st.t[:, c, :], p.t[:], [p.b], [kn_st.b])
                if seq == 0:
                    STORE(k1n_p[:, tl:tl + G].rearrange("(c p) t -> p c t", p=128), kn_st.t[:], "st_kn", [kn_st.b], [db("k1n0")])
                else:
                    for c in range(8):
                        STORE(k1n_c[c][:, tl:tl + G], kn_st.t[:, c, :], "st_kn", [kn_st.b], [db("k1n1")])
                for s in range(4):
                    for n in range(2):
                        p = psY[(s * 2 + n) % 2]
                        for kc in range(2):
                            MM(p.t[:], latT.t[:, 3 + kc, s * 128:(s + 1) * 128], wkv.t[:, kc, 1024 + n * 512:1024 + (n + 1) * 512],
                               kc == 0, kc == 1, [wkv.b, latT.b], [p.b])
                        COPY("act", v_st.t[:, s, n * 8:(n + 1) * 8, 0:64], p.t[:].rearrange("p (h d) -> p h d", d=64), [p.b], [v_st.b])
                if seq == 0:
                    STORE(v1_p[tl:tl + G, :].rearrange("(s p) c -> p s c", p=128),
                          v_st.t[:].rearrange("p s h c -> p s (h c)"), "st_v1", [v_st.b], [db("v10")])
                else:
                    for h in range(16):
                        STORE(vview(v1_c[h])[tl:tl + G, :].rearrange("(s p) c -> p s c", p=128), v_st.t[:, :, h, :],
                              "st_v1", [v_st.b], [db("v11")])

            order = list(range(8, 16)) + list(range(0, 8))
            p3_loads(0, order[0])
            norm_part(x1s[0], gO, hb, ss, rstd, junk, 4)
            transpose_part(hb, hT, psT, 4)
            for gi, g in enumerate(order):
                nxt = gi + 1 < 16
                if nxt:
                    p3_loads(gi + 1, order[gi + 1])
                p3_group(gi, g, (lambda gi=gi: norm_part(x1s[(gi + 1) % 2], gO, hb, ss, rstd, junk, 4)) if nxt else (lambda: None))
                if nxt:
                    transpose_part(hb, hT, psT, 4)
            cc_list = [("k1r", k1r_c, k1r_g)] + [("k1n", k1n_c[i], k1n_g[i]) for i in range(8)] + [("v1", v1_c[h], v1_g[h]) for h in range(16)]
            for ci, (nm, src, dst) in enumerate(cc_list):
                S.op("pool", (lambda s_, d_: (lambda e: e.collective_compute("AllGather", ALU.bypass, replica_groups=GROUPS4, ins=[s_], outs=[d_])))(src, dst),
                     reads=[db(nm + "1")], writes=[db("ccg1")], dma=True, stream="cc1", inc=1)
            S.barrier()
        if STOP <= 4:
            S.emit()
            return nc

        def k1_loader(seq, h, kb_):
            if seq == 0:
                LOAD(kb_.t[0:64, 0:SEQ_OWN], k1n_p[h * 64:(h + 1) * 64, :], "ld_k" + kb_.b.name, [db("k1n0")], [kb_.b])
                LOAD(kb_.t[64:96, 0:SEQ_OWN], k1r_p[:, :], "ld_k" + kb_.b.name, [db("k1r0")], [kb_.b])
            else:
                LOAD(kb_.t[0:64, :].rearrange("p (r t) -> p r t", r=4),
                     k1n_g[h // 2].rearrange("(r f) t -> f r t", f=128)[(h % 2) * 64:(h % 2 + 1) * 64], "ld_k" + kb_.b.name, [db("ccg1")], [kb_.b])
                LOAD(kb_.t[64:96, :].rearrange("p (r t) -> p r t", r=4),
                     k1r_g.rearrange("(r f) t -> f r t", f=32), "ld_k" + kb_.b.name, [db("ccg1")], [kb_.b])

        def v1_loader(seq, h, vb_):
            if seq == 0:
                LOAD(vb_.t[:, 0:32, 0:65], v1_p[:, h * 65:(h + 1) * 65].rearrange("(t p) c -> p t c", p=128), "ld_v" + vb_.b.name, [db("v10")], [vb_.b])
            else:
                LOAD(vb_.t[:, :, 0:65], vview(v1_g[h]).rearrange("(t p) c -> p t c", p=128), "ld_v" + vb_.b.name, [db("ccg1")], [vb_.b])

        def q1_loader(seq, h, qb_):
            LOAD(qb_.t[0:96, :], q1T_d[h, :, seq * SEQ_OWN:(seq + 1) * SEQ_OWN], "ld_q" + qb_.b.name, [db("q1T")], [qb_.b])

        dense_phase(1, 96, 96 ** -0.5, 16, 1, k1_loader, v1_loader, q1_loader, 0, KG=2, lag=2)
        if STOP <= 5:
            S.emit()
            return nc

        outproj_phase(1, w_out_o, x1_d, [db("x1")], True)
        S.emit()
    return nc


def _na_table(r0, R, base_off, nkt, rpb):
    q = np.arange(128)
    qr = r0 + q // 64
    qc = q % 64
    k = np.arange(128)
    kr_in = k // 64
    kc = k % 64
    rs = np.clip(qr - 4, 0, R - 8)
    cs = np.clip(qc - 8, 0, 64 - 16)
    out = np.full((128, 8, nkt, 128), NEG, np.float32)
    for kt in range(nkt):
        kr = r0 + base_off + 2 * kt + kr_in
        valid = ((kr[:, None] >= rs[None, :]) & (kr[:, None] < rs[None, :] + 8) & (kr[:, None] >= 0) & (kr[:, None] < R)
                 & (kc[:, None] >= cs[None, :]) & (kc[:, None] < cs[None, :] + 16))
        ro = np.clip(kr[:, None] - qr[None, :] + 7, 0, 14)
        co = np.clip(kc[:, None] - qc[None, :] + 15, 0, 30)
        vals = rpb[:, ro, co]
        vals = np.where(valid[None], vals, np.float32(NEG))
        out[:, :, kt, :] = vals.transpose(1, 0, 2)
    return out


def _rope_tables(pos, rot_dim):
    nf = rot_dim // 4
    inv = (np.float32(10000.0) ** (-np.arange(nf, dtype=np.float32) / np.float32(nf))).astype(np.float32)
    row = (pos // 64).astype(np.float32)
    col = (pos % 64).astype(np.float32)
    ang = np.concatenate([row[:, None] * inv[None], col[:, None] * inv[None]], axis=-1).astype(np.float32)
    return np.cos(ang).astype(np.float32), np.sin(ang).astype(np.float32)


_PROG = None


def kernel(x_prompt, x_sample, norm_e, w_in_e, rpb_a, qnorm_b, knorm_b, w_out_e,
           norm_o, w_in_o, qlat_g, kvlat_g, w_uq, w_ukv, w_out_o, norm_f):
    global _PROG
    f = lambda a: np.ascontiguousarray(np.asarray(a, dtype=np.float32))
    x_prompt, x_sample = f(x_prompt), f(x_sample)
    w_in_e0, w_out_e0, w_in_o0, w_uq0, w_ukv0, w_out_o0 = f(w_in_e)[0], f(w_out_e)[0], f(w_in_o)[0], f(w_uq)[0], f(w_ukv)[0], f(w_out_o)[0]
    rpb = f(rpb_a)[0]
    kr = w_in_o0[:, 640:672]
    w1x = np.concatenate([w_in_o0, kr[:, 16:32], kr[:, 0:16]], axis=1)
    uq = w_uq0.reshape(384, 16, 96)
    uq_sw = np.concatenate([uq[:, :, 0:64], uq[:, :, 80:96], uq[:, :, 64:80]], axis=2)
    wuq2 = np.concatenate([uq.reshape(384, 1536), uq_sw.reshape(384, 1536)], axis=1)
    ukv = w_ukv0.reshape(256, 16, 128)
    wukv2 = np.concatenate([ukv[:, :, 0:64].reshape(256, 1024), ukv[:, :, 64:128].reshape(256, 1024)], axis=1)
    bm_int = _na_table(8, 64, -4, 5, rpb).reshape(128, -1)
    ident = np.eye(128, dtype=np.float32)
    selA_h = np.tile(np.eye(8, dtype=np.float32).reshape(1, 64), (128, 1))
    selB_h = np.zeros((128, 8, 64), np.float32)
    for hh in range(8):
        selB_h[64 + hh, hh, :] = 1.0
    selB_h = selB_h.reshape(128, 512)
    common = dict(w_in_e=w_in_e0, w_out_e=w_out_e0, w1x=f(w1x), wuq2=f(wuq2), wukv2=f(wukv2), w_out_o=w_out_o0,
                  norm_e=f(norm_e).reshape(1, 1024), norm_o=f(norm_o).reshape(1, 1024), norm_f=f(norm_f).reshape(1, 1024),
                  qnorm_b=f(qnorm_b).reshape(1, 64), knorm_b=f(knorm_b).reshape(1, 64),
                  qlat_g=f(qlat_g).reshape(1, 384), kvlat_g=f(kvlat_g).reshape(1, 256),
                  bm_int=f(bm_int), ident=ident, selA=selA_h, selB=selB_h)
    edge_p = [_na_table(r0, 64, -6, 7, rpb).reshape(128, -1) for r0 in (0, 2, 60, 62)]
    in_maps = []
    for c in range(8):
        sq, j = c // 4, c % 4
        xs = x_sample[sq]
        x_own = np.concatenate([x_prompt[c], xs[j * 4096:(j + 1) * 4096]], axis=0)
        halo = np.zeros((1024, 1024), np.float32)
        if j > 0:
            halo[0:512] = xs[j * 4096 - 512:j * 4096]
        if j < 3:
            halo[512:1024] = xs[(j + 1) * 4096:(j + 1) * 4096 + 512]
        pos = np.concatenate([np.arange(4096), j * 4096 + np.arange(4096)])
        cos0, sin0 = _rope_tables(pos, 64)
        cs0 = np.concatenate([cos0, cos0, sin0, sin0], axis=1)
        cos1, sin1 = _rope_tables(pos, 32)
        c1 = np.ones((96, 8192), np.float32)
        s1 = np.zeros((96, 8192), np.float32)
        c1[64:80] = cos1.T
        c1[80:96] = cos1.T
        s1[64:80] = -sin1.T
        s1[80:96] = sin1.T
        edge_s = [_na_table(64 * j + r0, 256, -6, 7, rpb).reshape(128, -1) for r0 in (0, 2, 60, 62)]
        m = dict(common)
        m.update(x_own=f(x_own), x_halo=halo, cs0=f(cs0), c1=c1, s1=s1, ck=f(c1[64:96]), sk=f(s1[64:96]),
                 bm_edge=f(np.stack(edge_p + edge_s, axis=0)))
        in_maps.append(m)
    if _PROG is None:
        _PROG = build_program()
    res = run_bass_kernel_spmd(_PROG, in_maps, core_ids=list(range(8)))
    if DEBUG:
        kernel.last = res.results
    y_prompt = np.stack([np.asarray(res.results[c]["y_out"])[0:4096] for c in range(8)], axis=0)
    y_sample = np.stack([np.concatenate([np.asarray(res.results[sq * 4 + j]["y_out"])[4096:8192] for j in range(4)], axis=0)
                         for sq in range(2)], axis=0)
    return (y_prompt.astype(np.float32), y_sample.astype(np.float32))
```

```python
import contextlib
import os
import numpy as np
import concourse.bass as bass
import concourse.mybir as mybir
from concourse.bass_utils import run_bass_kernel_spmd

F32 = mybir.dt.float32
BF16 = mybir.dt.bfloat16
AF = mybir.ActivationFunctionType
ALU = mybir.AluOpType
AX = mybir.AxisListType

ENGS = ("pe", "act", "dve", "pool", "sp")
MAXV = 8000
EPS = 1e-6
NEG = -30000.0
DEBUG = bool(int(os.environ.get("KDEBUG", "0")))
STOP = int(os.environ.get("KSTOP", "99"))
NOCC = os.environ.get("KNOCC", "")
KSUB = int(os.environ.get("KSUB", "99"))
KNG = int(os.environ.get("KNG", "99"))
KP1 = int(os.environ.get("KP1", "99"))
KQT = int(os.environ.get("KQT", "32"))


class Buf:
    __slots__ = ("name", "writers", "readers", "excl")

    def __init__(self, name="", excl=False):
        self.name = name
        self.writers = []
        self.readers = []
        self.excl = excl


class Op:
    __slots__ = ("eng", "fn", "deps", "dma", "stream", "idx", "signal", "sem", "val", "inc")


class Sched:
    def __init__(self, nc):
        self.nc = nc
        self.ops = {e: [] for e in ENGS}
        self.streams = {}
        self.bar = []

    def op(self, eng, fn, reads=(), writes=(), dma=False, stream=None, inc=None):
        o = Op()
        o.eng, o.fn, o.dma, o.stream = eng, fn, dma, stream
        o.signal = dma
        o.sem, o.val = None, 0
        o.inc = inc if inc is not None else (16 if dma else 1)
        deps = list(self.bar)
        if any(b.excl for b in reads):
            writes = list(writes) + [b for b in reads if b.excl and b not in writes]
            reads = [b for b in reads if not b.excl]
        for b in reads:
            deps.extend(b.writers)
        for b in writes:
            deps.extend(b.writers)
            deps.extend(b.readers)
        best = {}
        for d in deps:
            if (not d.dma) and d.eng == "pe" and eng == "pe" and not dma:
                continue
            key = ("s", d.stream) if d.dma else ("e", d.eng)
            if key not in best or best[key].idx < d.idx:
                best[key] = d
        o.deps = list(best.values())
        for d in o.deps:
            d.signal = True
        if dma:
            lst = self.streams.setdefault(stream, [])
            o.idx = len(lst)
            lst.append(o)
        else:
            o.idx = len(self.ops[eng])
        self.ops[eng].append(o)
        for b in reads:
            b.readers.append(o)
        for b in writes:
            b.writers = [o]
            b.readers = []
        return o

    def barrier(self):
        bar = []
        for e in ENGS:
            comp = [o for o in self.ops[e] if not o.dma]
            if comp:
                bar.append(comp[-1])
        for sname, lst in self.streams.items():
            if lst and not str(sname).startswith("cc"):
                bar.append(lst[-1])
        self.bar = bar

    def emit(self, final_eng="sp"):
        nc = self.nc
        semreq = []

        def newsem(name):
            semreq.append(name)
            return len(semreq) - 1

        for e in ENGS:
            cur, cnt = None, 0
            for o in self.ops[e]:
                if o.dma or not o.signal:
                    continue
                if cur is None or cnt >= MAXV:
                    cur, cnt = newsem(f"m{e}{len(semreq)}"), 0
                cnt += 1
                o.sem, o.val = cur, cnt
        for sname, lst in self.streams.items():
            cur, cnt = None, 0
            for o in lst:
                if cur is None or cnt >= MAXV * 16:
                    cur, cnt = newsem(f"d{len(semreq)}"), 0
                cnt += o.inc
                o.sem, o.val = cur, cnt
        print("semaphores requested:", len(semreq), flush=True)
        with contextlib.ExitStack() as st:
            sems = [st.enter_context(nc.semaphore(n)) for n in semreq]
            block = st.enter_context(nc.Block())
            handles = {"pe": "tensor", "act": "scalar", "dve": "vector", "pool": "gpsimd", "sp": "sync"}
            finals = [lst[-1] for lst in self.streams.values() if lst]

            def make(e):
                def body(eh):
                    waited = {}
                    for o in self.ops[e]:
                        for d in o.deps:
                            if waited.get(d.sem, 0) < d.val:
                                eh.wait_ge(sems[d.sem], d.val)
                                waited[d.sem] = d.val
                        ins = o.fn(eh)
                        if o.signal:
                            ins.then_inc(sems[o.sem], o.inc)
                    if e == final_eng:
                        for d in finals:
                            if waited.get(d.sem, 0) < d.val:
                                eh.wait_ge(sems[d.sem], d.val)
                                waited[d.sem] = d.val
                return body

            for e in ENGS:
                getattr(block, handles[e])(make(e))


NTOK = 8192
SEQ_OWN = 4096
NLOC = 5120
G = 512


def build_program():
    nc = bass.Bass("TRN2", target_bir_lowering=False)
    S = Sched(nc)
    scratch_kind = "ExternalOutput" if DEBUG else "Internal"

    def din(name, shape, dt=F32):
        return nc.dram_tensor(name, list(shape), dt, kind="ExternalInput").ap()

    def dscr(name, shape, dt, dbg=True):
        return nc.dram_tensor(name, list(shape), dt, kind=(scratch_kind if dbg else "Internal")).ap()

    x_own = din("x_own", [NTOK, 1024])
    x_halo = din("x_halo", [1024, 1024])
    w_in_e = din("w_in_e", [1024, 3328])
    w_out_e = din("w_out_e", [1024, 1024])
    w1x = din("w1x", [1024, 1728])
    wuq2 = din("wuq2", [384, 3072])
    wukv2 = din("wukv2", [256, 2048])
    w_out_o = din("w_out_o", [1024, 1024])
    norm_e = din("norm_e", [1, 1024])
    norm_o = din("norm_o", [1, 1024])
    norm_f = din("norm_f", [1, 1024])
    qnorm_b = din("qnorm_b", [1, 64])
    knorm_b = din("knorm_b", [1, 64])
    qlat_g = din("qlat_g", [1, 384])
    kvlat_g = din("kvlat_g", [1, 256])
    cs0 = din("cs0", [NTOK, 128])
    c1 = din("c1", [96, NTOK])
    s1 = din("s1", [96, NTOK])
    ck = din("ck", [32, NTOK])
    sk = din("sk", [32, NTOK])
    bm_int = din("bm_int", [128, 8 * 5 * 128])
    bm_edge = din("bm_edge", [8, 128, 8 * 7 * 128])
    ident_in = din("ident", [128, 128])
    selA_in = din("selA", [128, 64])
    selB_in = din("selB", [128, 512])
    y_out = nc.dram_tensor("y_out", [NTOK, 1024], F32, kind="ExternalOutput").ap()

    qaT_d = dscr("qaT_d", [512, NTOK], BF16)
    kaT_d = dscr("kaT_d", [2, 512, NLOC], BF16)
    va_d = dscr("va_d", [2, NLOC, 520], BF16)
    qbT_d = dscr("qbT_d", [512, NTOK], BF16)
    kbT_p = dscr("kbT_p", [128, SEQ_OWN], BF16)
    vb_p = dscr("vb_p", [SEQ_OWN, 130], BF16)
    kbT_c = dscr("kbT_c", [128, SEQ_OWN], BF16, dbg=False)
    vb_c = [dscr(f"vb_c{g}", [128, 2080], BF16, dbg=False) for g in range(2)]
    kbT_g = dscr("kbT_g", [512, SEQ_OWN], BF16, dbg=False)
    vb_g = [dscr(f"vb_g{g}", [512, 2080], BF16, dbg=False) for g in range(2)]
    gT_d = [dscr(f"gT{l}_d", [1024, NTOK], F32) for l in range(2)]
    mgT_d = [dscr(f"mgT{l}_d", [1024, NTOK], BF16) for l in range(2)]
    x1_d = dscr("x1_d", [NTOK, 1024], F32)
    q1T_d = dscr("q1T_d", [16, 96, NTOK], BF16)
    k1n_p = dscr("k1n_p", [1024, SEQ_OWN], BF16)
    k1r_p = dscr("k1r_p", [32, SEQ_OWN], BF16)
    v1_p = dscr("v1_p", [SEQ_OWN, 1040], BF16)
    k1n_c = [dscr(f"k1n_c{i}", [128, SEQ_OWN], BF16, dbg=False) for i in range(8)]
    k1r_c = dscr("k1r_c", [32, SEQ_OWN], BF16, dbg=False)
    v1_c = [dscr(f"v1_c{h}", [128, 2080], BF16, dbg=False) for h in range(16)]
    k1n_g = [dscr(f"k1n_g{i}", [512, SEQ_OWN], BF16, dbg=False) for i in range(8)]
    k1r_g = dscr("k1r_g", [128, SEQ_OWN], BF16, dbg=False)
    v1_g = [dscr(f"v1_g{h}", [512, 2080], BF16, dbg=False) for h in range(16)]

    def vview(ap):
        return ap.rearrange("p (a c) -> (p a) c", c=65)

    D = {}

    def db(name):
        if name not in D:
            D[name] = Buf(name)
        return D[name]

    GROUPS4 = [[0, 1, 2, 3], [4, 5, 6, 7]]

    def LOAD(out, in_, stream, reads=(), writes=(), q="sp"):
        return S.op(q, lambda e: e.dma_start(out=out, in_=in_), reads=reads, writes=writes, dma=True, stream=stream)

    def STORE(out, in_, stream, reads=(), writes=()):
        return S.op("sp", lambda e: e.dma_start(out=out, in_=in_), reads=reads, writes=writes, dma=True, stream=stream)

    def MM(out, lhsT, rhs, start, stop, reads, writes):
        return S.op("pe", lambda e: e.matmul(out, lhsT=lhsT, rhs=rhs, start=start, stop=stop), reads=reads, writes=writes)

    def TR(out, in_, ident, reads, writes):
        return S.op("pe", lambda e: e.transpose(out=out, in_=in_, identity=ident), reads=reads, writes=writes)

    def ACT(out, in_, func, reads, writes, scale=1.0, accum=None):
        if accum is None:
            return S.op("act", lambda e: e.activation(out=out, in_=in_, func=func, scale=scale), reads=reads, writes=writes)
        return S.op("act", lambda e: e.activation(out=out, in_=in_, func=func, scale=scale, accum_out=accum), reads=reads, writes=writes)

    def DVE(fn, reads, writes):
        return S.op("dve", fn, reads=reads, writes=writes)

    def COPY(eng, out, in_, reads, writes):
        if eng == "act":
            return S.op("act", lambda e: e.copy(out=out, in_=in_), reads=reads, writes=writes)
        return S.op(eng, lambda e: e.tensor_copy(out=out, in_=in_), reads=reads, writes=writes)

    def TT(out, in0, in1, op, reads, writes, eng="dve"):
        return S.op(eng, lambda e: e.tensor_tensor(out=out, in0=in0, in1=in1, op=op), reads=reads, writes=writes)

    def TS(out, in0, s1_, s2_, op0, op1, reads, writes):
        if op1 is None:
            assert op0 == ALU.pow and s1_ == -0.5
            S.op("act", lambda e: e.activation(out=out, in_=in0, func=AF.Sqrt), reads=reads, writes=writes)
            return S.op("dve", lambda e: e.reciprocal(out=out, in_=out), reads=writes, writes=writes)
        return S.op("dve", lambda e: e.tensor_scalar(out=out, in0=in0, scalar1=s1_, scalar2=s2_, op0=op0, op1=op1), reads=reads, writes=writes)

    def STT(out, in0, scalar, in1, op0, op1, reads, writes):
        return S.op("dve", lambda e: e.scalar_tensor_tensor(out=out, in0=in0, scalar=scalar, in1=in1, op0=op0, op1=op1), reads=reads, writes=writes)

    def MEMSET(ap, val, writes, eng="dve"):
        return S.op(eng, lambda e: e.memset(ap, val), writes=writes)

    class T:
        def __init__(self, t, name, excl=False):
            self.t = t
            self.b = Buf(name, excl)

    with contextlib.ExitStack() as top:
        uid = [0]

        def sbuf(st, name, shape, dt):
            uid[0] += 1
            name = f"{name}_{uid[0]}"
            return T(st.enter_context(nc.sbuf_tensor(name, list(shape), dt)), name)

        def psum(st, name, shape, dt):
            uid[0] += 1
            name = f"{name}_{uid[0]}"
            return T(st.enter_context(nc.psum_tensor(name, list(shape), dt)), name, excl=True)

        ident_f = sbuf(top, "ident_f", [128, 128], F32)
        ident_b = sbuf(top, "ident_b", [128, 128], BF16)
        LOAD(ident_f.t[:], ident_in, "c_ident", writes=[ident_f.b])
        COPY("dve", ident_b.t[:], ident_f.t[:], [ident_f.b], [ident_b.b])
        ones_f = sbuf(top, "ones_f", [128, 64], F32)
        MEMSET(ones_f.t[:], 1.0, [ones_f.b])
        selA = sbuf(top, "selA", [128, 8, 8], F32)
        selB = sbuf(top, "selB", [128, 8, 64], F32)
        LOAD(selA.t[:].rearrange("p h m -> p (h m)"), selA_in, "c_selA", writes=[selA.b])
        LOAD(selB.t[:].rearrange("p h m -> p (h m)"), selB_in, "c_selB", writes=[selB.b])

        def bcast_load(st, name, src, n):
            t = sbuf(st, name, [128, n], F32)
            LOAD(t.t[:], src.broadcast_to([128, n]), "c_" + name, writes=[t.b])
            return t

        def load_weight(st, dst, src, nk, ncols, stage):
            for k in range(nk):
                sg = stage[k % 2]
                LOAD(sg.t[:, 0:ncols], src[k * 128:(k + 1) * 128, :], "wst" + sg.b.name, writes=[sg.b])
                COPY("act" if k % 2 else "dve", dst.t[:, k, :], sg.t[:, 0:ncols], [sg.b], [dst.b])

        def norm_part(xg, gvec, hb, ss, rstd, junk, nsub):
            MEMSET(ss.t[:], 0.0, [ss.b])
            for s in range(nsub):
                ACT(junk.t[:], xg.t[:, s, :], AF.Square, [xg.b, ss.b], [junk.b, ss.b], accum=ss.t[:, s:s + 1])
            TS(rstd.t[:], ss.t[:], 1.0 / 1024, EPS, ALU.mult, ALU.add, [ss.b], [rstd.b])
            TS(rstd.t[:], rstd.t[:], -0.5, None, ALU.pow, None, [rstd.b], [rstd.b])
            for s in range(nsub):
                STT(hb.t[:, s, :], xg.t[:, s, :], rstd.t[:, s:s + 1], gvec.t[:], ALU.mult, ALU.mult,
                    [xg.b, rstd.b, gvec.b], [hb.b])

        def transpose_part(hb, hT, psT, nsub):
            for s in range(nsub):
                p = psT[s % 2]
                for kc in range(8):
                    TR(p.t[:, kc, :], hb.t[:, s, kc * 128:(kc + 1) * 128], ident_b.t[:], [hb.b, ident_b.b], [p.b])
                COPY("act" if s % 2 else "dve", hT.t[:, :, s * 128:(s + 1) * 128], p.t[:], [p.b], [hT.b])

        with contextlib.ExitStack() as ph:
            w0 = sbuf(ph, "w0", [128, 8, 3328], BF16)
            with contextlib.ExitStack() as wst:
                stage = [sbuf(wst, f"stg{i}", [128, 3328], F32) for i in range(2)]
                load_weight(wst, w0, w_in_e, 8, 3328, stage)
                S.barrier()
            gE = bcast_load(ph, "gE", norm_e, 1024)
            gq = bcast_load(ph, "gq", qnorm_b, 64)
            gk = bcast_load(ph, "gk", knorm_b, 64)
            xg = [sbuf(ph, f"xg{i}", [128, 4, 1024], F32) for i in range(2)]
            cst = [sbuf(ph, f"cst{i}", [128, 4, 128], F32) for i in range(2)]
            hb = sbuf(ph, "hb", [128, 4, 1024], BF16)
            hT = sbuf(ph, "hT", [128, 8, 512], BF16)
            junk = sbuf(ph, "junk", [128, 1024], BF16)
            ss = sbuf(ph, "ss", [128, 4], F32)
            rstd = sbuf(ph, "rstd", [128, 4], F32)
            fm_st = [sbuf(ph, f"fm_st{i}", [128, 8, 512], BF16) for i in range(2)]
            g_st = sbuf(ph, "g_st", [128, 8, 512], F32)
            va_st = sbuf(ph, "va_st", [128, 4, 8, 65], BF16)
            vb_st = sbuf(ph, "vb_st", [128, 4, 2, 65], BF16)
            sq = sbuf(ph, "sq", [128, 640], F32)
            ssq = sbuf(ph, "ssq", [128, 10], F32)
            qn = sbuf(ph, "qn", [128, 10, 64], F32)
            tA = sbuf(ph, "tA", [128, 10, 64], F32)
            tB = sbuf(ph, "tB", [128, 10, 64], F32)
            qrs = [sbuf(ph, f"qr{i}", [128, 10, 64], BF16) for i in range(2)]
            qbT_st = sbuf(ph, "qbT_st", [128, 4, 512], BF16)
            kbT_st = sbuf(ph, "kbT_st", [128, 512], BF16)
            psT = [psum(ph, f"psT{i}", [128, 8, 128], BF16) for i in range(2)]
            psF = [psum(ph, f"psF{i}", [128, 512], F32) for i in range(2)]
            psK = [psum(ph, f"psK{i}", [128, 1024], F32) for i in range(2)]
            MEMSET(va_st.t[:], 1.0, [va_st.b])
            MEMSET(vb_st.t[:], 1.0, [vb_st.b])

            def p0_loads(gi, src, seq, own_idx, halo_col):
                xb = xg[gi % 2]
                LOAD(xb.t[:], src.rearrange("(s p) d -> p s d", p=128), "xg" + xb.b.name, writes=[xb.b])
                if own_idx is not None:
                    t0 = own_idx * G
                    cb = cst[gi % 2]
                    LOAD(cb.t[:], cs0[t0:t0 + G, :].rearrange("(s p) c -> p s c", p=128), "cs" + cb.b.name, writes=[cb.b])

            def p0_group(gi, src, seq, own_idx, halo_col, mid_hook, late_hook):
                xb = xg[gi % 2]
                own = own_idx is not None
                if own:
                    t0 = own_idx * G
                    cb = cst[gi % 2]
                    loc = 512 + (own_idx % 8) * G
                else:
                    loc = halo_col
                if KSUB <= 0:
                    return
                if KSUB <= 1:
                    return
                fm = fm_st[gi % 2]
                chunks = ([0, 1, 2, 3] if own else []) + [4, 5, 6, 7]
                for j, cc in enumerate(chunks):
                    p = psF[j % 2]
                    for kc in range(8):
                        MM(p.t[:], w0.t[:, kc, cc * 128:(cc + 1) * 128], hT.t[:, kc, :], kc == 0, kc == 7, [w0.b, hT.b], [p.b])
                    COPY("act" if j % 2 else "dve", fm.t[:, cc, :], p.t[:], [p.b], [fm.b])
                if own:
                    STORE(qaT_d[:, t0:t0 + G].rearrange("(c p) t -> p c t", p=128), fm.t[:, 0:4, :], "st_qa", [fm.b], [db("qaT")])
                STORE(kaT_d[seq, :, loc:loc + G].rearrange("(c p) t -> p c t", p=128), fm.t[:, 4:8, :], "st_ka", [fm.b], [db("kaT")])
                if KSUB <= 2:
                    return
                if own:
                    for j in range(8):
                        cc = 18 + j
                        p = psF[j % 2]
                        for kc in range(8):
                            MM(p.t[:], w0.t[:, kc, cc * 128:(cc + 1) * 128], hT.t[:, kc, :], kc == 0, kc == 7, [w0.b, hT.b], [p.b])
                        ACT(g_st.t[:, j, :], p.t[:], AF.Silu, [p.b], [g_st.b])
                    STORE(gT_d[0][:, t0:t0 + G].rearrange("(c p) t -> p c t", p=128), g_st.t[:], "st_g0", [g_st.b], [db("gT0")])
                mid_hook()
                if KSUB <= 3:
                    return
                for s in range(4):
                    p = psF[s % 2]
                    for kc in range(8):
                        MM(p.t[:], hT.t[:, kc, s * 128:(s + 1) * 128], w0.t[:, kc, 1024:1536], kc == 0, kc == 7, [w0.b, hT.b], [p.b])
                    COPY("act", va_st.t[:, s, :, 0:64], p.t[:].rearrange("p (h d) -> p h d", d=64), [p.b], [va_st.b])
                STORE(va_d[seq, loc:loc + G, :].rearrange("(s p) c -> p s c", p=128), va_st.t[:].rearrange("p s h c -> p s (h c)"),
                      "st_va", [va_st.b], [db("va")])
                if not own or KSUB <= 4:
                    late_hook()
                    return
                def tm_mm(s):
                    p = psK[s % 2]
                    for kc in range(8):
                        MM(p.t[:, 0:512], hT.t[:, kc, s * 128:(s + 1) * 128], w0.t[:, kc, 1536:2048], kc == 0, kc == 7, [w0.b, hT.b], [p.b])
                    for kc in range(8):
                        MM(p.t[:, 512:768], hT.t[:, kc, s * 128:(s + 1) * 128], w0.t[:, kc, 2048:2304], kc == 0, kc == 7, [w0.b, hT.b], [p.b])

                def tm_chain(s):
                    p = psK[s % 2]
                    qr = qrs[s % 2]
                    COPY("act", vb_st.t[:, s, :, 0:64], p.t[:, 640:768].rearrange("p (h d) -> p h d", d=64), [p.b], [vb_st.b])
                    p3 = p.t[:, 0:640].rearrange("p (h d) -> p h d", d=64)
                    ACT(sq.t[:], p.t[:, 0:640], AF.Square, [p.b], [sq.b])
                    DVE(lambda e: e.reduce_sum(out=ssq.t[:], in_=sq.t[:].rearrange("p (h d) -> p h d", d=64), axis=AX.X), [sq.b], [ssq.b])
                    TS(ssq.t[:], ssq.t[:], 1.0 / 64, EPS, ALU.mult, ALU.add, [ssq.b], [ssq.b])
                    TS(ssq.t[:], ssq.t[:], -0.5, None, ALU.pow, None, [ssq.b], [ssq.b])
                    TT(qn.t[:], p3, ssq.t[:].unsqueeze(2).broadcast_to([128, 10, 64]), ALU.mult, [p.b, ssq.b], [qn.b])
                    TT(qn.t[:, 0:8, :], qn.t[:, 0:8, :], gq.t[:].unsqueeze(1).broadcast_to([128, 8, 64]), ALU.mult, [qn.b, gq.b], [qn.b])
                    TT(qn.t[:, 8:10, :], qn.t[:, 8:10, :], gk.t[:].unsqueeze(1).broadcast_to([128, 2, 64]), ALU.mult, [qn.b, gk.b], [qn.b])
                    cosb = cb.t[:, s, 0:64].unsqueeze(1).broadcast_to([128, 10, 64])
                    sinb = cb.t[:, s, 64:128].unsqueeze(1).broadcast_to([128, 10, 64])
                    TT(tA.t[:], qn.t[:], cosb, ALU.mult, [qn.b, cb.b], [tA.b])
                    TT(tB.t[:], qn.t[:], sinb, ALU.mult, [qn.b, cb.b], [tB.b])
                    TT(qr.t[:, :, 0:32], tA.t[:, :, 0:32], tB.t[:, :, 32:64], ALU.subtract, [tA.b, tB.b], [qr.b])
                    TT(qr.t[:, :, 32:64], tB.t[:, :, 0:32], tA.t[:, :, 32:64], ALU.add, [tA.b, tB.b], [qr.b])

                def tm_tr(s):
                    qr = qrs[s % 2]
                    pt = psT[s % 2]
                    qr2 = qr.t[:].rearrange("p h d -> p (h d)")
                    for c in range(5):
                        TR(pt.t[:, c, :], qr2[:, c * 128:(c + 1) * 128], ident_b.t[:], [qr.b, ident_b.b], [pt.b])
                    COPY("dve", qbT_st.t[:, :, s * 128:(s + 1) * 128], pt.t[:, 0:4, :], [pt.b], [qbT_st.b])
                    COPY("act", kbT_st.t[:, s * 128:(s + 1) * 128], pt.t[:, 4, :], [pt.b], [kbT_st.b])

                tm_mm(0)
                tm_chain(0)
                tm_mm(1)
                tm_chain(1)
                tm_mm(2)
                tm_tr(0)
                tm_chain(2)
                tm_mm(3)
                tm_tr(1)
                late_hook()
                tm_chain(3)

                def tail():
                    tm_tr(2)
                    tm_tr(3)
                    STORE(qbT_d[:, t0:t0 + G].rearrange("(c p) t -> p c t", p=128), qbT_st.t[:], "st_qb", [qbT_st.b], [db("qbT")])
                    tl = (own_idx % 8) * G
                    kdst = kbT_p if seq == 0 else kbT_c
                    STORE(kdst[:, tl:tl + G], kbT_st.t[:], "st_kb", [kbT_st.b], [db("kbT%d" % seq)])
                    if seq == 0:
                        STORE(vb_p[tl:tl + G, :].rearrange("(s p) c -> p s c", p=128), vb_st.t[:].rearrange("p s h c -> p s (h c)"),
                              "st_vb", [vb_st.b], [db("vb0")])
                    else:
                        for g2 in range(2):
                            STORE(vview(vb_c[g2])[tl:tl + G, :].rearrange("(s p) c -> p s c", p=128), vb_st.t[:, :, g2, :],
                                  "st_vb", [vb_st.b], [db("vb1")])
                return tail

            jobs = []
            for hgi in range(min(2, KNG)):
                jobs.append((x_halo[hgi * G:(hgi + 1) * G, :], 1, None, 0 if hgi == 0 else 4608))
            for g in range(8, min(16, 8 + KNG)):
                jobs.append((x_own[g * G:(g + 1) * G, :], 1, g, None))
            n_sample_jobs = len(jobs)
            for g in range(0, min(8, KNG)):
                jobs.append((x_own[g * G:(g + 1) * G, :], 0, g, None))
            p0_loads(0, *jobs[0])
            norm_part(xg[0], gE, hb, ss, rstd, junk, 4)
            transpose_part(hb, hT, psT, 4)
            for gi, job in enumerate(jobs):
                nxt = gi + 1 < len(jobs)
                if nxt:
                    p0_loads(gi + 1, *jobs[gi + 1])
                tail = p0_group(gi, *job, (lambda gi=gi: norm_part(xg[(gi + 1) % 2], gE, hb, ss, rstd, junk, 4)) if nxt else (lambda: None),
                                (lambda: transpose_part(hb, hT, psT, 4)) if nxt else (lambda: None))
                if tail is not None:
                    tail()
                if gi == n_sample_jobs - 1:
                    if "a" not in NOCC:
                        S.op("pool", lambda e: e.collective_compute("AllGather", ALU.bypass, replica_groups=GROUPS4, ins=[kbT_c], outs=[kbT_g]),
                             reads=[db("kbT1")], writes=[db("ccg0")], dma=True, stream="cc0", inc=1)
                    if "b" not in NOCC:
                        for g2 in range(2):
                            S.op("pool", (lambda s_, d_: (lambda e: e.collective_compute("AllGather", ALU.bypass, replica_groups=GROUPS4, ins=[s_], outs=[d_])))(vb_c[g2], vb_g[g2]),
                                 reads=[db("vb1")], writes=[db("ccg0")], dma=True, stream="cc0", inc=1)
            S.barrier()
        if STOP <= 0:
            S.emit()
            return nc

        with contextlib.ExitStack() as ph:
            kaT = sbuf(ph, "kaT", [128, 4, NLOC], BF16)
            va = sbuf(ph, "va", [128, 40, 520], BF16)
            qaT = sbuf(ph, "qaT", [128, 4, SEQ_OWN], BF16)
            bmi = sbuf(ph, "bmi", [128, 8 * 5 * 128], F32)
            bme = sbuf(ph, "bme", [128, 8 * 7 * 128], F32)
            s_sb = [sbuf(ph, f"s_sb{i}", [128, 896], F32) for i in range(3)]
            pT = [sbuf(ph, f"pT{i}", [128, 896], BF16) for i in range(4)]
            oA = [sbuf(ph, f"oA{i}", [65, 8, 128], F32) for i in range(2)]
            gtA = [sbuf(ph, f"gtA{i}", [64, 8, 128], F32) for i in range(2)]
            mgA = [sbuf(ph, f"mgA{i}", [64, 8, 128], BF16) for i in range(2)]
            psS = [psum(ph, f"psS{i}", [128, 1024], F32) for i in range(3)]
            psO = psum(ph, "psO", [128, 4, 128], F32)
            psB = psum(ph, "psB", [128, 4, 128], F32)
            rs8 = sbuf(ph, "rs8", [128, 128], F32)
            LOAD(bmi.t[:], bm_int, "bmi", writes=[bmi.b])
            it = 0
            pend = []
            deferred1 = []

            def tick1():
                for d in list(deferred1):
                    d[0] -= 1
                    if d[0] <= 0:
                        deferred1.remove(d)
                        d[1]()

            def flush_pend(keep=0):
                while len(pend) > keep:
                    pend.pop(0)()

            def mk_pv1(pt, nkt, kt0, h, qt, seq):
                def f():
                    hl, hf = h % 4, h // 4
                    for k in range(nkt):
                        MM(psO.t[0:65, hl, :], va.t[:, kt0 + k, h * 65:(h + 1) * 65], pt.t[:, k * 128:(k + 1) * 128],
                           k == 0, k == nkt - 1, [va.b, pt.b], [psO.b])
                    if hl == 3:
                        ob, gt, mgo = oA[qt % 2], gtA[qt % 2], mgA[qt % 2]
                        t0 = seq * SEQ_OWN + qt * 128
                        hs = slice(hf * 4, hf * 4 + 4)
                        if hf == 0:
                            LOAD(gt.t[:], gT_d[0][0:512, t0:t0 + 128].rearrange("(h d) t -> d h t", d=64), "ld_gt" + gt.b.name, [db("gT0")], [gt.b])
                        COPY("act", ob.t[:, hs, :], psO.t[0:65, :, :], [psO.b], [ob.b])
                        TT(ob.t[0:64, hs, :], ob.t[0:64, hs, :], gt.t[:, hs, :], ALU.mult, [ob.b, gt.b], [ob.b], eng="pool")

                        def post():
                            for hh in range(4):
                                MM(psB.t[64:72, 0, :], selA.t[64:65, hh, :], ob.t[64:65, hf * 4 + hh, :], hh == 0, hh == 3, [selA.b, ob.b], [psB.b])
                            DVE(lambda e: e.reciprocal(out=rs8.t[64:68, :], in_=psB.t[64:68, 0, :]), [psB.b], [rs8.b])
                            for hh in range(4):
                                MM(psB.t[0:64, hh, :], selB.t[64:68, hh, :], rs8.t[64:68, :], True, True, [selB.b, rs8.b], [psB.b])
                            TT(mgo.t[:, hs, :], ob.t[0:64, hs, :], psB.t[0:64, :, :], ALU.mult, [ob.b, psB.b], [mgo.b])
                            if hf == 1:
                                STORE(mgT_d[0][0:512, t0:t0 + 128].rearrange("(h d) t -> d h t", d=64), mgo.t[:], "st_oA", [mgo.b], [db("mgT0")])
                        deferred1.append([3, post])
                return f

            for seq in range(2):
                flush_pend()
                if seq == 0:
                    MEMSET(kaT.t[:], 0.0, [kaT.b])
                    MEMSET(va.t[:], 0.0, [va.b], eng="pool")
                    LOAD(kaT.t[:, :, 512:4608], kaT_d[0, :, 512:4608].rearrange("(c p) t -> p c t", p=128), "ld_ka", [db("kaT")], [kaT.b])
                    LOAD(va.t[:, 4:36, :], va_d[0, 512:4608, :].rearrange("(t p) c -> p t c", p=128), "ld_va", [db("va")], [va.b])
                else:
                    LOAD(kaT.t[:], kaT_d[1].rearrange("(c p) t -> p c t", p=128), "ld_ka", [db("kaT")], [kaT.b])
                    LOAD(va.t[:], va_d[1].rearrange("(t p) c -> p t c", p=128), "ld_va", [db("va")], [va.b])
                LOAD(qaT.t[:], qaT_d[:, seq * SEQ_OWN:(seq + 1) * SEQ_OWN].rearrange("(c p) t -> p c t", p=128), "ld_qa", [db("qaT")], [qaT.b])
                for qt in (list(range(32)) if KQT >= 32 else [0, 1, 5, 30, 31][:KQT]):
                    edge = qt < 2 or qt >= 30
                    if edge:
                        ei = seq * 4 + (qt if qt < 2 else qt - 28)
                        LOAD(bme.t[:], bm_edge[ei], "bme", writes=[bme.b])
                        nkt, kt0, bm = 7, qt + 1, bme
                    else:
                        nkt, kt0, bm = 5, qt + 2, bmi
                    n = nkt * 128
                    for h in range(8):
                        c, base = h // 2, (h % 2) * 64
                        ps = psS[it % 3]
                        ssb = s_sb[it % 3]
                        pt = pT[it % 4]
                        it += 1
                        for k in range(nkt):
                            MM(ps.t[:, k * 128:(k + 1) * 128], kaT.t[base:base + 64, c, (kt0 + k) * 128:(kt0 + k + 1) * 128],
                               qaT.t[base:base + 64, c, qt * 128:(qt + 1) * 128], True, True, [kaT.b, qaT.b], [ps.b])
                        STT(ssb.t[:, 0:n], ps.t[:, 0:n], 0.125, bm.t[:, h * n:(h + 1) * n], ALU.mult, ALU.add, [ps.b, bm.b], [ssb.b])
                        ACT(pt.t[:, 0:n], ssb.t[:, 0:n], AF.Exp, [ssb.b], [pt.b])
                        pend.append(mk_pv1(pt, nkt, kt0, h, qt, seq))
                        flush_pend(keep=2)
                        tick1()
            flush_pend()
            for d in list(deferred1):
                d[1]()
            S.barrier()
        if STOP <= 1:
            S.emit()
            return nc

        def dense_phase(layer, dk, scale, nkv, qper, k_loader, v_loader, q_loader, head_base, KG=3, rowpack=False, lag=1):
            with contextlib.ExitStack() as ph:
                kT = [sbuf(ph, f"kT{i}", [128, 16384], BF16) for i in range(2)]
                vv = [sbuf(ph, f"vv{i}", [128, 128, 80], BF16) for i in range(2)]
                qT = [sbuf(ph, f"qT{i}", [128, SEQ_OWN], BF16) for i in range(2)]
                pT = [sbuf(ph, f"pT{i}", [128, KG * 512], BF16) for i in range(lag + 2)]
                ost = [sbuf(ph, f"ost{i}", [65, 512], F32) for i in range(2)]
                gtD = [sbuf(ph, f"gtD{i}", [64, 512], F32) for i in range(2)]
                mgD = [sbuf(ph, f"mgD{i}", [64, 512], BF16) for i in range(2)]
                psS = [psum(ph, f"psS{i}", [128, KG * 512], F32) for i in range(lag + 1)]
                psO = [psum(ph, f"psO{i}", [128, 512], F32) for i in range(1)]
                psB = psum(ph, "psB", [128, 512], F32)
                it = 0
                qbi = 0
                pend = []

                def mk_pv(po, ob, gt, mgo, vb_, pt, k0, sz, nkt, seq, h, qb):
                    def f():
                        for k in range(sz):
                            kt = k0 + k
                            MM(po.t[0:65, :], vb_.t[:, kt, 0:65], pt.t[:, k * 512:(k + 1) * 512],
                               kt == 0, kt == nkt - 1, [vb_.b, pt.b], [po.b])
                        if k0 + sz == nkt:
                            t0 = seq * SEQ_OWN + qb * 512
                            f0 = (head_base + h) * 64
                            COPY("dve", ob.t[:], po.t[0:65, :], [po.b], [ob.b])
                            DVE(lambda e: e.reciprocal(out=ob.t[64:65, :], in_=ob.t[64:65, :]), [ob.b], [ob.b])
                            TT(ob.t[0:64, :], ob.t[0:64, :], gt.t[:], ALU.mult, [ob.b, gt.b], [ob.b])

                            def post():
                                MM(psB.t[0:64, :], ones_f.t[64:65, 0:64], ob.t[64:65, :], True, True, [ones_f.b, ob.b], [psB.b])
                                TT(mgo.t[:], ob.t[0:64, :], psB.t[0:64, :], ALU.mult, [ob.b, psB.b], [mgo.b])
                                STORE(mgT_d[layer][f0:f0 + 64, t0:t0 + 512], mgo.t[:], "st_oD", [mgo.b], [db("mgT%d" % layer)])
                            deferred.append([6, post])
                    return f

                deferred = []

                def tick():
                    for d in list(deferred):
                        d[0] -= 1
                        if d[0] <= 0:
                            deferred.remove(d)
                            d[1]()

                jobs = []
                kvi = 0
                for seq in range(2):
                    for g in range(nkv):
                        for hq in range(qper):
                            jobs.append((seq, g, hq, kvi))
                        kvi += 1

                def job_loads(j):
                    seq, g, hq, kvi_ = jobs[j]
                    if hq == 0:
                        k_loader(seq, g, kT[kvi_ % 2])
                        v_loader(seq, g, vv[kvi_ % 2])
                    q_loader(seq, g * qper + hq, qT[j % 2])

                job_loads(0)
                for j, (seq, g, hq, kvi_) in enumerate(jobs):
                    nk = SEQ_OWN if seq == 0 else 4 * SEQ_OWN
                    nkt = nk // 128
                    kb_, vb_, qb_ = kT[kvi_ % 2], vv[kvi_ % 2], qT[j % 2]
                    h = g * qper + hq
                    for qb in range(8):
                        if qb == 2 and j + 1 < len(jobs):
                            job_loads(j + 1)
                        po = psO[0]
                        ob, gt, mgo = ost[qbi % 2], gtD[qbi % 2], mgD[qbi % 2]
                        qbi += 1
                        t0 = seq * SEQ_OWN + qb * 512
                        f0 = (head_base + h) * 64
                        LOAD(gt.t[:], gT_d[layer][f0:f0 + 64, t0:t0 + 512], "ld_gt" + gt.b.name, [db("gT%d" % layer)], [gt.b])
                        for k0 in range(0, nkt, KG):
                            sz = min(KG, nkt - k0)
                            ps = psS[it % (lag + 1)]
                            pt = pT[it % (lag + 2)]
                            it += 1
                            for k in range(sz):
                                kt = k0 + k
                                r0 = (kt % 2) * 64 if rowpack else 0
                                MM(ps.t[:, k * 512:(k + 1) * 512], kb_.t[r0:r0 + dk, kt * 128:(kt + 1) * 128],
                                   qb_.t[r0:r0 + dk, qb * 512:(qb + 1) * 512], True, True, [kb_.b, qb_.b], [ps.b])
                            ACT(pt.t[:, 0:sz * 512], ps.t[:, 0:sz * 512], AF.Exp, [ps.b], [pt.b], scale=scale)
                            pend.append(mk_pv(po, ob, gt, mgo, vb_, pt, k0, sz, nkt, seq, h, qb))
                            while len(pend) > lag:
                                pend.pop(0)()
                            tick()
                while pend:
                    pend.pop(0)()
                for d in list(deferred):
                    d[1]()
                S.barrier()

        def k0_loader(seq, g, kb_):
            for r0 in (0, 64):
                if seq == 0:
                    LOAD(kb_.t[r0:r0 + 64, 0:SEQ_OWN], kbT_p[g * 64:(g + 1) * 64, :], "ld_k" + kb_.b.name, [db("kbT0")], [kb_.b])
                else:
                    LOAD(kb_.t[r0:r0 + 64, :].rearrange("p (r t) -> p r t", r=4),
                         kbT_g.rearrange("(r p) t -> p r t", p=128)[g * 64:(g + 1) * 64], "ld_k" + kb_.b.name, [db("ccg0")], [kb_.b])

        def v0_loader(seq, g, vb_):
            if seq == 0:
                LOAD(vb_.t[:, 0:32, 0:65], vb_p[:, g * 65:(g + 1) * 65].rearrange("(t p) c -> p t c", p=128), "ld_v" + vb_.b.name, [db("vb0")], [vb_.b])
            else:
                LOAD(vb_.t[:, :, 0:65], vview(vb_g[g]).rearrange("(t p) c -> p t c", p=128), "ld_v" + vb_.b.name, [db("ccg0")], [vb_.b])

        def q0_loader(seq, h, qb_):
            for r0 in (0, 64):
                LOAD(qb_.t[r0:r0 + 64, :], qbT_d[h * 64:(h + 1) * 64, seq * SEQ_OWN:(seq + 1) * SEQ_OWN], "ld_q" + qb_.b.name, [db("qbT")], [qb_.b])

        dense_phase(0, 64, 0.125, 2, 4, k0_loader, v0_loader, q0_loader, 8, KG=2, rowpack=True, lag=2)
        if STOP <= 2:
            S.emit()
            return nc

        def outproj_phase(layer, w_src, x_src, x_buf, final):
            with contextlib.ExitStack() as ph:
                wo = sbuf(ph, "wo", [128, 8, 1024], BF16)
                with contextlib.ExitStack() as wst:
                    stage = [sbuf(wst, f"stg{i}", [128, 1024], F32) for i in range(2)]
                    load_weight(wst, wo, w_src, 8, 1024, stage)
                    S.barrier()
                gF = bcast_load(ph, "gF", norm_f, 1024)
                mgs = [sbuf(ph, f"mg{i}", [128, 8, 512], BF16) for i in range(2)]
                xg = [sbuf(ph, f"xg{i}", [128, 4, 1024], F32) for i in range(2)]
                yo = [sbuf(ph, f"yo{i}", [128, 4, 1024], F32) for i in range(2)]
                junk = sbuf(ph, "junk", [128, 1024], BF16)
                ss = sbuf(ph, "ss", [128, 4], F32)
                rstd = sbuf(ph, "rstd", [128, 4], F32)
                psY = [psum(ph, f"psY{i}", [128, 512], F32) for i in range(6)]

                def loads(g):
                    t0 = g * G
                    LOAD(xg[g % 2].t[:], x_src[t0:t0 + G, :].rearrange("(s p) d -> p s d", p=128), "ld_x" + xg[g % 2].b.name, x_buf, [xg[g % 2].b])
                    LOAD(mgs[g % 2].t[:], mgT_d[layer][:, t0:t0 + G].rearrange("(c p) t -> p c t", p=128), "ld_mg" + mgs[g % 2].b.name,
                         [db("mgT%d" % layer)], [mgs[g % 2].b])

                loads(0)
                for g in range(16):
                    t0 = g * G
                    if g + 1 < 16:
                        loads(g + 1)
                    xb, mg = xg[g % 2], mgs[g % 2]
                    for s_ in range(4):
                        ps2 = [psY[(s_ * 2 + n) % 6] for n in range(2)]
                        for c in range(8):
                            for n in range(2):
                                MM(ps2[n].t[:], mg.t[:, c, s_ * 128:(s_ + 1) * 128], wo.t[:, c, n * 512:(n + 1) * 512], c == 0, c == 7, [mg.b, wo.b], [ps2[n].b])
                        for n in range(2):
                            TT(xb.t[:, s_, n * 512:(n + 1) * 512], ps2[n].t[:], xb.t[:, s_, n * 512:(n + 1) * 512], ALU.add, [ps2[n].b, xb.b], [xb.b])
                    if not final:
                        STORE(x1_d[t0:t0 + G, :].rearrange("(s p) d -> p s d", p=128), xb.t[:], "st_x1", [xb.b], [db("x1")])
                        continue
                    yb = yo[g % 2]
                    MEMSET(ss.t[:], 0.0, [ss.b])
                    for s_ in range(4):
                        ACT(junk.t[:], xb.t[:, s_, :], AF.Square, [xb.b, ss.b], [junk.b, ss.b], accum=ss.t[:, s_:s_ + 1])
                    TS(rstd.t[:], ss.t[:], 1.0 / 1024, EPS, ALU.mult, ALU.add, [ss.b], [rstd.b])
                    TS(rstd.t[:], rstd.t[:], -0.5, None, ALU.pow, None, [rstd.b], [rstd.b])
                    for s_ in range(4):
                        STT(yb.t[:, s_, :], xb.t[:, s_, :], rstd.t[:, s_:s_ + 1], gF.t[:], ALU.mult, ALU.mult, [xb.b, rstd.b, gF.b], [yb.b])
                    STORE(y_out[t0:t0 + G, :].rearrange("(s p) d -> p s d", p=128), yb.t[:], "st_y", [yb.b], [db("y")])
                S.barrier()

        outproj_phase(0, w_out_e, x_own, [], False)
        if STOP <= 3:
            S.emit()
            return nc

        with contextlib.ExitStack() as ph:
            w1 = sbuf(ph, "w1", [128, 8, 1728], BF16)
            wq = sbuf(ph, "wq", [128, 3, 3072], BF16)
            wkv = sbuf(ph, "wkv", [128, 2, 2048], BF16)
            with contextlib.ExitStack() as wst:
                stage = [sbuf(wst, f"stg{i}", [128, 3072], F32) for i in range(2)]
                load_weight(wst, w1, w1x, 8, 1728, stage)
                load_weight(wst, wq, wuq2, 3, 3072, stage)
                load_weight(wst, wkv, wukv2, 2, 2048, stage)
                S.barrier()
            gO = bcast_load(ph, "gO", norm_o, 1024)
            gql = bcast_load(ph, "gql", qlat_g, 384)
            gkl = bcast_load(ph, "gkl", kvlat_g, 256)
            x1s = [sbuf(ph, f"x1s{i}", [128, 4, 1024], F32) for i in range(2)]
            hb = sbuf(ph, "hb", [128, 4, 1024], BF16)
            hT = sbuf(ph, "hT", [128, 8, 512], BF16)
            junk = sbuf(ph, "junk", [128, 1024], BF16)
            ss = sbuf(ph, "ss", [128, 4], F32)
            rstd = sbuf(ph, "rstd", [128, 4], F32)
            ss2 = sbuf(ph, "ss2", [128, 2], F32)
            g_st = sbuf(ph, "g_st", [128, 8, 512], F32)
            lats = [sbuf(ph, f"lat{i}", [128, 640], BF16) for i in range(2)]
            latT = sbuf(ph, "latT", [128, 5, 512], BF16)
            c1ts = [sbuf(ph, f"c1t{i}", [96, 512], F32) for i in range(2)]
            s1ts = [sbuf(ph, f"s1t{i}", [96, 512], F32) for i in range(2)]
            ckts = [sbuf(ph, f"ckt{i}", [32, 512], F32) for i in range(2)]
            skts = [sbuf(ph, f"skt{i}", [32, 512], F32) for i in range(2)]
            ta = sbuf(ph, "ta", [96, 512], F32)
            tb = sbuf(ph, "tb", [96, 512], F32)
            q_st = [sbuf(ph, f"q_st{i}", [96, 512], BF16) for i in range(4)]
            kr_st = sbuf(ph, "kr_st", [32, 512], BF16)
            kn_sts = [sbuf(ph, f"kn_st{i}", [128, 8, 512], BF16) for i in range(2)]
            v_sts = [sbuf(ph, f"v_st{i}", [128, 4, 16, 65], BF16) for i in range(2)]
            psT = [psum(ph, f"psT{i}", [128, 8, 128], BF16) for i in range(2)]
            psF = [psum(ph, f"psF{i}", [128, 512], F32) for i in range(2)]
            psL = [psum(ph, f"psL{i}", [128, 1024], F32) for i in range(2)]
            psY = psF
            for v_st in v_sts:
                MEMSET(v_st.t[:], 1.0, [v_st.b])

            def p3_loads(gi, g):
                t0 = g * G
                x1 = x1s[gi % 2]
                LOAD(x1.t[:], x1_d[t0:t0 + G, :].rearrange("(s p) d -> p s d", p=128), "ld_x" + x1.b.name, [db("x1")], [x1.b], q="pool")
                LOAD(c1ts[gi % 2].t[:], c1[:, t0:t0 + G], "ld_c1%d" % (gi % 2), writes=[c1ts[gi % 2].b], q="pool")
                LOAD(s1ts[gi % 2].t[:], s1[:, t0:t0 + G], "ld_s1%d" % (gi % 2), writes=[s1ts[gi % 2].b], q="pool")
                LOAD(ckts[gi % 2].t[:], ck[:, t0:t0 + G], "ld_ck%d" % (gi % 2), writes=[ckts[gi % 2].b], q="pool")
                LOAD(skts[gi % 2].t[:], sk[:, t0:t0 + G], "ld_sk%d" % (gi % 2), writes=[skts[gi % 2].b], q="pool")

            def p3_group(gi, g, mid_hook):
                t0 = g * G
                seq = g // 8
                tl = (g % 8) * G
                x1 = x1s[gi % 2]
                c1t, s1t, ckt, skt = c1ts[gi % 2], s1ts[gi % 2], ckts[gi % 2], skts[gi % 2]
                for j in range(8):
                    p = psF[j % 2]
                    c0 = 672 + j * 128
                    for kc in range(8):
                        MM(p.t[:], w1.t[:, kc, c0:c0 + 128], hT.t[:, kc, :], kc == 0, kc == 7, [w1.b, hT.b], [p.b])
                    ACT(g_st.t[:, j, :], p.t[:], AF.Silu, [p.b], [g_st.b])
                STORE(gT_d[1][:, t0:t0 + G].rearrange("(c p) t -> p c t", p=128), g_st.t[:], "st_g1", [g_st.b], [db("gT1")])
                pa, pb = psF[0], psF[1]
                for kc in range(8):
                    MM(pa.t[0:32, :], w1.t[:, kc, 640:672], hT.t[:, kc, :], kc == 0, kc == 7, [w1.b, hT.b], [pa.b])
                for kc in range(8):
                    MM(pb.t[0:32, :], w1.t[:, kc, 1696:1728], hT.t[:, kc, :], kc == 0, kc == 7, [w1.b, hT.b], [pb.b])
                TT(ta.t[0:32, :], pa.t[0:32, :], ckt.t[:], ALU.mult, [pa.b, ckt.b], [ta.b])
                TT(tb.t[0:32, :], pb.t[0:32, :], skt.t[:], ALU.mult, [pb.b, skt.b], [tb.b])
                TT(kr_st.t[:], ta.t[0:32, :], tb.t[0:32, :], ALU.add, [ta.b, tb.b], [kr_st.b])
                STORE((k1r_p if seq == 0 else k1r_c)[:, tl:tl + G], kr_st.t[:], "st_kr", [kr_st.b], [db("k1r%d" % seq)])
                def lat_mm(s):
                    p = psL[s % 2]
                    for kc in range(8):
                        MM(p.t[:, 0:384], hT.t[:, kc, s * 128:(s + 1) * 128], w1.t[:, kc, 0:384], kc == 0, kc == 7, [w1.b, hT.b], [p.b])
                    for kc in range(8):
                        MM(p.t[:, 512:768], hT.t[:, kc, s * 128:(s + 1) * 128], w1.t[:, kc, 384:640], kc == 0, kc == 7, [w1.b, hT.b], [p.b])

                def lat_chain(s):
                    p = psL[s % 2]
                    lat = lats[s % 2]
                    MEMSET(ss2.t[:], 0.0, [ss2.b])
                    ACT(junk.t[:, 0:384], p.t[:, 0:384], AF.Square, [p.b, ss2.b], [junk.b, ss2.b], accum=ss2.t[:, 0:1])
                    ACT(junk.t[:, 384:640], p.t[:, 512:768], AF.Square, [p.b, ss2.b], [junk.b, ss2.b], accum=ss2.t[:, 1:2])
                    TS(ss2.t[:, 0:1], ss2.t[:, 0:1], 1.0 / 384, EPS, ALU.mult, ALU.add, [ss2.b], [ss2.b])
                    TS(ss2.t[:, 1:2], ss2.t[:, 1:2], 1.0 / 256, EPS, ALU.mult, ALU.add, [ss2.b], [ss2.b])
                    TS(ss2.t[:], ss2.t[:], -0.5, None, ALU.pow, None, [ss2.b], [ss2.b])
                    STT(lat.t[:, 0:384], p.t[:, 0:384], ss2.t[:, 0:1], gql.t[:], ALU.mult, ALU.mult, [p.b, ss2.b, gql.b], [lat.b])
                    STT(lat.t[:, 384:640], p.t[:, 512:768], ss2.t[:, 1:2], gkl.t[:], ALU.mult, ALU.mult, [p.b, ss2.b, gkl.b], [lat.b])

                def lat_tr(s):
                    lat = lats[s % 2]
                    pt = psT[s % 2]
                    for c in range(5):
                        TR(pt.t[:, c, :], lat.t[:, c * 128:(c + 1) * 128], ident_b.t[:], [lat.b, ident_b.b], [pt.b])
                    COPY("act", latT.t[:, :, s * 128:(s + 1) * 128], pt.t[:, 0:5, :], [pt.b], [latT.b])

                lat_mm(0)
                lat_chain(0)
                lat_mm(1)
                lat_chain(1)
                lat_mm(2)
                lat_tr(0)
                lat_chain(2)
                lat_mm(3)
                lat_tr(1)
                lat_chain(3)
                lat_tr(2)
                lat_tr(3)
                for h in range(16):
                    if h % 2 == 0:
                        pa_t, pb_t, pa_b, pb_b = psF[0].t[0:96, :], psF[1].t[0:96, :], psF[0].b, psF[1].b
                    else:
                        pa_t, pb_t, pa_b, pb_b = psL[0].t[0:96, 0:512], psL[1].t[0:96, 0:512], psL[0].b, psL[1].b
                    qs = q_st[h % 4]
                    for kc in range(3):
                        MM(pa_t, wq.t[:, kc, h * 96:(h + 1) * 96], latT.t[:, kc, :], kc == 0, kc == 2, [wq.b, latT.b], [pa_b])
                    for kc in range(3):
                        MM(pb_t, wq.t[:, kc, 1536 + h * 96:1536 + (h + 1) * 96], latT.t[:, kc, :], kc == 0, kc == 2, [wq.b, latT.b], [pb_b])
                    TT(ta.t[:], pa_t, c1t.t[:], ALU.mult, [pa_b, c1t.b], [ta.b])
                    TT(tb.t[:], pb_t, s1t.t[:], ALU.mult, [pb_b, s1t.b], [tb.b])
                    TT(qs.t[:], ta.t[:], tb.t[:], ALU.add, [ta.b, tb.b], [qs.b])
                    STORE(q1T_d[h, :, t0:t0 + G], qs.t[:], "st_q1", [qs.b], [db("q1T")])
                mid_hook()
                kn_st = kn_sts[gi % 2]
                v_st = v_sts[gi % 2]
                for c in range(8):
                    p = psY[c % 2]
                    for kc in range(2):
                        MM(p.t[:], wkv.t[:, kc, c * 128:(c + 1) * 128], latT.t[:, 3 + kc, :], kc == 0, kc == 1, [wkv.b, latT.b], [p.b])
                    COPY("act", kn_st.t[:, c, :], p.t[:], [p.b], [kn_st.b])
                if seq == 0:
                    STORE(k1n_p[:, tl:tl + G].rearrange("(c p) t -> p c t", p=128), kn_st.t[:], "st_kn", [kn_st.b], [db("k1n0")])
                else:
                    for c in range(8):
                        STORE(k1n_c[c][:, tl:tl + G], kn_st.t[:, c, :], "st_kn", [kn_st.b], [db("k1n1")])
                for s in range(4):
                    for n in range(2):
                        p = psY[(s * 2 + n) % 2]
                        for kc in range(2):
                            MM(p.t[:], latT.t[:, 3 + kc, s * 128:(s + 1) * 128], wkv.t[:, kc, 1024 + n * 512:1024 + (n + 1) * 512],
                               kc == 0, kc == 1, [wkv.b, latT.b], [p.b])
                        COPY("act", v_st.t[:, s, n * 8:(n + 1) * 8, 0:64], p.t[:].rearrange("p (h d) -> p h d", d=64), [p.b], [v_st.b])
                if seq == 0:
                    STORE(v1_p[tl:tl + G, :].rearrange("(s p) c -> p s c", p=128),
                          v_st.t[:].rearrange("p s h c -> p s (h c)"), "st_v1", [v_st.b], [db("v10")])
                else:
                    for h in range(16):
                        STORE(vview(v1_c[h])[tl:tl + G, :].rearrange("(s p) c -> p s c", p=128), v_st.t[:, :, h, :],
                              "st_v1", [v_st.b], [db("v11")])

            order = list(range(8, 16)) + list(range(0, 8))
            p3_loads(0, order[0])
            norm_part(x1s[0], gO, hb, ss, rstd, junk, 4)
            transpose_part(hb, hT, psT, 4)
            for gi, g in enumerate(order):
                nxt = gi + 1 < 16
                if nxt:
                    p3_loads(gi + 1, order[gi + 1])
                p3_group(gi, g, (lambda gi=gi: norm_part(x1s[(gi + 1) % 2], gO, hb, ss, rstd, junk, 4)) if nxt else (lambda: None))
                if nxt:
                    transpose_part(hb, hT, psT, 4)
            cc_list = [("k1r", k1r_c, k1r_g)] + [("k1n", k1n_c[i], k1n_g[i]) for i in range(8)] + [("v1", v1_c[h], v1_g[h]) for h in range(16)]
            for ci, (nm, src, dst) in enumerate(cc_list):
                S.op("pool", (lambda s_, d_: (lambda e: e.collective_compute("AllGather", ALU.bypass, replica_groups=GROUPS4, ins=[s_], outs=[d_])))(src, dst),
                     reads=[db(nm + "1")], writes=[db("ccg1")], dma=True, stream="cc1", inc=1)
            S.barrier()
        if STOP <= 4:
            S.emit()
            return nc

        def k1_loader(seq, h, kb_):
            if seq == 0:
                LOAD(kb_.t[0:64, 0:SEQ_OWN], k1n_p[h * 64:(h + 1) * 64, :], "ld_k" + kb_.b.name, [db("k1n0")], [kb_.b])
                LOAD(kb_.t[64:96, 0:SEQ_OWN], k1r_p[:, :], "ld_k" + kb_.b.name, [db("k1r0")], [kb_.b])
            else:
                LOAD(kb_.t[0:64, :].rearrange("p (r t) -> p r t", r=4),
                     k1n_g[h // 2].rearrange("(r f) t -> f r t", f=128)[(h % 2) * 64:(h % 2 + 1) * 64], "ld_k" + kb_.b.name, [db("ccg1")], [kb_.b])
                LOAD(kb_.t[64:96, :].rearrange("p (r t) -> p r t", r=4),
                     k1r_g.rearrange("(r f) t -> f r t", f=32), "ld_k" + kb_.b.name, [db("ccg1")], [kb_.b])

        def v1_loader(seq, h, vb_):
            if seq == 0:
                LOAD(vb_.t[:, 0:32, 0:65], v1_p[:, h * 65:(h + 1) * 65].rearrange("(t p) c -> p t c", p=128), "ld_v" + vb_.b.name, [db("v10")], [vb_.b])
            else:
                LOAD(vb_.t[:, :, 0:65], vview(v1_g[h]).rearrange("(t p) c -> p t c", p=128), "ld_v" + vb_.b.name, [db("ccg1")], [vb_.b])

        def q1_loader(seq, h, qb_):
            LOAD(qb_.t[0:96, :], q1T_d[h, :, seq * SEQ_OWN:(seq + 1) * SEQ_OWN], "ld_q" + qb_.b.name, [db("q1T")], [qb_.b])

        dense_phase(1, 96, 96 ** -0.5, 16, 1, k1_loader, v1_loader, q1_loader, 0, KG=2, lag=2)
        if STOP <= 5:
            S.emit()
            return nc

        outproj_phase(1, w_out_o, x1_d, [db("x1")], True)
        S.emit()
    return nc


def _na_table(r0, R, base_off, nkt, rpb):
    q = np.arange(128)
    qr = r0 + q // 64
    qc = q % 64
    k = np.arange(128)
    kr_in = k // 64
    kc = k % 64
    rs = np.clip(qr - 4, 0, R - 8)
    cs = np.clip(qc - 8, 0, 64 - 16)
    out = np.full((128, 8, nkt, 128), NEG, np.float32)
    for kt in range(nkt):
        kr = r0 + base_off + 2 * kt + kr_in
        valid = ((kr[:, None] >= rs[None, :]) & (kr[:, None] < rs[None, :] + 8) & (kr[:, None] >= 0) & (kr[:, None] < R)
                 & (kc[:, None] >= cs[None, :]) & (kc[:, None] < cs[None, :] + 16))
        ro = np.clip(kr[:, None] - qr[None, :] + 7, 0, 14)
        co = np.clip(kc[:, None] - qc[None, :] + 15, 0, 30)
        vals = rpb[:, ro, co]
        vals = np.where(valid[None], vals, np.float32(NEG))
        out[:, :, kt, :] = vals.transpose(1, 0, 2)
    return out


def _rope_tables(pos, rot_dim):
    nf = rot_dim // 4
    inv = (np.float32(10000.0) ** (-np.arange(nf, dtype=np.float32) / np.float32(nf))).astype(np.float32)
    row = (pos // 64).astype(np.float32)
    col = (pos % 64).astype(np.float32)
    ang = np.concatenate([row[:, None] * inv[None], col[:, None] * inv[None]], axis=-1).astype(np.float32)
    return np.cos(ang).astype(np.float32), np.sin(ang).astype(np.float32)


_PROG = None


def kernel(x_prompt, x_sample, norm_e, w_in_e, rpb_a, qnorm_b, knorm_b, w_out_e,
           norm_o, w_in_o, qlat_g, kvlat_g, w_uq, w_ukv, w_out_o, norm_f):
    global _PROG
    f = lambda a: np.ascontiguousarray(np.asarray(a, dtype=np.float32))
    x_prompt, x_sample = f(x_prompt), f(x_sample)
    w_in_e0, w_out_e0, w_in_o0, w_uq0, w_ukv0, w_out_o0 = f(w_in_e)[0], f(w_out_e)[0], f(w_in_o)[0], f(w_uq)[0], f(w_ukv)[0], f(w_out_o)[0]
    rpb = f(rpb_a)[0]
    kr = w_in_o0[:, 640:672]
    w1x = np.concatenate([w_in_o0, kr[:, 16:32], kr[:, 0:16]], axis=1)
    uq = w_uq0.reshape(384, 16, 96)
    uq_sw = np.concatenate([uq[:, :, 0:64], uq[:, :, 80:96], uq[:, :, 64:80]], axis=2)
    wuq2 = np.concatenate([uq.reshape(384, 1536), uq_sw.reshape(384, 1536)], axis=1)
    ukv = w_ukv0.reshape(256, 16, 128)
    wukv2 = np.concatenate([ukv[:, :, 0:64].reshape(256, 1024), ukv[:, :, 64:128].reshape(256, 1024)], axis=1)
    bm_int = _na_table(8, 64, -4, 5, rpb).reshape(128, -1)
    ident = np.eye(128, dtype=np.float32)
    selA_h = np.tile(np.eye(8, dtype=np.float32).reshape(1, 64), (128, 1))
    selB_h = np.zeros((128, 8, 64), np.float32)
    for hh in range(8):
        selB_h[64 + hh, hh, :] = 1.0
    selB_h = selB_h.reshape(128, 512)
    common = dict(w_in_e=w_in_e0, w_out_e=w_out_e0, w1x=f(w1x), wuq2=f(wuq2), wukv2=f(wukv2), w_out_o=w_out_o0,
                  norm_e=f(norm_e).reshape(1, 1024), norm_o=f(norm_o).reshape(1, 1024), norm_f=f(norm_f).reshape(1, 1024),
                  qnorm_b=f(qnorm_b).reshape(1, 64), knorm_b=f(knorm_b).reshape(1, 64),
                  qlat_g=f(qlat_g).reshape(1, 384), kvlat_g=f(kvlat_g).reshape(1, 256),
                  bm_int=f(bm_int), ident=ident, selA=selA_h, selB=selB_h)
    edge_p = [_na_table(r0, 64, -6, 7, rpb).reshape(128, -1) for r0 in (0, 2, 60, 62)]
    in_maps = []
    for c in range(8):
        sq, j = c // 4, c % 4
        xs = x_sample[sq]
        x_own = np.concatenate([x_prompt[c], xs[j * 4096:(j + 1) * 4096]], axis=0)
        halo = np.zeros((1024, 1024), np.float32)
        if j > 0:
            halo[0:512] = xs[j * 4096 - 512:j * 4096]
        if j < 3:
            halo[512:1024] = xs[(j + 1) * 4096:(j + 1) * 4096 + 512]
        pos = np.concatenate([np.arange(4096), j * 4096 + np.arange(4096)])
        cos0, sin0 = _rope_tables(pos, 64)
        cs0 = np.concatenate([cos0, cos0, sin0, sin0], axis=1)
        cos1, sin1 = _rope_tables(pos, 32)
        c1 = np.ones((96, 8192), np.float32)
        s1 = np.zeros((96, 8192), np.float32)
        c1[64:80] = cos1.T
        c1[80:96] = cos1.T
        s1[64:80] = -sin1.T
        s1[80:96] = sin1.T
        edge_s = [_na_table(64 * j + r0, 256, -6, 7, rpb).reshape(128, -1) for r0 in (0, 2, 60, 62)]
        m = dict(common)
        m.update(x_own=f(x_own), x_halo=halo, cs0=f(cs0), c1=c1, s1=s1, ck=f(c1[64:96]), sk=f(s1[64:96]),
                 bm_edge=f(np.stack(edge_p + edge_s, axis=0)))
        in_maps.append(m)
    if _PROG is None:
        _PROG = build_program()
    res = run_bass_kernel_spmd(_PROG, in_maps, core_ids=list(range(8)))
    if DEBUG:
        kernel.last = res.results
    y_prompt = np.stack([np.asarray(res.results[c]["y_out"])[0:4096] for c in range(8)], axis=0)
    y_sample = np.stack([np.concatenate([np.asarray(res.results[sq * 4 + j]["y_out"])[4096:8192] for j in range(4)], axis=0)
                         for sq in range(2)], axis=0)
    return (y_prompt.astype(np.float32), y_sample.astype(np.float32))
```
